# Optimizing a Trainium2 kernel written in Bass

```python
import jax, jax.numpy as jnp
from jax import lax
import numpy as np

D_MODEL = 1024
BATCH = 8
SEQ = 2048
DEPTH = 1
DEC_BATCH = 128
DEC_SEQ = 4
PAST_LEN = 16384
PAGE_SIZE = 128

H_A = 4
D_A = D_MODEL
DK_A = D_A // H_A
MLSTM_CHUNK = 64
CONV_W = 4
D_B = D_MODEL
HEAD_B = 64
H_B = D_B // HEAD_B
LORA_W = 64
LORA_A = 64
LORA_G = 128
D_FF = 4 * D_MODEL
N_COND = 6 * D_MODEL
OFF_V_A = 2 * D_A
OFF_I = 3 * D_A
OFF_F = OFF_I + H_A
OFF_RWKV = OFF_F + H_A
N_RWKV = 3 * D_B + LORA_W + LORA_A + LORA_G
OFF_GATE = OFF_RWKV + N_RWKV
N_IN = OFF_GATE + 2 * D_MODEL
ALPHA = (2.0 * DEPTH) ** 0.25
BETA = (8.0 * DEPTH) ** -0.25
LN_EPS = 1e-5
MLSTM_NORM_EPS = 1e-6
RWKV_NORM_EPS = 64e-5

kernel_name = 'hybrid_mlstm_rwkv7_decoder_step'


def layer_norm(x, g, b):
    xf = x.astype(jnp.float32)
    mu = xf.mean(-1, keepdims=True)
    var = jnp.square(xf - mu).mean(-1, keepdims=True)
    return ((xf - mu) * lax.rsqrt(var + LN_EPS) * g + b).astype(x.dtype)


def head_norm(o, eps):
    mu = o.mean(-1, keepdims=True)
    var = jnp.square(o - mu).mean(-1, keepdims=True)
    return (o - mu) * lax.rsqrt(var + eps)


def mlstm_chunk_step(carry, inp):
    C, n, m = carry
    q, k, v, ig, lf = inp
    L = q.shape[2]
    causal = jnp.tril(jnp.ones((L, L), dtype=bool))
    b = jnp.cumsum(lf, axis=-1)
    g = b + m[..., None]
    dlog = jnp.where(causal, b[..., :, None] - b[..., None, :] + ig[..., None, :], -jnp.inf)
    m_t = jnp.maximum(g, dlog.max(-1))
    w_inter = jnp.exp(g - m_t)
    s = jnp.einsum('bhtk,bhsk->bhts', q, k) * jnp.exp(dlog - m_t[..., None])
    num = w_inter[..., None] * jnp.einsum('bhvk,bhtk->bhtv', C, q) + jnp.einsum('bhts,bhsv->bhtv', s, v)
    den = w_inter * jnp.einsum('bhk,bhtk->bht', n, q) + s.sum(-1)
    h = num / jnp.maximum(jnp.abs(den), jnp.exp(-m_t))[..., None]
    b_last = b[..., -1]
    wlog = b_last[..., None] - b + ig
    m_new = jnp.maximum(b_last + m, wlog.max(-1))
    decay = jnp.exp(b_last + m - m_new)
    wts = jnp.exp(wlog - m_new[..., None])
    C_new = decay[..., None, None] * C + jnp.einsum('bhs,bhsv,bhsk->bhvk', wts, v, k)
    n_new = decay[..., None] * n + jnp.einsum('bhs,bhsk->bhk', wts, k)
    return (C_new, n_new, m_new), h


def mlstm_scan(q, k, v, ig, lf, C0, n0, m0):
    B, T = q.shape[:2]
    L = MLSTM_CHUNK if T % MLSTM_CHUNK == 0 else T
    nC = T // L

    def to_chunks(a):
        a = a.reshape((B, nC, L) + a.shape[2:])
        return jnp.moveaxis(jnp.moveaxis(a, 1, 0), 3, 2)

    xs = tuple(to_chunks(a) for a in (q, k, v, ig, lf))
    (C, n, m), h = lax.scan(mlstm_chunk_step, (C0, n0, m0), xs)
    h = jnp.moveaxis(jnp.moveaxis(h, 2, 3), 0, 1).reshape(B, T, H_A, DK_A)
    return h, C, n, m


def rwkv_step(S, inp):
    r, w, k, v, kk, a = inp
    sa = jnp.einsum('bhvk,bhk->bhv', S, -kk)
    S = S * w[:, :, None, :] + sa[..., None] * (kk * a)[:, :, None, :] + v[..., None] * k[:, :, None, :]
    return S, jnp.einsum('bhvk,bhk->bhv', S, r)


def rwkv_scan(r, w, k, v, kk, a, S0):
    xs = tuple(jnp.moveaxis(t, 1, 0) for t in (r, w, k, v, kk, a))
    S, o = lax.scan(rwkv_step, S0, xs)
    return jnp.moveaxis(o, 0, 1), S


def token_mixer(h, st, p):
    C0, n0, m0, conv0, S0, shift0 = st
    B, T, _ = h.shape
    f32 = jnp.float32
    proj = h @ p['w_in']
    qk_pad = jnp.concatenate([conv0.astype(proj.dtype), proj[..., :OFF_V_A]], axis=1)
    qk = p['conv_b'] + sum(qk_pad[:, j:j + T] * p['conv_w'][j] for j in range(CONV_W))
    qk = jax.nn.silu(qk.astype(f32))
    q = qk[..., :D_A].reshape(B, T, H_A, DK_A)
    k = qk[..., D_A:].reshape(B, T, H_A, DK_A) * DK_A ** -0.5
    v = proj[..., OFF_V_A:OFF_I].astype(f32).reshape(B, T, H_A, DK_A)
    ig = (proj[..., OFF_I:OFF_F] + p['mlstm_i_bias']).astype(f32)
    lf = jax.nn.log_sigmoid((proj[..., OFF_F:OFF_RWKV] + p['mlstm_f_bias']).astype(f32))
    h_a, C1, n1, m1 = mlstm_scan(q, k, v, ig, lf, C0.astype(f32), n0.astype(f32), m0.astype(f32))
    gate_a = jax.nn.sigmoid(proj[..., OFF_GATE:OFF_GATE + D_MODEL].astype(f32))
    y_a = gate_a * head_norm(h_a, MLSTM_NORM_EPS).reshape(B, T, D_A) * p['mlstm_norm_w']
    pr = proj[..., OFF_RWKV:OFF_GATE]
    prev_row = shift0.astype(h.dtype) @ p['w_in'][:, OFF_RWKV:OFF_GATE]
    pr_prev = jnp.concatenate([prev_row[:, None], pr[:, :-1]], axis=1)
    xs = pr + (pr_prev - pr) * p['rwkv_mu']
    o1 = 3 * D_B
    o2 = o1 + LORA_W
    o3 = o2 + LORA_A
    r = xs[..., :D_B]
    kr = xs[..., D_B:2 * D_B]
    vr = xs[..., 2 * D_B:o1]
    w_log = -jax.nn.softplus(-(p['rwkv_w0'] + jnp.tanh(xs[..., o1:o2]) @ p['rwkv_w2']).astype(f32)) - 0.5
    decay = jnp.exp(-jnp.exp(w_log))
    a = jax.nn.sigmoid((p['rwkv_a0'] + xs[..., o2:o3] @ p['rwkv_a2']).astype(f32))
    g = jax.nn.sigmoid(xs[..., o3:]) @ p['rwkv_g2']
    kk = kr * p['rwkv_k_k']
    kmod = kr * (1.0 + (a - 1.0) * p['rwkv_k_a'])

    def split(t):
        return t.astype(f32).reshape(B, T, H_B, HEAD_B)

    rh, kh, vh, wh, ah, kkh = split(r), split(kmod), split(vr), split(decay), split(a), split(kk)
    kkh = kkh / jnp.maximum(jnp.linalg.norm(kkh, axis=-1, keepdims=True), 1e-12)
    o, S1 = rwkv_scan(rh, wh, kh, vh, kkh, ah, S0.astype(f32))
    o = head_norm(o, RWKV_NORM_EPS).reshape(B, T, D_B) * p['rwkv_lnx_w'] + p['rwkv_lnx_b']
    o = o + ((rh * kh * p['rwkv_r_k']).sum(-1, keepdims=True) * vh).reshape(B, T, D_B)
    y_b = o * g
    gate_b = jax.nn.sigmoid(proj[..., OFF_GATE + D_MODEL:].astype(f32))
    u = y_a + gate_b * y_b
    y = u.astype(h.dtype) @ p['w_out']
    new_st = (C1.astype(C0.dtype), n1.astype(n0.dtype), m1.astype(m0.dtype),
              qk_pad[:, T:].astype(conv0.dtype), S1.astype(S0.dtype), h[:, -1].astype(shift0.dtype))
    return y, new_st


def decoder_layer(x, c, st, p):
    mod = jax.nn.silu(c) @ p['w_cond'] + p['b_cond']
    sh1, sc1, g1, sh2, sc2, g2 = [t[:, None, :] for t in jnp.split(mod, 6, axis=-1)]
    y_mix, new_st = token_mixer(x * (1.0 + sc1) + sh1, st, p)
    x = layer_norm(ALPHA * x + g1 * y_mix, p['ln1_g'], p['ln1_b'])
    h2 = x * (1.0 + sc2) + sh2
    y_ff = jnp.square(jax.nn.relu(h2 @ p['w_up'])) @ p['w_down']
    x = layer_norm(ALPHA * x + g2 * y_ff, p['ln2_g'], p['ln2_b'])
    return x, new_st


def setup_inputs(seed: int = 0) -> dict:
    key = jax.random.key(seed)
    ks = jax.random.split(key, 40)
    f32 = jnp.float32

    def nrm(i, shape, s):
        return jax.random.normal(ks[i], shape, f32) * s

    L = DEPTH
    return {
        'x_prompt': nrm(0, (BATCH, SEQ, D_MODEL), 1.0),
        'x_sample': nrm(1, (DEC_BATCH, DEC_SEQ, D_MODEL), 1.0),
        'c_prompt': nrm(2, (BATCH, D_MODEL), 1.0),
        'c_sample': nrm(3, (DEC_BATCH, D_MODEL), 1.0),
        'state_mlstm_C': nrm(4, (L, DEC_BATCH, H_A, DK_A, DK_A), 0.05),
        'state_mlstm_n': nrm(5, (L, DEC_BATCH, H_A, DK_A), 0.1),
        'state_mlstm_m': nrm(6, (L, DEC_BATCH, H_A), 0.5),
        'state_mlstm_conv': nrm(7, (L, DEC_BATCH, CONV_W - 1, 2 * D_A), 1.0),
        'state_rwkv_S': nrm(8, (L, DEC_BATCH, H_B, HEAD_B, HEAD_B), 0.1),
        'state_rwkv_shift': nrm(9, (L, DEC_BATCH, D_MODEL), 1.0),
        'w_cond': nrm(10, (L, D_MODEL, N_COND), D_MODEL ** -0.5),
        'b_cond': nrm(11, (L, N_COND), 0.02),
        'w_in': nrm(12, (L, D_MODEL, N_IN), D_MODEL ** -0.5),
        'mlstm_i_bias': nrm(13, (L, H_A), 0.1),
        'mlstm_f_bias': jnp.linspace(3.0, 6.0, H_A, dtype=f32)[None] + nrm(14, (L, H_A), 0.1),
        'conv_w': nrm(15, (L, CONV_W, 2 * D_A), CONV_W ** -0.5),
        'conv_b': nrm(16, (L, 2 * D_A), 0.02),
        'mlstm_norm_w': 1.0 + nrm(17, (L, D_A), 0.02),
        'rwkv_mu': jax.random.uniform(ks[18], (L, N_RWKV), f32),
        'rwkv_w0': jax.random.uniform(ks[19], (L, D_B), f32, -6.0, -1.0),
        'rwkv_w2': nrm(20, (L, LORA_W, D_B), 0.1 * LORA_W ** -0.5),
        'rwkv_a0': nrm(21, (L, D_B), 0.1),
        'rwkv_a2': nrm(22, (L, LORA_A, D_B), LORA_A ** -0.5),
        'rwkv_g2': nrm(23, (L, LORA_G, D_B), LORA_G ** -0.5),
        'rwkv_k_k': 0.85 + nrm(24, (L, D_B), 0.02),
        'rwkv_k_a': 1.0 + nrm(25, (L, D_B), 0.02),
        'rwkv_r_k': nrm(26, (L, H_B, HEAD_B), 0.1),
        'rwkv_lnx_w': 1.0 + nrm(27, (L, D_B), 0.02),
        'rwkv_lnx_b': nrm(28, (L, D_B), 0.02),
        'w_out': nrm(29, (L, D_MODEL, D_MODEL), BETA * D_MODEL ** -0.5),
        'ln1_g': 1.0 + nrm(30, (L, D_MODEL), 0.02),
        'ln1_b': nrm(31, (L, D_MODEL), 0.02),
        'w_up': nrm(32, (L, D_MODEL, D_FF), D_MODEL ** -0.5),
        'w_down': nrm(33, (L, D_FF, D_MODEL), BETA * D_FF ** -0.5),
        'ln2_g': 1.0 + nrm(34, (L, D_MODEL), 0.02),
        'ln2_b': nrm(35, (L, D_MODEL), 0.02),
    }


def reference(x_prompt, x_sample, c_prompt, c_sample, state_mlstm_C, state_mlstm_n, state_mlstm_m,
              state_mlstm_conv, state_rwkv_S, state_rwkv_shift, w_cond, b_cond, w_in, mlstm_i_bias,
              mlstm_f_bias, conv_w, conv_b, mlstm_norm_w, rwkv_mu, rwkv_w0, rwkv_w2, rwkv_a0, rwkv_a2,
              rwkv_g2, rwkv_k_k, rwkv_k_a, rwkv_r_k, rwkv_lnx_w, rwkv_lnx_b, w_out, ln1_g, ln1_b, w_up,
              w_down, ln2_g, ln2_b):
    bp = x_prompt.shape[0]
    dt = x_prompt.dtype
    zero_st = (jnp.zeros((bp, H_A, DK_A, DK_A), dt), jnp.zeros((bp, H_A, DK_A), dt),
               jnp.zeros((bp, H_A), dt), jnp.zeros((bp, CONV_W - 1, 2 * D_A), dt),
               jnp.zeros((bp, H_B, HEAD_B, HEAD_B), dt), jnp.zeros((bp, D_MODEL), dt))
    yp = x_prompt
    ys = x_sample
    new_p = [[] for _ in range(6)]
    new_s = [[] for _ in range(6)]
    for l in range(DEPTH):
        p = {'w_cond': w_cond[l], 'b_cond': b_cond[l], 'w_in': w_in[l], 'mlstm_i_bias': mlstm_i_bias[l],
             'mlstm_f_bias': mlstm_f_bias[l], 'conv_w': conv_w[l], 'conv_b': conv_b[l],
             'mlstm_norm_w': mlstm_norm_w[l], 'rwkv_mu': rwkv_mu[l], 'rwkv_w0': rwkv_w0[l],
             'rwkv_w2': rwkv_w2[l], 'rwkv_a0': rwkv_a0[l], 'rwkv_a2': rwkv_a2[l], 'rwkv_g2': rwkv_g2[l],
             'rwkv_k_k': rwkv_k_k[l], 'rwkv_k_a': rwkv_k_a[l], 'rwkv_r_k': rwkv_r_k[l],
             'rwkv_lnx_w': rwkv_lnx_w[l], 'rwkv_lnx_b': rwkv_lnx_b[l], 'w_out': w_out[l],
             'ln1_g': ln1_g[l], 'ln1_b': ln1_b[l], 'w_up': w_up[l], 'w_down': w_down[l],
             'ln2_g': ln2_g[l], 'ln2_b': ln2_b[l]}
        yp, st_p = decoder_layer(yp, c_prompt, zero_st, p)
        st_in = (state_mlstm_C[l], state_mlstm_n[l], state_mlstm_m[l], state_mlstm_conv[l],
                 state_rwkv_S[l], state_rwkv_shift[l])
        ys, st_s = decoder_layer(ys, c_sample, st_in, p)
        for lst, t in zip(new_p, st_p):
            lst.append(t)
        for lst, t in zip(new_s, st_s):
            lst.append(t)
    C_p, n_p, m_p, conv_p, S_p, shift_p = [jnp.stack(t) for t in new_p]
    C_s, n_s, m_s, conv_s, S_s, shift_s = [jnp.stack(t) for t in new_s]
    return (yp, ys, C_p, n_p, m_p, conv_p, S_p, shift_p, C_s, n_s, m_s, conv_s, S_s, shift_s)
```

```python
import numpy as np
from contextlib import ExitStack
import concourse.bass as bass
import concourse.mybir as mybir
from concourse.bass_utils import run_bass_kernel_spmd

F32 = mybir.dt.float32
BF16 = mybir.dt.bfloat16
AF = mybir.ActivationFunctionType
ALU = mybir.AluOpType
AX = mybir.AxisListType

NCORES = 8
D = 1024
TP = 2048
NS = 16
TS = 4
NST = NS * TS
NTOK = TP + NST
OFF_RWKV = 3080
N_RWKV = 3328
OFF_GATE = 6408
N_IN = 8456
ALPHA = 2.0 ** 0.25
LN_EPS = 1e-5


class Tk:
    __slots__ = ("w", "rs", "name")

    def __init__(self, name=""):
        self.w = None
        self.rs = []
        self.name = name


class Sched:
    ENG = ("pe", "dve", "act", "pool", "sp")
    NOSELF = ("pe",)

    def __init__(self, nc, ctx, n_dma_sems=12):
        self.nc = nc
        self.E = {"pe": nc.tensor, "dve": nc.vector, "act": nc.scalar, "pool": nc.gpsimd, "sp": nc.sync}
        self.semh = {}
        for e in self.ENG:
            self.semh[e] = ctx.enter_context(nc.semaphore("s_" + e))
        self.cnt = {e: 0 for e in self.ENG}
        self.seen = {e: {} for e in self.ENG}
        self.dq = {}
        for q in ("sp", "pool", "act"):
            sems = []
            for i in range(n_dma_sems):
                key = ("d", q, i)
                self.semh[key] = ctx.enter_context(nc.semaphore("d_%s_%d" % (q, i)))
                sems.append(key)
            self.dq[q] = {"keys": sems, "val": [0] * n_dma_sems, "nxt": 0}
        self.n_ins = 0
        self.n_wait = 0

    def _wait(self, en, ev, selfwait=True):
        if ev is None:
            return
        key, val = ev
        if key == en and (not selfwait or en in self.NOSELF):
            return
        if self.seen[en].get(key, 0) >= val:
            return
        self.E[en].wait_ge(self.semh[key], val)
        self.seen[en][key] = val
        self.n_wait += 1

    def _deps(self, en, reads, writes):
        for t in reads:
            self._wait(en, t.w, True)
        for t in writes:
            self._wait(en, t.w, False)
            for r in t.rs:
                self._wait(en, r, False)

    def _commit(self, ev, reads, writes):
        for t in reads:
            t.rs.append(ev)
            if len(t.rs) > 48:
                d = {}
                for k, v in t.rs:
                    if d.get(k, 0) < v:
                        d[k] = v
                t.rs = list(d.items())
        for t in writes:
            t.w = ev
            t.rs = []

    def op(self, en, fn, reads=(), writes=()):
        self._deps(en, reads, writes)
        ins = fn(self.E[en])
        self.cnt[en] += 1
        ins.then_inc(self.semh[en], 1)
        ev = (en, self.cnt[en])
        self._commit(ev, reads, writes)
        self.n_ins += 1
        return ev

    def dma(self, q, out, in_, reads=(), writes=(), **kw):
        self._deps(q, reads, writes)
        st = self.dq[q]
        i = st["nxt"]
        st["nxt"] = (i + 1) % len(st["keys"])
        key = st["keys"][i]
        if st["val"][i] > 0:
            self._wait(q, (key, st["val"][i]))
        ins = self.E[q].dma_start(out=out, in_=in_, **kw)
        st["val"][i] += 16
        ins.then_inc(self.semh[key], 16)
        ev = (key, st["val"][i])
        self._commit(ev, reads, writes)
        self.n_ins += 1
        return ev

    def finish(self):
        for q, st in self.dq.items():
            for i, key in enumerate(st["keys"]):
                if st["val"][i] > 0:
                    self._wait("sp", (key, st["val"][i]))
        for e in self.ENG:
            if e != "sp" and self.cnt[e] > 0:
                self._wait("sp", (e, self.cnt[e]))


IN_SPECS = [
    ("xp", [TP, D]), ("xs", [NST, D]), ("cc", [17, D]),
    ("stC", [NS * 4 * 256, 256]), ("stn", [NS * 4, 256]), ("stm", [NS, 4]), ("stconv", [NS * 3, 2048]),
    ("stS", [NS * 16 * 64, 64]), ("stshift", [NS, D]),
    ("w_cond", [D, 6144]), ("w_in", [D, N_IN]), ("w_out", [D, D]), ("w_up", [D, 4096]), ("w_down", [4096, D]),
    ("pk0", [128, 128]), ("pk1", [114, 128]),
    ("ifb", [4, 2]),
    ("vecs", [7, D]),
    ("w2", [64, D]), ("a2", [64, D]), ("g2", [128, D]),
]
OUT_SPECS = [
    ("yp", [TP, D]), ("ys", [NST, D]), ("C_p", [4 * 256, 256]), ("n_p", [4, 256]), ("m_p", [1, 4]),
    ("conv_p", [3, 2048]), ("S_p", [16 * 64, 64]), ("shift_p", [1, D]),
    ("C_s", [NS * 4 * 256, 256]), ("n_s", [NS * 4, 256]), ("m_s", [NS, 4]), ("conv_s", [NS * 3, 2048]),
    ("S_s", [NS * 16 * 64, 64]), ("shift_s", [NS, D]),
]

STAGE = 1


def build():
    nc = bass.Bass("TRN2", target_bir_lowering=False)
    I = {n: nc.dram_tensor(n, s, F32, kind="ExternalInput").ap() for n, s in IN_SPECS}
    O = {n: nc.dram_tensor(n, s, F32, kind="ExternalOutput").ap() for n, s in OUT_SPECS}
    with ExitStack() as ctx:
        S = Sched(nc, ctx)
        toks = {}

        def tk(*key):
            if key not in toks:
                toks[key] = Tk(str(key))
            return toks[key]

        def sb(name, shape, dt=F32):
            return ctx.enter_context(nc.sbuf_tensor("sb_" + name, shape, dt))

        PSALL = ctx.enter_context(nc.psum_tensor("psall", [128, 4096], F32))
        ps_i = [0]

        def nps(n=1):
            i = ps_i[0]
            if n > 1 and i % n:
                i += n - i % n
            if i + n > 8:
                i = 0
            ps_i[0] = (i + n) % 8
            if n == 1:
                return PSALL[:, i * 512:(i + 1) * 512], tk("ps", i)
            return PSALL[:, i * 512:(i + n) * 512], [tk("ps", i + j) for j in range(n)]

        ev_i = [0]

        def evac_eng():
            ev_i[0] ^= 1
            return "dve" if ev_i[0] else "act"

        def copy(en, out, in_, R, W):
            if en == "act":
                S.op("act", lambda e: e.copy(out=out, in_=in_), R, W)
            else:
                S.op(en, lambda e: e.tensor_copy(out=out, in_=in_), R, W)

        def mm(out, lhsT, rhs, start, stop, R, W):
            S.op("pe", lambda e: e.matmul(out, lhsT=lhsT, rhs=rhs, start=start, stop=stop), R, W)

        def tr(out, in_, idn, R, W):
            S.op("pe", lambda e: e.transpose(out=out, in_=in_, identity=idn), R, W)

        def nps(n=1):
            i = ps_i[0]
            if n == 1:
                ps_i[0] = (i + 1) % 6
            elif n == 2:
                i = (i + 1) // 2 * 2
                if i > 4:
                    i = 0
                ps_i[0] = (i + 2) % 6
            else:
                i = 0
                ps_i[0] = 4
            return PSALL[:, i * 512:(i + n) * 512], [tk("ps", i + j) for j in range(n)]

        P7 = PSALL[:, 7 * 512:8 * 512]
        P7t = [tk("ps", 7)]
        P6 = PSALL[:, 6 * 512:7 * 512]
        P6t = [tk("ps", 6)]

        cur = [ctx]

        def sb(name, shape, dt=F32):
            return cur[0].enter_context(nc.sbuf_tensor("sb_" + name, shape, dt))

        def barrier():
            evs = [(e, S.cnt[e]) for e in S.ENG if S.cnt[e] > 0]
            for q, st in S.dq.items():
                for i2, key in enumerate(st["keys"]):
                    if st["val"][i2] > 0:
                        evs.append((key, st["val"][i2]))
            for e in S.ENG:
                for ev in evs:
                    S._wait(e, ev, False)

        def dve(fn, R, W):
            S.op("dve", fn, R, W)

        def act(fn, R, W):
            S.op("act", fn, R, W)

        def pool(fn, R, W):
            S.op("pool", fn, R, W)

        ident = sb("ident", [128, 128]); identb = sb("identb", [128, 128], BF16)
        pkT0 = sb("pkT0", [128, 128]); pkT1 = sb("pkT1", [128, 114])
        modT = sb("modT", [128, 48, 17])
        yat = [sb("yat%d" % i, [128, 8, 128], BF16) for i in range(2)]
        YA = nc.dram_tensor("YA", [17, 128, 1024], BF16).ap()
        X1 = nc.dram_tensor("X1s", [17, 128, 1024], F32).ap()
        stg = [sb("stg%d" % i, [128, 1540]) for i in range(3)]
        hTt = [sb("hTt%d" % i, [128, 8, 128], BF16) for i in range(2)]
        hsh = sb("hsh", [128, 8, NS], BF16)
        hl = sb("hl", [128, 8]); hs32 = sb("hs32", [128, 8, NST])
        NEGp = sb("NEGp", [128, 128]); NEGs = sb("NEGs", [64, 64])
        Bsel = sb("Bsel", [16, 64]); sel_last = sb("sel_last", [64, 16])
        BM = sb("BM", [128, 16, 64], BF16)
        ifb = sb("ifb", [4, 2]); nfb = sb("nfb", [4, 1])
        ones4 = sb("ones4", [4, 128]); zeros4 = sb("zeros4", [4, 128])

        S.op("pool", lambda e: e.memset(ident[:], 0.0), [], [tk("ident")])
        S.op("pool", lambda e: e.affine_select(out=ident[:], in_=ident[:], pattern=[[-1, 128]], compare_op=ALU.not_equal,
                                               fill=1.0, base=0, channel_multiplier=1), [tk("ident")], [tk("ident")])
        copy("dve", identb[:], ident[:], [tk("ident")], [tk("identb")])
        CI = [tk("ident")]
        CIB = [tk("identb")]
        pool(lambda e: e.memset(NEGp[:], 0.0), [], [tk("NEGp")])
        pool(lambda e: e.affine_select(out=NEGp[:], in_=NEGp[:], pattern=[[1, 128]], compare_op=ALU.is_ge, fill=-30000.0, base=0,
                                       channel_multiplier=-1), [tk("NEGp")], [tk("NEGp")])
        pool(lambda e: e.memset(Bsel[:], 1.0), [], [tk("Bsel")])
        pool(lambda e: e.affine_select(out=Bsel[:], in_=Bsel[:], pattern=[[1, 64]], compare_op=ALU.is_ge, fill=0.0, base=0,
                                       channel_multiplier=-4), [tk("Bsel")], [tk("Bsel")])
        pool(lambda e: e.affine_select(out=Bsel[:], in_=Bsel[:], pattern=[[-1, 64]], compare_op=ALU.is_ge, fill=0.0, base=3,
                                       channel_multiplier=4), [tk("Bsel")], [tk("Bsel")])
        pool(lambda e: e.memset(sel_last[:], 0.0), [], [tk("sel_last")])
        pool(lambda e: e.affine_select(out=sel_last[:], in_=sel_last[:], pattern=[[-4, 16]], compare_op=ALU.not_equal, fill=1.0, base=-3,
                                       channel_multiplier=1), [tk("sel_last")], [tk("sel_last")])
        pool(lambda e: e.memset(BM[:], 1.0), [], [tk("BM")])
        pool(lambda e: e.affine_select(out=BM[:], in_=BM[:], pattern=[[-4, 16], [1, 64]], compare_op=ALU.is_ge, fill=0.0, base=0,
                                       channel_multiplier=0), [tk("BM")], [tk("BM")])
        pool(lambda e: e.affine_select(out=BM[:], in_=BM[:], pattern=[[4, 16], [-1, 64]], compare_op=ALU.is_ge, fill=0.0, base=3,
                                       channel_multiplier=0), [tk("BM")], [tk("BM")])
        p, pt = nps()
        mm(p[0:64, 0:64], Bsel[:, :], Bsel[:, :], True, True, [tk("Bsel")], pt)
        dve(lambda e: e.tensor_scalar(out=NEGs[:], in0=p[0:64, 0:64], scalar1=-1.0, scalar2=30000.0, op0=ALU.add, op1=ALU.mult), pt, [tk("NEGs")])
        pool(lambda e: e.affine_select(out=NEGs[:], in_=NEGs[:], pattern=[[1, 64]], compare_op=ALU.is_ge, fill=-30000.0, base=0,
                                       channel_multiplier=-1), [tk("NEGs")], [tk("NEGs")])
        pool(lambda e: e.memset(ones4[:], 1.0), [], [tk("c4")])
        pool(lambda e: e.memset(zeros4[:], 0.0), [], [tk("c4")])
        S.dma("sp", ifb[:], I["ifb"], [], [tk("ifb")])
        dve(lambda e: e.tensor_scalar(out=nfb[:], in0=ifb[:, 1:2], scalar1=-1.0, scalar2=None, op0=ALU.mult), [tk("ifb")], [tk("nfb")])

        with ExitStack() as c0:
            cur[0] = c0
            pk0 = sb("pk0", [128, 128]); pk1 = sb("pk1", [114, 128])
            S.dma("sp", pk0[:], I["pk0"], [], [tk("pk0")])
            S.dma("sp", pk1[:], I["pk1"], [], [tk("pk1")])
            p, pt = nps()
            tr(p[:, 0:128], pk0[:], ident[:], [tk("pk0")] + CI, pt)
            tr(p[:, 128:242], pk1[:], ident[0:114, 0:114], [tk("pk1")] + CI, pt)
            copy("dve", pkT0[:], p[:, 0:128], pt, [tk("pkT0")])
            copy("dve", pkT1[:], p[:, 128:242], pt, [tk("pkT1")])
            bcT = pkT0[:, 0:48]
            cc = sb("cc", [17, D]); csT = sb("csT", [128, 8, 17])
            S.dma("sp", cc[:], I["cc"], [], [tk("cc")])
            act(lambda e: e.activation(out=cc[:], in_=cc[:], func=AF.Silu), [tk("cc")], [tk("cc")])
            p, pt = nps()
            for k in range(8):
                tr(p[:, k * 17:(k + 1) * 17], cc[:, k * 128:(k + 1) * 128], ident[0:17, 0:17], [tk("cc")] + CI, pt)
            copy("dve", csT[:].rearrange("p k s -> p (k s)"), p[:, 0:136], pt, [tk("csT")])
            wc = [sb("wc%d" % i, [128, 8, 512]) for i in range(2)]
            wcv = I["w_cond"].rearrange("(k p) c -> p k c", p=128)
            for blk in range(12):
                wt = wc[blk % 2]; wtk = tk("wc", blk % 2)
                S.dma(("sp", "pool")[blk % 2], wt[:], wcv[:, :, blk * 512:(blk + 1) * 512], [], [wtk])
                if blk % 4 == 0:
                    p, pt = nps()
                for jj in range(4):
                    j = blk * 4 + jj
                    o = (j % 16) * 17
                    for k in range(8):
                        mm(p[:, o:o + 17], wt[:, k, jj * 128:(jj + 1) * 128], csT[:, k, :], k == 0, k == 7, [wtk, tk("csT")], pt)
                if blk % 4 == 3:
                    g = blk // 4
                    dve(lambda e: e.tensor_tensor(out=modT[:, g * 16:(g + 1) * 16, :], in0=p[:, 0:272].rearrange("p (j s) -> p j s", s=17),
                                                  in1=bcT[:, g * 16:(g + 1) * 16].unsqueeze(2).to_broadcast([128, 16, 17]), op=ALU.add),
                        pt + [tk("pkT0")], [tk("modT")])
            dve(lambda e: e.tensor_scalar_add(out=modT[:, 8:16, :], in0=modT[:, 8:16, :], scalar1=1.0), [tk("modT")], [tk("modT")])
            dve(lambda e: e.tensor_scalar_add(out=modT[:, 32:40, :], in0=modT[:, 32:40, :], scalar1=1.0), [tk("modT")], [tk("modT")])
            barrier()
        cur[0] = ctx
        MT = [tk("modT")]

        XLOADED = set()

        def hT_load(i):
            x = stg[i % 3]; xk = tk("stg", i % 3)
            S.dma("sp", x[:, 0:D], I["xp"][i * 128:(i + 1) * 128, :], [], [xk])
            XLOADED.add(i)

        def make_hT(i, want_out, want_shift):
            ht = hTt[i % 2]; htk = tk("hTt", i % 2)
            x = stg[i % 3]; xk = tk("stg", i % 3)
            if i < 16:
                if i in XLOADED:
                    XLOADED.discard(i)
                else:
                    S.dma("sp", x[:, 0:D], I["xp"][i * 128:(i + 1) * 128, :], [], [xk])
                for half in range(2):
                    p, pt = nps()
                    for kk in range(4):
                        k = half * 4 + kk
                        tr(p[:, kk * 128:(kk + 1) * 128], x[:, k * 128:(k + 1) * 128], ident[:], [xk] + CI, pt)
                    for kk in range(4):
                        k = half * 4 + kk
                        en = evac_eng()
                        src = p[:, kk * 128:(kk + 1) * 128]
                        dst = ht[:, k, :]
                        if en == "dve":
                            dve(lambda e: e.tensor_scalar(out=dst, in0=src, scalar1=modT[:, 8 + k, 0:1], scalar2=modT[:, k, 0:1],
                                                          op0=ALU.mult, op1=ALU.add), pt + MT, [htk])
                        else:
                            act(lambda e: e.activation(out=dst, in_=src, func=AF.Identity, bias=modT[:, k, 0:1], scale=modT[:, 8 + k, 0:1]),
                                pt + MT, [htk])
                        if i == 15 and want_out:
                            dve(lambda e: e.tensor_scalar(out=hl[:, k:k + 1], in0=p[:, kk * 128 + 127:kk * 128 + 128], scalar1=modT[:, 8 + k, 0:1],
                                                          scalar2=modT[:, k, 0:1], op0=ALU.mult, op1=ALU.add), pt + MT, [tk("hl")])
                if i == 15 and want_out:
                    p, pt = nps(2)
                    for k in range(8):
                        tr(p[0:1, k * 128:(k + 1) * 128], hl[:, k:k + 1], ident[:], [tk("hl")] + CI, pt)
                    rw = stg[(i + 1) % 3]; rwk = tk("stg", (i + 1) % 3)
                    copy("act", rw[0:1, 0:D], p[0:1, :], pt, [rwk])
                    S.dma("sp", O["shift_p"], rw[0:1, 0:D], [rwk], [tk("o_shift_p")])
            else:
                S.dma("sp", x[0:NST, 0:D], I["xs"], [], [xk])
                p, pt = nps()
                for k in range(8):
                    tr(p[:, k * 64:(k + 1) * 64], x[0:NST, k * 128:(k + 1) * 128], ident[0:NST, 0:NST], [xk] + CI, pt)
                hsv = hs32[:].rearrange("p k (s t) -> p k s t", t=TS)
                dve(lambda e: e.tensor_tensor(out=hsv, in0=p[:, :].rearrange("p (k s t) -> p k s t", k=8, t=TS),
                                              in1=modT[:, 8:16, 1:17].unsqueeze(3).to_broadcast([128, 8, NS, TS]), op=ALU.mult), pt + MT, [tk("hs32")])
                dve(lambda e: e.tensor_tensor(out=hsv, in0=hsv, in1=modT[:, 0:8, 1:17].unsqueeze(3).to_broadcast([128, 8, NS, TS]), op=ALU.add),
                    [tk("hs32")] + MT, [tk("hs32")])
                copy("act", ht[:, :, 0:NST], hs32[:], [tk("hs32")], [htk])
                if want_out:
                    p, pt = nps(2)
                    for k in range(8):
                        tr(p[0:NS, k * 128:(k + 1) * 128], hsv[:, k, :, TS - 1], ident[:], [tk("hs32")] + CI, pt)
                    rw = stg[(i + 2) % 3]; rwk = tk("stg", (i + 2) % 3)
                    copy("act", rw[0:NS, 0:D], p[0:NS, :], pt, [rwk])
                    S.dma("sp", O["shift_s"], rw[0:NS, 0:D], [rwk], [tk("o_shift_s")])
                if want_shift:
                    x2 = stg[(i + 1) % 3]; x2k = tk("stg", (i + 1) % 3)
                    S.dma("sp", x2[0:NS, 0:D], I["stshift"], [], [x2k])
                    p, pt = nps()
                    for k in range(8):
                        tr(p[:, k * 16:(k + 1) * 16], x2[0:NS, k * 128:(k + 1) * 128], ident[0:NS, 0:NS], [x2k] + CI, pt)
                    copy("dve", hsh[:], p[:, 0:128].rearrange("p (k s) -> p k s", k=8), pt, [tk("hsh")])
            return ht, htk, x, xk

        def load_weights(W, wtk, specs, src):
            n = 0
            nk = W.shape[1]
            for k in range(nk):
                for (c0, c1, d0) in specs:
                    st = stg[n % 3]; stk = tk("stg", n % 3)
                    S.dma(("sp", "pool")[n % 2], st[:, 0:c1 - c0], src[k * 128:(k + 1) * 128, c0:c1], [], [stk])
                    en = ("dve", "act")[n % 2]
                    copy(en, W[:, k, d0:d0 + (c1 - c0)], st[:, 0:c1 - c0], [stk], [wtk])
                    n += 1
        with ExitStack() as c1:
            cur[0] = c1
            Wm = sb("Wm", [128, 8, 4104], BF16); WMK = tk("Wm")
            load_weights(Wm, WMK, [(0, 1540, 0), (1540, 3080, 1540), (OFF_GATE, OFF_GATE + 1024, 3080)], I["w_in"])
            normw = sb("normw", [128, D])
            S.dma("sp", normw[:], I["vecs"][0:1, :].to_broadcast([128, D]), [], [tk("normw")])
            qkpad1 = sb("qkpad", [128, 16 * 131]); carry = sb("carry", [128, 16, 3])
            acc = sb("acc", [128, 8, 128]); acc2 = sb("acc2", [128, 8, 128]); acc3 = sb("acc3", [128, 8, 128]); acc4 = sb("acc4", [128, 8, 128])
            qkT = sb("qkT", [128, 16, 128], BF16)
            k_tok = sb("k_tok", [128, D], BF16)
            v_ext = sb("v_ext", [128, 4, 257], BF16)
            ga = sb("ga", [128, D])
            R_igL = [sb("R_ig%d" % j, [4, 128]) for j in range(2)]; R_eL = [sb("R_e%d" % j, [4, 128]) for j in range(2)]
            R_F = [sb("R_F%d" % i, [4, 128]) for i in range(2)]
            R_Mx = [sb("R_Mx%d" % i, [4, 128]) for i in range(2)]
            R_cL = [sb("R_c%d" % j, [4, 128]) for j in range(2)]; R_AL = [sb("R_A%d" % j, [4, 128]) for j in range(2)]
            R_AWL = [sb("R_AW%d" % j, [4, 128]) for j in range(2)]
            tokrL = [sb("tokr%d" % j, [128, 16]) for j in range(2)]; emtL = [sb("emt%d" % j, [128, 4]) for j in range(2)]
            wit = sb("wit", [128, 4]); mtokL = [sb("mtok%d" % j, [128, 4]) for j in range(2)]
            E = sb("E", [128, 4, 128]); WI = sb("WI", [128, 4, 128])
            qtil = sb("qtil", [128, 2, 128], BF16)
            qtil4 = sb("qtil4", [128, 4, 2, 128], BF16); ST4 = sb("ST4", [128, 4, 128], BF16); dn4 = sb("dn4", [128, 4, 4])
            ST = sb("ST", [128, 128], BF16)
            X = {}
            HTS = {}
            GDONE = set()
            h_a = sb("h_a", [128, 4, 256]); xc = sb("xc", [128, 4, 256])
            sm = sb("sm", [128, 16])
            dn = sb("dn", [128, 4])
            pool(lambda e: e.memset(v_ext[:], 1.0), [], [tk("v_ext")])
            pool(lambda e: e.memset(carry[:], 0.0), [], [tk("carry")])

            def gates(i, ht, htk):
                T = 128 if i < 16 else NST
                smp = i == 16
                pp_ = i % 2
                R_ig = R_igL[pp_]; R_e = R_eL[pp_]; R_c = R_cL[pp_]; R_A = R_AL[pp_]; R_AW = R_AWL[pp_]
                tokr = tokrL[pp_]; emt = emtL[pp_]; mtok = mtokL[pp_]
                p, pt = nps()
                for k in range(8):
                    mm(p[0:4, 0:T], Wm[:, k, 3072:3076], ht[:, k, 0:T], k == 0, k == 7, [WMK, htk], pt)
                for k in range(8):
                    mm(p[0:4, 128:128 + T], Wm[:, k, 3076:3080], ht[:, k, 0:T], k == 0, k == 7, [WMK, htk], pt)
                RK = [tk("rowsA", pp_)]
                act(lambda e: e.activation(out=R_ig[:, 0:T], in_=p[0:4, 0:T], func=AF.Identity, bias=ifb[:, 0:1], scale=1.0), pt + [tk("ifb")], RK)
                act(lambda e: e.activation(out=R_e[:, 0:T], in_=p[0:4, 128:128 + T], func=AF.Exp, bias=nfb[:, 0:1], scale=-1.0), pt + [tk("nfb")], RK)
                act(lambda e: e.activation(out=R_e[:, 0:T], in_=R_e[:, 0:T], func=AF.Ln, bias=1.0, scale=1.0), RK, RK)
                RF = R_F[i % 2]; RFp = R_F[(i + 1) % 2]; RM = R_Mx[i % 2]; RMp = R_Mx[(i + 1) % 2]
                if not smp:
                    dve(lambda e: e.tensor_tensor_scan(out=RF[:, 0:T], data0=ones4[:, 0:T], data1=R_e[:, 0:T],
                                                       initial=(0.0 if i == 0 else RFp[:, 127:128]), op0=ALU.mult, op1=ALU.subtract), RK + [tk("c4")], RK)
                    dve(lambda e: e.tensor_tensor(out=R_c[:, 0:T], in0=R_ig[:, 0:T], in1=RF[:, 0:T], op=ALU.subtract), RK, RK)
                    dve(lambda e: e.tensor_tensor_scan(out=RM[:, 0:T], data0=zeros4[:, 0:T], data1=R_c[:, 0:T],
                                                       initial=(0.0 if i == 0 else RMp[:, 127:128]), op0=ALU.add, op1=ALU.max), RK + [tk("c4")], RK)
                    dve(lambda e: e.tensor_scalar(out=R_A[:, 0:T], in0=RM[:, 0:T], scalar1=-1.0, scalar2=None, op0=ALU.mult), RK, RK)
                    if i == 0:
                        dve(lambda e: e.tensor_copy(out=R_AW[:, 0:T], in_=R_A[:, 0:T]), RK, RK)
                    else:
                        dve(lambda e: e.tensor_scalar(out=R_AW[:, 0:T], in0=R_A[:, 0:T], scalar1=RMp[:, 127:128], scalar2=None, op0=ALU.add), RK, RK)
                else:
                    m0r = sb("m0r", [4, NS])
                    S.dma("sp", m0r[:], I["stm"].rearrange("s h -> h s"), [], [tk("m0r")], allow_slow_non_contiguous=True)
                    v4 = lambda t_: t_[:, 0:NST].rearrange("p (s t) -> p s t", t=TS)
                    Fv = v4(RF); ev = v4(R_e); cvw = v4(R_c); igv = v4(R_ig); Mv = v4(RM); Av = v4(R_A); AWv = v4(R_AW)
                    dve(lambda e: e.tensor_scalar(out=Fv[:, :, 0], in0=ev[:, :, 0], scalar1=-1.0, scalar2=None, op0=ALU.mult), RK, RK)
                    for t in range(1, TS):
                        dve(lambda e: e.tensor_tensor(out=Fv[:, :, t], in0=Fv[:, :, t - 1], in1=ev[:, :, t], op=ALU.subtract), RK, RK)
                    dve(lambda e: e.tensor_tensor(out=R_c[:, 0:T], in0=R_ig[:, 0:T], in1=RF[:, 0:T], op=ALU.subtract), RK, RK)
                    dve(lambda e: e.tensor_tensor(out=Mv[:, :, 0], in0=cvw[:, :, 0], in1=m0r[:, :], op=ALU.max), RK + [tk("m0r")], RK)
                    for t in range(1, TS):
                        dve(lambda e: e.tensor_tensor(out=Mv[:, :, t], in0=Mv[:, :, t - 1], in1=cvw[:, :, t], op=ALU.max), RK, RK)
                    dve(lambda e: e.tensor_scalar(out=R_A[:, 0:T], in0=RM[:, 0:T], scalar1=-1.0, scalar2=None, op0=ALU.mult), RK, RK)
                    dve(lambda e: e.tensor_tensor(out=AWv, in0=Av, in1=m0r[:, :].unsqueeze(2).to_broadcast([4, NS, TS]), op=ALU.add), RK + [tk("m0r")], RK)
                p, pt = nps()
                for q, Rr in enumerate((R_c, R_A, RF, R_AW)):
                    tr(p[0:T, q * 4:(q + 1) * 4], Rr[:, 0:T], ident[0:4, 0:4], RK + CI, pt)
                copy("dve", tokr[0:T, :], p[0:T, 0:16], pt, [tk("tokr", pp_)])
                dve(lambda e: e.tensor_tensor(out=mtok[0:T, :], in0=tokr[0:T, 8:12], in1=tokr[0:T, 4:8], op=ALU.subtract), [tk("tokr", pp_)], [tk("mtok", pp_)])
                act(lambda e: e.activation(out=emt[0:T, :], in_=mtok[0:T, :], func=AF.Exp, scale=-1.0), [tk("mtok", pp_)], [tk("emt", pp_)])
                if smp:
                    return m0r
                return None

            def tile(i):
                CT = X.get("CT"); CTb = X.get("CTb"); wv = X.get("wv"); wv4 = X.get("wv4")
                T = 128 if i < 16 else NST
                smp = i == 16
                if i not in HTS:
                    HTS[i] = make_hT(i, True, False)
                ht, htk, x, xk = HTS.pop(i)
                if i + 1 < 16:
                    hT_load(i + 1)
                pp_ = i % 2
                R_A = R_AL[pp_]; R_AW = R_AWL[pp_]; tokr = tokrL[pp_]; emt = emtL[pp_]; mtok = mtokL[pp_]; RK = [tk("rowsA", pp_)]
                if i not in GDONE:
                    gates(i, ht, htk)
                GDONE.discard(i)
                pad = qkpad1; padk = tk("qkpad")
                if not smp:
                    padv = pad[:].rearrange("p (c t) -> p c t", t=131)
                    pool(lambda e: e.tensor_copy(out=padv[:, :, 0:3], in_=carry[:]), [tk("carry")], [padk])
                else:
                    padv = pad[:, 0:16 * 112].rearrange("p (c s j) -> p c s j", s=NS, j=7)
                    cv = stg[2]; cvk = tk("stg", 2)
                    S.dma("sp", cv[0:48, 0:1024], I["stconv"][:, 0:1024], [], [cvk])
                    cv2 = stg[1]; cv2k = tk("stg", 1)
                    S.dma("sp", cv2[0:48, 0:1024], I["stconv"][:, 1024:2048], [], [cv2k])
                    for hf, (cvx, cvxk) in enumerate(((cv, cvk), (cv2, cv2k))):
                        p, pt = nps()
                        for c in range(8):
                            tr(p[:, c * 48:(c + 1) * 48], cvx[0:48, c * 128:(c + 1) * 128], ident[0:48, 0:48], [cvxk] + CI, pt)
                        copy("dve", padv[:, hf * 8:(hf + 1) * 8, :, 0:3], p[:, 0:384].rearrange("p (c s j) -> p c s j", s=NS, j=3), pt, [padk])
                for cg in range(4):
                    p, pt = nps()
                    for c4 in range(4):
                        c = cg * 4 + c4
                        for k in range(8):
                            mm(p[:, c4 * T:(c4 + 1) * T], Wm[:, k, c * 128:(c + 1) * 128], ht[:, k, 0:T], k == 0, k == 7, [WMK, htk], pt)
                    if not smp:
                        copy(evac_eng(), padv[:, cg * 4:(cg + 1) * 4, 3:131], p[:, 0:512].rearrange("p (c t) -> p c t", t=128), pt, [padk])
                    else:
                        copy(evac_eng(), padv[:, cg * 4:(cg + 1) * 4, :, 3:7], p[:, 0:256].rearrange("p (c s t) -> p c s t", s=NS, t=TS), pt, [padk])
                if i == 15:
                    p, pt = nps(4)
                    for c in range(16):
                        tr(p[0:3, c * 128:(c + 1) * 128], padv[:, c, 128:131], ident[:], [padk] + CI, pt)
                    cvo = stg[2]; cvok = tk("stg", 2)
                    copy("act", cvo[0:3, 0:1024], p[0:3, 0:1024], pt, [cvok])
                    S.dma("sp", O["conv_p"][:, 0:1024], cvo[0:3, 0:1024], [cvok], [tk("o_conv_p")])
                    cvo = stg[1]; cvok = tk("stg", 1)
                    copy("act", cvo[0:3, 0:1024], p[0:3, 1024:2048], pt, [cvok])
                    S.dma("sp", O["conv_p"][:, 1024:2048], cvo[0:3, 0:1024], [cvok], [tk("o_conv_p")])
                if smp:
                    cst = acc[:].rearrange("p c t -> p (c t)")[:, 0:768].rearrange("p (c s j) -> p c s j", s=NS, j=3)
                    ACCK = [tk("acc", c8_) for c8_ in range(8)]
                    pool(lambda e: e.tensor_copy(out=cst, in_=padv[:, :, :, 4:7]), [padk], ACCK)
                    cst2 = acc[:].rearrange("p c t -> p (c t)")[:, 0:768].rearrange("p (c m) -> p c m", m=48)
                    p, pt = nps(4)
                    for c in range(16):
                        tr(p[0:48, c * 128:(c + 1) * 128], cst2[:, c, :], ident[:], ACCK + CI, pt)
                    for hf in range(2):
                        cvo = stg[2 - hf]; cvok = tk("stg", 2 - hf)
                        copy("act", cvo[0:48, 0:1024], p[0:48, hf * 1024:(hf + 1) * 1024], pt, [cvok])
                        S.dma("sp", O["conv_s"][:, hf * 1024:(hf + 1) * 1024], cvo[0:48, 0:1024], [cvok], [tk("o_conv_s")])
                for c in range(16):
                    c8 = c % 8
                    if not smp:
                        av = acc[:, c8, :]
                        sl = lambda j: padv[:, c, j:j + 128]
                    else:
                        av = acc[:, c8, 0:NST].rearrange("p (s t) -> p s t", t=TS)
                        sl = lambda j: padv[:, c, :, j:j + 4]
                    wj = lambda j: pkT0[:, 48 + 16 * j + c:49 + 16 * j + c]
                    ak = tk("acc", c8)
                    dve(lambda e: e.tensor_scalar(out=av, in0=sl(0), scalar1=wj(0), scalar2=None, op0=ALU.mult), [padk, tk("pkT0")], [ak])
                    for j in range(1, 4):
                        dve(lambda e: e.scalar_tensor_tensor(out=av, in0=sl(j), scalar=wj(j), in1=av, op0=ALU.mult, op1=ALU.add), [padk, tk("pkT0"), ak], [ak])
                    act(lambda e: e.activation(out=qkT[:, c, 0:T], in_=acc[:, c8, 0:T], func=AF.Silu, bias=pkT0[:, 112 + c:113 + c]),
                        [ak, tk("pkT0")], [tk("qkT")])
                if not smp:
                    pool(lambda e: e.tensor_copy(out=carry[:], in_=padv[:, :, 128:131]), [padk], [tk("carry")])
                if i + 1 < 16:
                    HTS[i + 1] = make_hT(i + 1, True, False)
                    gates(i + 1, HTS[i + 1][0], HTS[i + 1][1])
                    GDONE.add(i + 1)
                p, pt = nps()
                pb = p.bitcast(BF16)
                for c in range(8):
                    tr(pb[0:T, c * 128:(c + 1) * 128], qkT[:, 8 + c, 0:T], identb[:], [tk("qkT")] + CIB, pt)
                dve(lambda e: e.tensor_scalar(out=k_tok[0:T, :], in0=pb[0:T, 0:1024], scalar1=0.0625, scalar2=None, op0=ALU.mult), pt, [tk("k_tok")])
                p, pt = nps(2)
                for j in range(2):
                    for k in range(8):
                        mm(p[0:T, j * 512:(j + 1) * 512], ht[:, k, 0:T], Wm[:, k, 2048 + j * 512:2048 + (j + 1) * 512], k == 0, k == 7, [WMK, htk], [pt[j]])
                copy("act", v_ext[0:T, :, 0:256], p[0:T, 0:1024].rearrange("p (h v) -> p h v", h=4), pt, [tk("v_ext")])
                p, pt = nps(2)
                for j in range(2):
                    for k in range(8):
                        mm(p[0:T, j * 512:(j + 1) * 512], ht[:, k, 0:T], Wm[:, k, 3080 + j * 512:3080 + (j + 1) * 512], k == 0, k == 7, [WMK, htk], [pt[j]])
                act(lambda e: e.activation(out=ga[0:T, :], in_=p[0:T, 0:1024], func=AF.Sigmoid), pt, [tk("ga")])
                if i == 15:
                    S.dma("sp", O["m_p"], mtok[127:128, :], [tk("mtok", pp_)], [tk("o_m_p")])
                if smp:
                    act(lambda e: e.activation(out=wit[0:T, :], in_=tokr[0:T, 12:16], func=AF.Exp), [tk("tokr", pp_)], [tk("wit")])
                    p, pt = nps()
                    mm(p[0:NS, 0:4], sel_last[:, :], mtok[0:NST, :], True, True, [tk("sel_last"), tk("mtok", pp_)], pt)
                    mm(p[0:NS, 4:8], sel_last[:, :], wit[0:NST, :], True, True, [tk("sel_last"), tk("wit")], pt)
                    msd = sb("msd", [NS, 8])
                    copy("dve", msd[:], p[0:NS, 0:8], pt, [tk("msd")])
                    S.dma("sp", O["m_s"], msd[:, 0:4], [tk("msd")], [tk("o_m_s")])
                    n_sh = sb("n_sh", [NS, 4, 256]); nT = sb("nT", [128, 2, 4, NS], BF16)
                    S.dma("sp", n_sh[:], I["stn"].rearrange("(s h) k -> s h k", h=4), [], [tk("n_sh")])
                    p, pt = nps()
                    for kc in range(2):
                        for h in range(4):
                            tr(p[:, (kc * 4 + h) * NS:(kc * 4 + h + 1) * NS], n_sh[:, h, kc * 128:(kc + 1) * 128], ident[0:NS, 0:NS], [tk("n_sh")] + CI, pt)
                    copy("dve", nT[:].rearrange("p a h s -> p (a h s)"), p[:, 0:128], pt, [tk("nT")])
                    Cnat = [sb("Cnat%d" % j, [128, 2, 256]) for j in range(3)]
                    Cnew = [sb("Cnew%d" % j, [128, 2, 256]) for j in range(2)]
                    CTs = [sb("CTs%d" % j, [128, 2, 257], BF16) for j in range(2)]
                    qpad = sb("qpad", [128, 2, NS, NST], BF16)
                    wvm = [sb("wvm%d" % j, [NST, 256], BF16) for j in range(2)]
                NEG = NEGs if smp else NEGp
                if not smp:
                    pAs = []
                    for h in range(4):
                        pA, pAt = nps()
                        mm(pA[0:T, 0:T], ident[0:4, h:h + 1].to_broadcast([4, T]), R_A[:, 0:T], True, False, RK + CI, pAt)
                        mm(pA[0:T, 0:T], ident[0:T, 0:T], NEG[0:T, 0:T], False, True, CI + [tk("NEGp"), tk("NEGs")], pAt)
                        mm(pA[:, 128:128 + T], ident[0:4, h:h + 1].to_broadcast([4, 128]), R_AW[:, 0:T], True, True, RK + CI, pAt)
                        pAs.append((pA, pAt))
                    for h in range(4):
                        pA, pAt = pAs[h]
                        act(lambda e: e.activation(out=E[0:T, h, 0:T], in_=pA[0:T, 0:T], func=AF.Exp, bias=tokr[0:T, h:h + 1], scale=1.0),
                            pAt + [tk("tokr", pp_)], [tk("E", h)])
                        act(lambda e: e.activation(out=WI[:, h, 0:T], in_=pA[:, 128:128 + T], func=AF.Exp), pAt, [tk("WI", h)])
                    for h in range(4):
                        dve(lambda e: e.tensor_tensor(out=qtil4[:, h, :, :], in0=qkT[:, 2 * h:2 * h + 2, :],
                                                      in1=WI[:, h, :].unsqueeze(1).to_broadcast([128, 2, 128]), op=ALU.mult),
                            [tk("qkT"), tk("WI", h)], [tk("qtil4", h)])
                    p2s = []
                    for h in range(4):
                        p2, p2t = nps()
                        for ch in range(2):
                            mm(p2[:, 0:128], qkT[:, 8 + 2 * h + ch, :], qkT[:, 2 * h + ch, :], ch == 0, ch == 1, [tk("qkT")], p2t)
                        p2s.append((p2, p2t))
                    for h in range(4):
                        p2, p2t = p2s[h]
                        dve(lambda e: e.scalar_tensor_tensor(out=ST4[:, h, :], in0=p2[:, 0:128], scalar=0.0625, in1=E[:, h, :], op0=ALU.mult, op1=ALU.mult),
                            p2t + [tk("E", h)], [tk("ST4", h)])
                    p3s = []
                    for h in range(4):
                        p3, p3t = nps()
                        for ch in range(2):
                            mm(p3[:, 0:257], qtil4[:, h, ch, :], CTb[:, ch, h, :], ch == 0, False, [tk("qtil4", h), tk("CTb", h)], p3t)
                        mm(p3[:, 0:257], ST4[:, h, :], v_ext[:, h, :], False, True, [tk("ST4", h), tk("v_ext")], p3t)
                        p3s.append((p3, p3t))
                    for h in range(4):
                        p3, p3t = p3s[h]
                        act(lambda e: e.activation(out=dn4[:, h, 0:1], in_=p3[:, 256:257], func=AF.Abs), p3t, [tk("dn4", h)])
                        dve(lambda e: e.tensor_tensor(out=dn4[:, h, 1:2], in0=dn4[:, h, 0:1], in1=emt[:, h:h + 1], op=ALU.max), [tk("dn4", h), tk("emt", pp_)], [tk("dn4", h)])
                        dve(lambda e: e.reciprocal(out=dn4[:, h, 2:3], in_=dn4[:, h, 1:2]), [tk("dn4", h)], [tk("dn4", h)])
                        dve(lambda e: e.tensor_scalar(out=h_a[:, h, :], in0=p3[:, 0:256], scalar1=dn4[:, h, 2:3], scalar2=None, op0=ALU.mult),
                            p3t + [tk("dn4", h)], [tk("h_a")])
                    for h in range(4):
                        act(lambda e: e.activation(out=wv4[:, h, :], in_=v_ext[:, h, :], func=AF.Identity, scale=E[:, h, 127:128]),
                            [tk("v_ext"), tk("E", h)], [tk("wv4", h)])
                    for h in range(4):
                        p4, p4t = nps(2)
                        for ch in range(2):
                            mm(p4[:, ch * 512:ch * 512 + 257], k_tok[:, h * 256 + ch * 128:h * 256 + (ch + 1) * 128], wv4[:, h, :], True, True,
                               [tk("k_tok"), tk("wv4", h)], [p4t[ch]])
                        dve(lambda e: e.scalar_tensor_tensor(out=CT[:, :, h, :], in0=CT[:, :, h, :], scalar=WI[:, h, 127:128],
                                                             in1=p4[:, :].rearrange("p (a k) -> p a k", a=2)[:, :, 0:257], op0=ALU.mult, op1=ALU.add),
                            [tk("CT", h), tk("WI", h)] + p4t, [tk("CT", h)])
                        copy("act", CTb[:, :, h, :], CT[:, :, h, :], [tk("CT", h)], [tk("CTb", h)])
                for h in (range(4) if smp else []):
                    pA, pAt = nps()
                    mm(pA[0:T, 0:T], ident[0:4, h:h + 1].to_broadcast([4, T]), R_A[:, 0:T], True, False, RK + CI, pAt)
                    mm(pA[0:T, 0:T], ident[0:T, 0:T], NEG[0:T, 0:T], False, True, CI + [tk("NEGp"), tk("NEGs")], pAt)
                    mm(pA[:, 128:128 + T], ident[0:4, h:h + 1].to_broadcast([4, 128]), R_AW[:, 0:T], True, True, RK + CI, pAt)
                    act(lambda e: e.activation(out=E[0:T, h, 0:T], in_=pA[0:T, 0:T], func=AF.Exp, bias=tokr[0:T, h:h + 1], scale=1.0),
                        pAt + [tk("tokr", pp_)], [tk("E", h)])
                    act(lambda e: e.activation(out=WI[:, h, 0:T], in_=pA[:, 128:128 + T], func=AF.Exp), pAt, [tk("WI", h)])
                    dve(lambda e: e.tensor_tensor(out=qtil[:, :, 0:T], in0=qkT[:, 2 * h:2 * h + 2, 0:T],
                                                  in1=WI[:, h, 0:T].unsqueeze(1).to_broadcast([128, 2, T]), op=ALU.mult),
                        [tk("qkT"), tk("WI", h)], [tk("qtil")])
                    p2, p2t = nps()
                    for ch in range(2):
                        mm(p2[0:T, 0:T], qkT[:, 8 + 2 * h + ch, 0:T], qkT[:, 2 * h + ch, 0:T], ch == 0, ch == 1, [tk("qkT")], p2t)
                    dve(lambda e: e.scalar_tensor_tensor(out=ST[0:T, 0:T], in0=p2[0:T, 0:T], scalar=0.0625, in1=E[0:T, h, 0:T], op0=ALU.mult, op1=ALU.mult),
                        p2t + [tk("E", h)], [tk("ST")])
                    p3, p3t = P7, P7t
                    if not smp:
                        for ch in range(2):
                            mm(p3[0:T, 0:257], qtil[:, ch, 0:T], CTb[:, ch, h, :], ch == 0, False, [tk("qtil"), tk("CTb")], p3t)
                    else:
                        dve(lambda e: e.tensor_tensor(out=qpad[:], in0=qtil[:, :, 0:NST].unsqueeze(2).to_broadcast([128, 2, NS, NST]),
                                                      in1=BM[:].unsqueeze(1).to_broadcast([128, 2, NS, NST]), op=ALU.mult),
                            [tk("qtil"), tk("BM")], [tk("qpad")])
                        def stA(s):
                            pr_ = s * 4 + h
                            Cn = Cnat[pr_ % 3]; Cnk = tk("Cnat", pr_ % 3)
                            S.dma(("sp", "act")[s % 2], Cn[:], I["stC"][pr_ * 256:(pr_ + 1) * 256, :].rearrange("(vc p) k -> p vc k", p=128), [], [Cnk])
                            pT, pTt = nps()
                            for kc in range(2):
                                for vc in range(2):
                                    tr(pT[:, kc * 256 + vc * 128:kc * 256 + (vc + 1) * 128], Cn[:, vc, kc * 128:(kc + 1) * 128], ident[:], [Cnk] + CI, pTt)
                            Cs = CTs[s % 2]; Csk = tk("CTs", s % 2)
                            copy("act", Cs[:, :, 0:256], pT[:, 0:512].rearrange("p (a v) -> p a v", a=2), pTt, [Csk])
                            pool(lambda e: e.tensor_copy(out=Cs[:, :, 256:257], in_=nT[:, :, h, s:s + 1]), [tk("nT")], [Csk])

                        def stB(s):
                            pr_ = s * 4 + h
                            Cn = Cnat[pr_ % 3]; Cnk = tk("Cnat", pr_ % 3)
                            Cs = CTs[s % 2]; Csk = tk("CTs", s % 2)
                            for ch in range(2):
                                mm(p3[0:T, 0:257], qpad[:, ch, s, :], Cs[:, ch, :], (s == 0 and ch == 0), False, [tk("qpad"), Csk], p3t)
                            wm_ = wvm[s % 2]; wmk = tk("wvm", s % 2)
                            act(lambda e: e.activation(out=wm_[:, :], in_=v_ext[0:NST, h, 0:256], func=AF.Identity, scale=E[0:NST, h, 4 * s + 3:4 * s + 4]),
                                [tk("v_ext"), tk("E", h)], [wmk])
                            pU, pUt = nps()
                            for vc in range(2):
                                mm(pU[:, vc * 256:(vc + 1) * 256], wm_[:, vc * 128:(vc + 1) * 128], k_tok[0:NST, h * 256:(h + 1) * 256], True, True,
                                   [wmk, tk("k_tok")], pUt)
                            Cw = Cnew[s % 2]; Cwk = tk("Cnew", s % 2)
                            dve(lambda e: e.scalar_tensor_tensor(out=Cw[:], in0=Cn[:], scalar=WI[:, h, 4 * s + 3:4 * s + 4],
                                                                 in1=pU[:, 0:512].rearrange("p (a k) -> p a k", a=2), op0=ALU.mult, op1=ALU.add),
                                [Cnk, tk("WI", h)] + pUt, [Cwk])
                            S.dma("pool", O["C_s"][pr_ * 256:(pr_ + 1) * 256, :].rearrange("(vc p) k -> p vc k", p=128), Cw[:], [Cwk], [tk("o_C_s")])

                        stA(0)
                        for s in range(NS):
                            if s + 1 < NS:
                                stA(s + 1)
                            stB(s)
                    mm(p3[0:T, 0:257], ST[0:T, 0:T], v_ext[0:T, h, :], False, True, [tk("ST"), tk("v_ext")], p3t)
                    act(lambda e: e.activation(out=dn[0:T, 0:1], in_=p3[0:T, 256:257], func=AF.Abs), p3t, [tk("dn")])
                    dve(lambda e: e.tensor_tensor(out=dn[0:T, 1:2], in0=dn[0:T, 0:1], in1=emt[0:T, h:h + 1], op=ALU.max), [tk("dn"), tk("emt", pp_)], [tk("dn")])
                    dve(lambda e: e.reciprocal(out=dn[0:T, 2:3], in_=dn[0:T, 1:2]), [tk("dn")], [tk("dn")])
                    dve(lambda e: e.tensor_scalar(out=h_a[0:T, h, :], in0=p3[0:T, 0:256], scalar1=dn[0:T, 2:3], scalar2=None, op0=ALU.mult),
                        p3t + [tk("dn")], [tk("h_a")])
                    if not smp:
                        pool(lambda e: e.tensor_scalar(out=wv[0:T, :], in0=v_ext[0:T, h, :], scalar1=E[0:T, h, T - 1:T], scalar2=None, op0=ALU.mult),
                             [tk("v_ext"), tk("E", h)], [tk("wv")])
                        p4, p4t = nps(2)
                        for ch in range(2):
                            mm(p4[:, ch * 512:ch * 512 + 257], k_tok[0:T, h * 256 + ch * 128:h * 256 + (ch + 1) * 128], wv[0:T, :], True, True,
                               [tk("k_tok"), tk("wv")], [p4t[ch]])
                        dve(lambda e: e.scalar_tensor_tensor(out=CT[:, :, h, :], in0=CT[:, :, h, :], scalar=WI[:, h, T - 1:T],
                                                             in1=p4[:, :].rearrange("p (a k) -> p a k", a=2)[:, :, 0:257], op0=ALU.mult, op1=ALU.add),
                            [tk("CT", h), tk("WI", h)] + p4t, [tk("CT", h)])
                        copy("act", CTb[:, :, h, :], CT[:, :, h, :], [tk("CT", h)], [tk("CTb", h)])
                if smp:
                    Elast = sb("Elast", [NST, 4, NS], BF16)
                    copy("dve", Elast[:], E[0:NST, :, 3:NST:4], [tk("E", 0), tk("E", 1), tk("E", 2), tk("E", 3)], [tk("Elast")])
                    p, pt = nps(2)
                    for h in range(4):
                        mm(p[0:NS, h * 256:(h + 1) * 256], Elast[:, h, :], k_tok[0:NST, h * 256:(h + 1) * 256], True, True,
                           [tk("Elast"), tk("k_tok")], [pt[h // 2]])
                    for h in range(4):
                        dve(lambda e: e.scalar_tensor_tensor(out=n_sh[:, h, :], in0=n_sh[:, h, :], scalar=msd[:, 4 + h:5 + h], in1=p[0:NS, h * 256:(h + 1) * 256],
                                                             op0=ALU.mult, op1=ALU.add), [tk("n_sh"), tk("msd")] + pt, [tk("n_sh")])
                    S.dma("sp", O["n_s"].rearrange("(s h) k -> s h k", h=4), n_sh[:], [tk("n_sh")], [tk("o_n_s")])
                dve(lambda e: e.tensor_reduce(out=sm[0:T, 0:4], in_=h_a[0:T], axis=AX.X, op=ALU.add), [tk("h_a")], [tk("sm")])
                dve(lambda e: e.tensor_scalar(out=sm[0:T, 4:8], in0=sm[0:T, 0:4], scalar1=1.0 / 256.0, scalar2=None, op0=ALU.mult), [tk("sm")], [tk("sm")])
                dve(lambda e: e.tensor_tensor(out=xc[0:T], in0=h_a[0:T], in1=sm[0:T, 4:8].unsqueeze(2).to_broadcast([T, 4, 256]), op=ALU.subtract),
                    [tk("h_a"), tk("sm")], [tk("xc")])
                for h in range(4):
                    act(lambda e: e.activation(out=h_a[0:T, h, :], in_=xc[0:T, h, :], func=AF.Square, accum_out=sm[0:T, 8 + h:9 + h]),
                        [tk("xc")], [tk("h_a"), tk("sm")])
                dve(lambda e: e.tensor_scalar(out=sm[0:T, 8:12], in0=sm[0:T, 8:12], scalar1=1.0 / 256.0, scalar2=1e-6, op0=ALU.mult, op1=ALU.add),
                    [tk("sm")], [tk("sm")])
                act(lambda e: e.activation(out=sm[0:T, 12:16], in_=sm[0:T, 8:12], func=AF.Sqrt), [tk("sm")], [tk("sm")])
                dve(lambda e: e.reciprocal(out=sm[0:T, 8:12], in_=sm[0:T, 12:16]), [tk("sm")], [tk("sm")])
                dve(lambda e: e.tensor_tensor(out=xc[0:T], in0=xc[0:T], in1=sm[0:T, 8:12].unsqueeze(2).to_broadcast([T, 4, 256]), op=ALU.mult),
                    [tk("xc"), tk("sm")], [tk("xc")])
                xcf = xc[0:T].rearrange("p h v -> p (h v)")
                pool(lambda e: e.tensor_tensor(out=xcf, in0=xcf, in1=normw[0:T, :], op=ALU.mult), [tk("xc"), tk("normw")], [tk("xc")])
                dve(lambda e: e.tensor_tensor(out=xcf, in0=xcf, in1=ga[0:T, :], op=ALU.mult), [tk("xc"), tk("ga")], [tk("xc")])
                p, pt = nps(2)
                for c in range(8):
                    tr(p[:, c * T:(c + 1) * T], xc[0:T].rearrange("p h v -> p (h v)")[:, c * 128:(c + 1) * 128], ident[0:T, 0:T], [tk("xc")] + CI,
                       [pt[(c * T) // 512]])
                yt_ = yat[i % 2]; ytk = tk("yat", i % 2)
                copy("act", yt_[:, :, 0:T], p[:, 0:8 * T].rearrange("p (c t) -> p c t", t=T), pt, [ytk])
                S.dma("sp", YA[i].rearrange("p (c t) -> p c t", c=8)[:, :, 0:T], yt_[:, :, 0:T], [ytk], [tk("YA", i)])
            with ExitStack() as c1p:
                cur[0] = c1p
                X["CT"] = sb("CT", [128, 2, 4, 257]); X["CTb"] = sb("CTb", [128, 2, 4, 257], BF16); X["wv"] = sb("wv", [128, 257], BF16); X["wv4"] = sb("wv4", [128, 4, 257], BF16)
                CT = X["CT"]
                pool(lambda e: e.memset(X["CT"][:], 0.0), [], [tk("CT", h_) for h_ in range(4)])
                pool(lambda e: e.memset(X["CTb"][:], 0.0), [], [tk("CTb", h_) for h_ in range(4)])
                for i in range(16):
                    tile(i)
                Cout = sb("Cout", [128, 2, 256])
                for h in range(4):
                    pO, pOt = nps()
                    for vc in range(2):
                        for kc in range(2):
                            tr(pO[:, vc * 256 + kc * 128:vc * 256 + (kc + 1) * 128], CT[:, kc, h, vc * 128:(vc + 1) * 128], ident[:], [tk("CT", 0), tk("CT", 1), tk("CT", 2), tk("CT", 3)] + CI, pOt)
                    copy("act", Cout[:], pO[:, 0:512].rearrange("p (a k) -> p a k", a=2), pOt, [tk("Cout")])
                    S.dma("sp", O["C_p"][h * 256:(h + 1) * 256, :].rearrange("(vc p) k -> p vc k", p=128), Cout[:], [tk("Cout")], [tk("o_C_p")])
                p, pt = nps(2)
                for h in range(4):
                    for kc in range(2):
                        o_ = (h * 2 + kc) * 128
                        tr(p[0:1, o_:o_ + 128], CT[:, kc, h, 256:257], ident[:], [tk("CT", 0), tk("CT", 1), tk("CT", 2), tk("CT", 3)] + CI, [pt[o_ // 512]])
                rw = stg[0]; rwk = tk("stg", 0)
                copy("act", rw[0:1, 0:D], p[0:1, :], pt, [rwk])
                S.dma("sp", O["n_p"].rearrange("h k -> (h k)").rearrange("(o n) -> o n", o=1), rw[0:1, 0:D], [rwk], [tk("o_n_p")])
                barrier()
            with ExitStack() as c1s:
                cur[0] = c1s
                tile(16)
                barrier()
        cur[0] = ctx
        SCR6 = nc.dram_tensor("SCR6", [6, NST, D], F32).ap()
        SCRO = nc.dram_tensor("SCRO", [NST, D], F32).ap()
        with ExitStack() as c2:
            cur[0] = c2
            Wr = sb("Wr", [128, 8, 4352], BF16); WRK = tk("Wr")
            load_weights(Wr, WRK, [(3080, 4620, 0), (4620, 6160, 1540), (6160, 6408, 3080), (7432, 8456, 3328)], I["w_in"])
            WA2 = sb("WA2", [128, D], BF16); G2b = sb("G2b", [128, D], BF16)
            S.dma("sp", stg[0][0:64, 0:D], I["w2"], [], [tk("stg", 0)])
            S.dma("sp", stg[0][64:128, 0:D], I["a2"], [], [tk("stg", 0)])
            copy("dve", WA2[:], stg[0][:, 0:D], [tk("stg", 0)], [tk("WA2")])
            S.dma("sp", stg[1][:, 0:D], I["g2"], [], [tk("stg", 1)])
            copy("act", G2b[:], stg[1][:, 0:D], [tk("stg", 1)], [tk("G2b")])
            blk2 = sb("blk2", [128, 128]); ones128 = sb("ones128", [128, 128]); omka = sb("omka", [128, 8])
            pool(lambda e: e.memset(blk2[:], 0.0), [], [tk("blk2")])
            pool(lambda e: e.memset(blk2[0:64, 0:64], 1.0), [], [tk("blk2")])
            pool(lambda e: e.memset(blk2[64:128, 64:128], 1.0), [], [tk("blk2")])
            pool(lambda e: e.memset(ones128[:], 1.0), [], [tk("ones128")])
            eps64 = sb("eps64", [128, 1])
            pool(lambda e: e.memset(eps64[:], 64e-5), [], [tk("eps64")])
            dve(lambda e: e.tensor_scalar(out=omka[:], in0=pkT1[:, 50:58], scalar1=-1.0, scalar2=1.0, op0=ALU.mult, op1=ALU.add), [tk("pkT1")], [tk("omka")])
            PK = [tk("pkT1")]
            prg = [sb("prg%d" % j, [128, 516]) for j in range(2)]
            rcarry = sb("rcarry", [128, 26, 1])
            xsb = sb("xsb", [128, 26, 128])
            ETA = sb("ETA", [128, 8, 128]); KK = sb("KK", [128, 8, 128]); KM = sb("KM", [128, 8, 128])
            LW = sb("LW", [128, 8, 128]); LP = sb("LP", [128, 8, 128]); EX = sb("EX", [128, 8, 128]); TMP = sb("TMP", [128, 8, 128])
            RKV = sb("RKV", [128, 8, 128]); OT = LW
            lo24 = sb("lo24", [128, 128], BF16); sgb = sb("sgb", [128, 128], BF16)
            eplast = sb("eplast", [128, 8, 1])
            pool(lambda e: e.memset(rcarry[:], 0.0), [], [tk("rcarry")])
            mu = pkT1[:, 0:26]
            LWC = -0.6065306597126334

            def bc(ap, n, T):
                return ap.unsqueeze(2).to_broadcast([128, n, T])

            HTS2 = {}

            def front(i):
                T = 128 if i < 16 else NST
                smp = i == 16
                if i not in HTS2:
                    HTS2[i] = make_hT(i, False, smp)
                ht, htk, x, xk = HTS2.pop(i)
                if i + 1 < 16:
                    hT_load(i + 1)
                for grp in range(7):
                    cs0 = 4 * grp; n = min(4, 26 - cs0)
                    p, pt = nps()
                    for ci in range(n):
                        c = cs0 + ci
                        for k in range(8):
                            mm(p[:, ci * T:(ci + 1) * T], Wr[:, k, c * 128:(c + 1) * 128], ht[:, k, 0:T], k == 0, k == 7, [WRK, htk], pt)
                        if smp:
                            for k in range(8):
                                mm(p[:, 256 + ci * 16:256 + (ci + 1) * 16], Wr[:, k, c * 128:(c + 1) * 128], hsh[:, k, :], k == 0, k == 7, [WRK, tk("hsh")], pt)
                    pg = prg[grp % 2]; pgk = tk("prg", grp % 2)
                    if not smp:
                        pv = pg[:, 0:516].rearrange("p (c t) -> p c t", t=129)
                        pool(lambda e: e.tensor_copy(out=pv[:, 0:n, 0:1], in_=rcarry[:, cs0:cs0 + n, :]), [tk("rcarry")], [pgk])
                        copy("act", pv[:, 0:n, 1:129], p[:, 0:n * 128].rearrange("p (c t) -> p c t", t=128), pt, [pgk])
                        prev = pv[:, 0:n, 0:128]; cur_ = pv[:, 0:n, 1:129]
                        dst = xsb[:, cs0:cs0 + n, :]
                        mub = bc(mu[:, cs0:cs0 + n], n, 128)
                        pool(lambda e: e.tensor_copy(out=rcarry[:, cs0:cs0 + n, :], in_=pv[:, 0:n, 128:129]), [pgk], [tk("rcarry")])
                    else:
                        pv = pg[:, 0:320].rearrange("p (c s j) -> p c s j", s=NS, j=5)
                        copy("act", pv[:, 0:n, :, 1:5], p[:, 0:n * 64].rearrange("p (c s t) -> p c s t", s=NS, t=TS), pt, [pgk])
                        copy("act", pv[:, 0:n, :, 0], p[:, 256:256 + n * 16].rearrange("p (c s) -> p c s", s=NS), pt, [pgk])
                        prev = pv[:, 0:n, :, 0:4]; cur_ = pv[:, 0:n, :, 1:5]
                        dst = xsb[:, cs0:cs0 + n, 0:NST].rearrange("p c (s t) -> p c s t", t=TS)
                        mub = mu[:, cs0:cs0 + n].unsqueeze(2).unsqueeze(3).to_broadcast([128, n, NS, TS])
                    dve(lambda e: e.tensor_tensor(out=dst, in0=prev, in1=cur_, op=ALU.subtract), [pgk], [tk("xsb")])
                    dve(lambda e: e.tensor_tensor(out=dst, in0=dst, in1=mub, op=ALU.mult), [tk("xsb")] + PK, [tk("xsb")])
                    dve(lambda e: e.tensor_tensor(out=dst, in0=dst, in1=cur_, op=ALU.add), [tk("xsb"), pgk], [tk("xsb")])
                XS = [tk("xsb")]
                if i + 1 < 16:
                    HTS2[i + 1] = make_hT(i + 1, False, False)
                act(lambda e: e.activation(out=lo24[0:64, 0:T], in_=xsb[0:64, 24, 0:T], func=AF.Tanh), XS, [tk("lo24")])
                act(lambda e: e.copy(out=lo24[64:128, 0:T], in_=xsb[64:128, 24, 0:T]), XS, [tk("lo24")])
                act(lambda e: e.activation(out=sgb[:, 0:T], in_=xsb[:, 25, 0:T], func=AF.Sigmoid), XS, [tk("sgb")])
                nb = (8 * T + 511) // 512
                p, pt = nps(nb)
                for c in range(8):
                    mm(p[:, c * T:(c + 1) * T], WA2[0:64, c * 128:(c + 1) * 128], lo24[0:64, 0:T], True, True, [tk("WA2"), tk("lo24")], pt)
                for c in range(8):
                    act(lambda e: e.activation(out=LW[:, c, 0:T], in_=p[:, c * T:(c + 1) * T], func=AF.Sigmoid, bias=pkT1[:, 26 + c:27 + c]), pt + PK, [tk("LW")])
                p, pt = nps(nb)
                for c in range(8):
                    mm(p[:, c * T:(c + 1) * T], WA2[64:128, c * 128:(c + 1) * 128], lo24[64:128, 0:T], True, True, [tk("WA2"), tk("lo24")], pt)
                for c in range(8):
                    act(lambda e: e.activation(out=ETA[:, c, 0:T], in_=p[:, c * T:(c + 1) * T], func=AF.Sigmoid, bias=pkT1[:, 34 + c:35 + c]), pt + PK, [tk("ETA")])
                r_ = xsb[:, 0:8, 0:T]; kr = xsb[:, 8:16, 0:T]; vr = xsb[:, 16:24, 0:T]
                dve(lambda e: e.tensor_tensor(out=KK[:, :, 0:T], in0=kr, in1=bc(pkT1[:, 42:50], 8, T), op=ALU.mult), XS + PK, [tk("KK")])
                act(lambda e: e.activation(out=TMP[:, :, 0:T], in_=KK[:, :, 0:T], func=AF.Square), [tk("KK")], [tk("TMP")])
                p, pt = nps(nb)
                for c in range(8):
                    mm(p[:, c * T:(c + 1) * T], blk2[:], TMP[:, c, 0:T], True, True, [tk("blk2"), tk("TMP")], pt)
                pv8 = p[:, 0:8 * T].rearrange("p (c t) -> p c t", t=T)
                act(lambda e: e.activation(out=EX[:, :, 0:T], in_=pv8, func=AF.Sqrt), pt, [tk("EX")])
                dve(lambda e: e.tensor_scalar(out=EX[:, :, 0:T], in0=EX[:, :, 0:T], scalar1=1e-12, scalar2=None, op0=ALU.max), [tk("EX")], [tk("EX")])
                dve(lambda e: e.reciprocal(out=EX[:, :, 0:T], in_=EX[:, :, 0:T]), [tk("EX")], [tk("EX")])
                dve(lambda e: e.tensor_tensor(out=KK[:, :, 0:T], in0=KK[:, :, 0:T], in1=EX[:, :, 0:T], op=ALU.mult), [tk("KK"), tk("EX")], [tk("KK")])
                for c in range(8):
                    act(lambda e: e.activation(out=KM[:, c, 0:T], in_=ETA[:, c, 0:T], func=AF.Identity, scale=pkT1[:, 50 + c:51 + c], bias=omka[:, c:c + 1]),
                        [tk("ETA"), tk("omka")] + PK, [tk("KM")])
                dve(lambda e: e.tensor_tensor(out=KM[:, :, 0:T], in0=KM[:, :, 0:T], in1=kr, op=ALU.mult), [tk("KM")] + XS, [tk("KM")])
                pool(lambda e: e.tensor_tensor(out=TMP[:, :, 0:T], in0=r_, in1=KM[:, :, 0:T], op=ALU.mult), XS + [tk("KM")], [tk("TMP")])
                pool(lambda e: e.tensor_tensor(out=TMP[:, :, 0:T], in0=TMP[:, :, 0:T], in1=bc(pkT1[:, 58:66], 8, T), op=ALU.mult), [tk("TMP")] + PK, [tk("TMP")])
                p, pt = nps(nb)
                for c in range(8):
                    mm(p[:, c * T:(c + 1) * T], blk2[:], TMP[:, c, 0:T], True, True, [tk("blk2"), tk("TMP")], pt)
                pv8 = p[:, 0:8 * T].rearrange("p (c t) -> p c t", t=T)
                dve(lambda e: e.tensor_tensor(out=RKV[:, :, 0:T], in0=pv8, in1=vr, op=ALU.mult), pt + XS, [tk("RKV")])
                return T, smp, ht, htk

            def post(i, T, ht, htk):
                nb = (8 * T + 511) // 512
                p, pt = nps(nb)
                for c in range(8):
                    mm(p[:, c * T:(c + 1) * T], blk2[:], OT[:, c, 0:T], True, True, [tk("blk2"), tk("LW")], pt)
                pv8 = p[:, 0:8 * T].rearrange("p (c t) -> p c t", t=T)
                dve(lambda e: e.scalar_tensor_tensor(out=OT[:, :, 0:T], in0=pv8, scalar=-1.0 / 64.0, in1=OT[:, :, 0:T], op0=ALU.mult, op1=ALU.add),
                    pt + [tk("LW")], [tk("LW")])
                act(lambda e: e.activation(out=TMP[:, :, 0:T], in_=OT[:, :, 0:T], func=AF.Square), [tk("LW")], [tk("TMP")])
                p, pt = nps(nb)
                for c in range(8):
                    mm(p[:, c * T:(c + 1) * T], blk2[:], TMP[:, c, 0:T], True, True, [tk("blk2"), tk("TMP")], pt)
                pv8 = p[:, 0:8 * T].rearrange("p (c t) -> p c t", t=T)
                act(lambda e: e.activation(out=EX[:, :, 0:T], in_=pv8, func=AF.Ln, bias=eps64[:, 0:1], scale=1.0 / 64.0), pt + [tk("eps64")], [tk("EX")])
                act(lambda e: e.activation(out=EX[:, :, 0:T], in_=EX[:, :, 0:T], func=AF.Exp, scale=-0.5), [tk("EX")], [tk("EX")])
                dve(lambda e: e.tensor_tensor(out=OT[:, :, 0:T], in0=OT[:, :, 0:T], in1=EX[:, :, 0:T], op=ALU.mult), [tk("LW"), tk("EX")], [tk("LW")])
                for c in range(8):
                    act(lambda e: e.activation(out=OT[:, c, 0:T], in_=OT[:, c, 0:T], func=AF.Identity, scale=pkT1[:, 66 + c:67 + c], bias=pkT1[:, 74 + c:75 + c]),
                        [tk("LW")] + PK, [tk("LW")])
                dve(lambda e: e.tensor_tensor(out=OT[:, :, 0:T], in0=OT[:, :, 0:T], in1=RKV[:, :, 0:T], op=ALU.add), [tk("LW"), tk("RKV")], [tk("LW")])
                p, pt = nps(nb)
                for c in range(8):
                    mm(p[:, c * T:(c + 1) * T], G2b[:, c * 128:(c + 1) * 128], sgb[:, 0:T], True, True, [tk("G2b"), tk("sgb")], pt)
                pv8 = p[:, 0:8 * T].rearrange("p (c t) -> p c t", t=T)
                dve(lambda e: e.tensor_tensor(out=OT[:, :, 0:T], in0=pv8, in1=OT[:, :, 0:T], op=ALU.mult), pt + [tk("LW")], [tk("LW")])
                p, pt = nps(nb)
                for c in range(8):
                    for k in range(8):
                        mm(p[:, c * T:(c + 1) * T], Wr[:, k, 3328 + c * 128:3328 + (c + 1) * 128], ht[:, k, 0:T], k == 0, k == 7, [WRK, htk], pt)
                pv8 = p[:, 0:8 * T].rearrange("p (c t) -> p c t", t=T)
                act(lambda e: e.activation(out=EX[:, :, 0:T], in_=pv8, func=AF.Sigmoid), pt, [tk("EX")])
                dve(lambda e: e.tensor_tensor(out=OT[:, :, 0:T], in0=OT[:, :, 0:T], in1=EX[:, :, 0:T], op=ALU.mult), [tk("LW"), tk("EX")], [tk("LW")])
                yt_ = yat[i % 2]; ytk = tk("yat", i % 2)
                S.dma("sp", yt_[:, :, 0:T], YA[i].rearrange("p (c t) -> p c t", c=8)[:, :, 0:T], [tk("YA", i)], [ytk])
                dve(lambda e: e.tensor_tensor(out=yt_[:, :, 0:T], in0=yt_[:, :, 0:T], in1=OT[:, :, 0:T], op=ALU.add), [ytk, tk("LW")], [ytk])
                S.dma("sp", YA[i].rearrange("p (c t) -> p c t", c=8)[:, :, 0:T], yt_[:, :, 0:T], [ytk], [tk("YA", i)])

            with ExitStack() as c2p:
                cur[0] = c2p
                AR = sb("AR", [128, 8, 256], BF16); BT = sb("BT", [128, 8, 128], BF16); KT = sb("KT", [128, 8, 128], BF16)
                vb = sb("vb", [128, D], BF16); Bt_tok = sb("Bt_tok", [128, D], BF16); Kt_tok = sb("Kt_tok", [128, D], BF16)
                AM = sb("AM", [128, 4, 512], BF16)
                Nn = [sb("Nn%d" % j, [128, 4, 128], BF16) for j in range(2)]
                Nt = [sb("Nt%d" % j, [128, 4, 128], BF16) for j in range(2)]
                Yy = [sb("Yy%d" % j, [128, 4, 128], BF16) for j in range(2)]
                Xx = [sb("Xx%d" % j, [128, 4, 128], BF16) for j in range(2)]
                AFu = sb("AFu", [128, 4, 128], BF16); AoL = [sb("Ao%d" % j, [128, 4, 128], BF16) for j in range(3)]; AotL = [sb("Aot%d" % j, [128, 4, 128], BF16) for j in range(3)]
                Pb = sb("Pb", [128, 4, 128], BF16); Pb2 = sb("Pb2", [128, 4, 128], BF16)
                MK = [sb("MK%d" % j, [128, 128], BF16) for j in range(4)]
                selb = sb("selb", [8, 3, 128])
                Wb = sb("Wb", [128, 256], BF16); Ub = sb("Ub", [128, 256], BF16)
                Wb1 = sb("Wb1", [128, 256], BF16); Ub1 = sb("Ub1", [128, 256], BF16); Pb21 = sb("Pb21", [128, 4, 128], BF16)

                def carve(t3, n):
                    v = t3.bitcast(BF16).rearrange("p c t -> p (c t)")
                    return [v[:, q * 512:(q + 1) * 512].rearrange("p (j t) -> p j t", j=4) for q in range(n)]
                eta4 = ETA[:].bitcast(BF16).rearrange("p c t -> p (c t)").rearrange("p (j t) -> p j t", j=4)
                kk4 = carve(KK[:], 4); km4 = carve(KM[:], 4); lp4 = carve(LP[:], 4); xs4 = carve(xsb[:, 8:16, :], 4)
                SETS = [
                    {"AM": AM[:], "AFu": AFu[:], "Ao": [t_[:] for t_ in AoL], "Aot": [t_[:] for t_ in AotL], "Nn": [t_[:] for t_ in Nn], "Nt": [t_[:] for t_ in Nt],
                     "Xx": [t_[:] for t_ in Xx], "Yy": [t_[:] for t_ in Yy], "Pb": Pb[:], "Pb2": Pb2[:], "Wb": Wb, "Ub": Ub},
                    {"AM": eta4, "AFu": kk4[0], "Ao": kk4[1:4], "Aot": km4[0:3], "Pb": km4[3], "Nn": lp4[0:2], "Nt": lp4[2:4],
                     "Xx": xs4[0:2], "Yy": xs4[2:4], "Pb2": Pb21[:], "Wb": Wb1, "Ub": Ub1},
                ]
                M = sb("M", [128, 8, 64]); Mb = sb("Mb", [128, 8, 64], BF16); Mt = sb("Mt", [128, 8, 64])
                M4 = sb("M4", [128, 512], BF16); MTm = sb("MTm", [128, 128], BF16)
                DBG = False
                dbg = TMP[:].rearrange("p c t -> p (c t)")
                dbn = [0]

                def dump(ap, ncols, R):
                    if not DBG:
                        return
                    dve(lambda e: e.tensor_copy(out=dbg[:, 0:ncols], in_=ap), R + [tk("TMP")], [tk("TMP")])
                    S.dma("sp", O["yp"][dbn[0] * 128:(dbn[0] + 1) * 128, 0:ncols], dbg[:, 0:ncols], [tk("TMP")], [tk("o_yp")])
                    dbn[0] += 1
                pool(lambda e: e.memset(M[:], 0.0), [], [tk("M")])
                pool(lambda e: e.memset(Mb[:], 0.0), [], [tk("Mb")])
                pool(lambda e: e.memset(M4[:], 1.0), [], [tk("M4")])
                for q in range(4):
                    pool(lambda e: e.affine_select(out=M4[:, q * 128:(q + 1) * 128], in_=M4[:, q * 128:(q + 1) * 128], pattern=[[1, 128]], compare_op=ALU.is_ge,
                                                   fill=0.0, base=(-1 if q % 2 == 0 else 0), channel_multiplier=-1), [tk("M4")], [tk("M4")])
                pool(lambda e: e.memset(MTm[:], 1.0), [], [tk("MTm")])
                pool(lambda e: e.affine_select(out=MTm[:], in_=MTm[:], pattern=[[-1, 128]], compare_op=ALU.is_ge, fill=0.0, base=-1, channel_multiplier=1),
                     [tk("MTm")], [tk("MTm")])
                pool(lambda e: e.memset(selb[:], 1.0), [], [tk("selb")])
                for q, bs in enumerate((16, 32, 64)):
                    nb_ = 128 // bs
                    pool(lambda e: e.affine_select(out=selb[0:nb_, q, :], in_=selb[0:nb_, q, :], pattern=[[1, 128]], compare_op=ALU.is_ge, fill=0.0, base=0,
                                                   channel_multiplier=-bs), [tk("selb")], [tk("selb")])
                    pool(lambda e: e.affine_select(out=selb[0:nb_, q, :], in_=selb[0:nb_, q, :], pattern=[[-1, 128]], compare_op=ALU.is_ge, fill=0.0, base=bs - 1,
                                                   channel_multiplier=bs), [tk("selb")], [tk("selb")])
                p, pt = nps()
                for q, bs in enumerate((16, 32, 64)):
                    nb_ = 128 // bs
                    mm(p[:, q * 128:(q + 1) * 128], selb[0:nb_, q, :], selb[0:nb_, q, :], True, True, [tk("selb")], pt)
                copy("dve", MK[0][:], p[:, 0:128], pt, [tk("MK")])
                copy("dve", MK[1][:], p[:, 128:256], pt, [tk("MK")])
                dve(lambda e: e.tensor_tensor(out=MK[2][:], in0=p[:, 256:384], in1=MK[1][:], op=ALU.subtract), pt + [tk("MK")], [tk("MK")])
                dve(lambda e: e.tensor_tensor(out=MK[1][:], in0=MK[1][:], in1=MK[0][:], op=ALU.subtract), [tk("MK")], [tk("MK")])
                dve(lambda e: e.tensor_scalar(out=MK[3][:], in0=p[:, 256:384], scalar1=-1.0, scalar2=1.0, op0=ALU.mult, op1=ALU.add), pt, [tk("MK")])
                for i in range(16):
                    T, smp, ht, htk = front(i)
                    STEP = 9.0
                    if STEP < 2:
                        continue
                    for c in range(8):
                        dve(lambda e: e.tensor_tensor_scan(out=LP[:, c, :], data0=ones128[:], data1=LW[:, c, :], initial=0.0, op0=ALU.mult, op1=ALU.add),
                            [tk("LW"), tk("ones128")], [tk("LP")])
                    pool(lambda e: e.tensor_tensor(out=TMP[:], in0=LP[:], in1=LW[:], op=ALU.subtract), [tk("LP"), tk("LW")], [tk("TMP")])
                    act(lambda e: e.activation(out=EX[:], in_=TMP[:], func=AF.Exp, scale=LWC), [tk("TMP")], [tk("EX")])
                    dve(lambda e: e.scalar_tensor_tensor(out=AR[:, :, 0:128], in0=KK[:], scalar=-1.0, in1=EX[:], op0=ALU.mult, op1=ALU.mult),
                        [tk("KK"), tk("EX")], [tk("AR")])
                    act(lambda e: e.activation(out=EX[:], in_=LP[:], func=AF.Exp, scale=-LWC), [tk("LP")], [tk("EX")])
                    pool(lambda e: e.tensor_tensor(out=TMP[:], in0=KK[:], in1=ETA[:], op=ALU.mult), [tk("KK"), tk("ETA")], [tk("TMP")])
                    dve(lambda e: e.tensor_tensor(out=BT[:], in0=TMP[:], in1=EX[:], op=ALU.mult), [tk("TMP"), tk("EX")], [tk("BT")])
                    dve(lambda e: e.tensor_tensor(out=KT[:], in0=KM[:], in1=EX[:], op=ALU.mult), [tk("KM"), tk("EX")], [tk("KT")])
                    act(lambda e: e.activation(out=EX[:], in_=LP[:], func=AF.Exp, scale=LWC), [tk("LP")], [tk("EX")])
                    dve(lambda e: e.tensor_tensor(out=AR[:, :, 128:256], in0=xsb[:, 0:8, :], in1=EX[:], op=ALU.mult), [tk("xsb"), tk("EX")], [tk("AR")])
                    pool(lambda e: e.tensor_copy(out=eplast[:], in_=EX[:, :, 127:128]), [tk("EX")], [tk("eplast")])
                    if i == 0:
                        for blk in range(1):
                            dump(xsb[:, blk * 8:(blk + 1) * 8, :].rearrange("p c t -> p (c t)"), 1024, [tk("xsb")])
                        dump(KK[:].rearrange("p c t -> p (c t)"), 1024, [tk("KK")])
                        dump(KM[:].rearrange("p c t -> p (c t)"), 1024, [tk("KM")])
                        dump(LP[:].rearrange("p c t -> p (c t)"), 1024, [tk("LP")])
                        dump(AR[:, 0:4, :].rearrange("p c t -> p (c t)"), 1024, [tk("AR")])
                        dump(BT[:].rearrange("p c t -> p (c t)"), 1024, [tk("BT")])
                        dump(KT[:].rearrange("p c t -> p (c t)"), 1024, [tk("KT")])
                    p, pt = nps(2)
                    for c in range(8):
                        tr(p[:, c * 128:(c + 1) * 128], xsb[:, 16 + c, :], ident[:], [tk("xsb")] + CI, [pt[c // 4]])
                    copy("act", vb[:], p[:, 0:1024], pt, [tk("vb")])
                    for (src, srck, dstb, dstk) in ((BT, tk("BT"), Bt_tok, tk("Bt_tok")), (KT, tk("KT"), Kt_tok, tk("Kt_tok"))):
                        p, pt = nps()
                        pb = p.bitcast(BF16)
                        for c in range(8):
                            tr(pb[:, c * 128:(c + 1) * 128], src[:, c, :], identb[:], [srck] + CIB, pt)
                        copy("dve", dstb[:], pb[:, 0:1024], pt, [dstk])
                    if STEP < 3:
                        continue
                    barrier()
                    mkb = lambda l: MK[l][:].unsqueeze(1).to_broadcast([128, 4, 128])
                    idb4 = identb[:].unsqueeze(1).to_broadcast([128, 4, 128])
                    fl = lambda t_: t_.rearrange("p j t -> p (j t)")
                    for gp in range(2):
                        GS = []
                        for s_ in range(2):
                            g4 = 2 * gp + s_
                            hs = [(4 * g4, 2 * g4, 0), (4 * g4 + 2, 2 * g4 + 1, 0), (4 * g4 + 1, 2 * g4, 1), (4 * g4 + 3, 2 * g4 + 1, 1)]
                            GS.append((s_, g4, hs, SETS[s_]))
                        for (s_, g4, hs, B) in GS:
                            pNa, pNat = nps(); pNb, pNbt = nps()
                            for j, (h, c, hf) in enumerate(hs):
                                ps_ = slice(64 * hf, 64 * hf + 64)
                                pa, pat = nps()
                                mm(pa[:, 0:256], BT[ps_, c, :], AR[ps_, c, :], True, True, [tk("BT"), tk("AR")], pat)
                                mm(pa[:, 256:512], KT[ps_, c, :], AR[ps_, c, :], True, True, [tk("KT"), tk("AR")], pat)
                                dve(lambda e: e.tensor_tensor(out=B["AM"][:, j, :], in0=pa[:, 0:512], in1=M4[:], op=ALU.mult), pat + [tk("M4")], [tk("AM", s_, j)])
                                pN, pNt = (pNa, pNat) if hf == 0 else (pNb, pNbt)
                                mm(pN[:, (j % 2) * 128:(j % 2 + 1) * 128], AR[ps_, c, 0:128], BT[ps_, c, :], True, True, [tk("BT"), tk("AR")], pNt)
                            for hf_, (pNx, pNxt) in enumerate(((pNa, pNat), (pNb, pNbt))):
                                dve(lambda e: e.tensor_tensor(out=B["AFu"][:, 2 * hf_:2 * hf_ + 2, :], in0=pNx[:, 0:256].rearrange("p (j t) -> p j t", j=2),
                                                              in1=MTm[:].unsqueeze(1).to_broadcast([128, 2, 128]), op=ALU.mult), pNxt + [tk("MTm")], [tk("AF", s_)])
                        for (s_, g4, hs, B) in GS:
                            AMK = [tk("AM", s_, j) for j in range(4)]
                            AtF = B["AM"][:, :, 0:128]
                            pool(lambda e: e.tensor_tensor(out=B["Nn"][0], in0=B["AFu"], in1=mkb(0), op=ALU.mult), [tk("AF", s_), tk("MK")], [tk("Nn", s_, 0)])
                            dve(lambda e: e.tensor_tensor(out=B["Nt"][0], in0=AtF, in1=mkb(0), op=ALU.mult), AMK + [tk("MK")], [tk("Nt", s_, 0)])
                            pool(lambda e: e.tensor_tensor(out=B["Xx"][0], in0=B["Nn"][0], in1=idb4, op=ALU.add), [tk("Nn", s_, 0)] + CIB, [tk("Xx", s_, 0)])
                            dve(lambda e: e.tensor_tensor(out=B["Yy"][0], in0=B["Nt"][0], in1=idb4, op=ALU.add), [tk("Nt", s_, 0)] + CIB, [tk("Yy", s_, 0)])
                        for (s_, g4, hs, B) in GS:
                            AMK = [tk("AM", s_, j) for j in range(4)]
                            AtF = B["AM"][:, :, 0:128]
                            for l in range(1, 4):
                                pool(lambda e: e.tensor_tensor(out=B["Ao"][l - 1], in0=B["AFu"], in1=mkb(l), op=ALU.mult), [tk("AF", s_), tk("MK")], [tk("Ao", s_, l)])
                                pool(lambda e: e.tensor_tensor(out=B["Aot"][l - 1], in0=AtF, in1=mkb(l), op=ALU.mult), AMK + [tk("MK")], [tk("Aot", s_, l)])
                        for r in range(3):
                            a = r % 2; b = (r + 1) % 2
                            for (s_, g4, hs, B) in GS:
                                pn, pnt = nps()
                                for j in range(4):
                                    mm(pn[:, j * 128:(j + 1) * 128], B["Nt"][a][:, j, :], B["Nn"][a][:, j, :], True, True, [tk("Nt", s_, a), tk("Nn", s_, a)], pnt)
                                pq, pqt = nps()
                                for j in range(4):
                                    mm(pq[:, j * 128:(j + 1) * 128], B["Nn"][a][:, j, :], B["Nt"][a][:, j, :], True, True, [tk("Nt", s_, a), tk("Nn", s_, a)], pqt)
                                copy("act", fl(B["Nn"][b]), pn[:, 0:512], pnt, [tk("Nn", s_, b)])
                                copy("dve", fl(B["Nt"][b]), pq[:, 0:512], pqt, [tk("Nt", s_, b)])
                            for (s_, g4, hs, B) in GS:
                                px, pxt = nps()
                                for j in range(4):
                                    mm(px[:, j * 128:(j + 1) * 128], identb[:], B["Xx"][a][:, j, :], True, False, CIB + [tk("Xx", s_, a)], pxt)
                                    mm(px[:, j * 128:(j + 1) * 128], B["Nt"][b][:, j, :], B["Xx"][a][:, j, :], False, True, [tk("Nt", s_, b), tk("Xx", s_, a)], pxt)
                                py, pyt = nps()
                                for j in range(4):
                                    mm(py[:, j * 128:(j + 1) * 128], identb[:], B["Yy"][a][:, j, :], True, False, CIB + [tk("Yy", s_, a)], pyt)
                                    mm(py[:, j * 128:(j + 1) * 128], B["Nn"][b][:, j, :], B["Yy"][a][:, j, :], False, True, [tk("Nn", s_, b), tk("Yy", s_, a)], pyt)
                                copy("act", fl(B["Xx"][b]), px[:, 0:512], pxt, [tk("Xx", s_, b)])
                                copy("dve", fl(B["Yy"][b]), py[:, 0:512], pyt, [tk("Yy", s_, b)])
                        cu = 1
                        for l in range(1, 4):
                            if l < 3:
                                for (s_, g4, hs, B) in GS:
                                    pp, ppt = nps()
                                    for j in range(4):
                                        mm(pp[:, j * 128:(j + 1) * 128], B["Aot"][l - 1][:, j, :], B["Xx"][cu][:, j, :], True, True, [tk("Aot", s_, l), tk("Xx", s_, cu)], ppt)
                                    copy("act", fl(B["Pb"]), pp[:, 0:512], ppt, [tk("Pb", s_)])
                            for (s_, g4, hs, B) in GS:
                                pp2, pp2t = nps()
                                for j in range(4):
                                    mm(pp2[:, j * 128:(j + 1) * 128], B["Ao"][l - 1][:, j, :], B["Yy"][cu][:, j, :], True, True, [tk("Ao", s_, l), tk("Yy", s_, cu)], pp2t)
                                copy("dve", fl(B["Pb2"]), pp2[:, 0:512], pp2t, [tk("Pb2", s_)])
                            if l < 3:
                                for (s_, g4, hs, B) in GS:
                                    px, pxt = nps()
                                    for j in range(4):
                                        mm(px[:, j * 128:(j + 1) * 128], identb[:], B["Xx"][cu][:, j, :], True, False, CIB + [tk("Xx", s_, cu)], pxt)
                                        mm(px[:, j * 128:(j + 1) * 128], B["Yy"][cu][:, j, :], B["Pb"][:, j, :], False, True, [tk("Yy", s_, cu), tk("Pb", s_)], pxt)
                                    copy("act", fl(B["Xx"][1 - cu]), px[:, 0:512], pxt, [tk("Xx", s_, 1 - cu)])
                            for (s_, g4, hs, B) in GS:
                                py, pyt = nps()
                                for j in range(4):
                                    mm(py[:, j * 128:(j + 1) * 128], identb[:], B["Yy"][cu][:, j, :], True, False, CIB + [tk("Yy", s_, cu)], pyt)
                                    mm(py[:, j * 128:(j + 1) * 128], B["Xx"][cu][:, j, :], B["Pb2"][:, j, :], False, True, [tk("Xx", s_, cu), tk("Pb2", s_)], pyt)
                                copy("dve", fl(B["Yy"][1 - cu]), py[:, 0:512], pyt, [tk("Yy", s_, 1 - cu)])
                            cu = 1 - cu
                        for (s_, g4, hs, B) in GS:
                            pwa, pwat = nps(); pwb, pwbt = nps()
                            for j, (h, c, hf) in enumerate(hs):
                                ps_ = slice(64 * hf, 64 * hf + 64)
                                pw, pwt = (pwa, pwat) if hf == 0 else (pwb, pwbt)
                                mm(pw[:, (j % 2) * 64:(j % 2 + 1) * 64], AR[ps_, c, 0:128], Mb[ps_, c, :], True, False, [tk("AR"), tk("Mb")], pwt)
                                mm(pw[:, (j % 2) * 64:(j % 2 + 1) * 64], B["AM"][:, j, 256:384], vb[:, h * 64:(h + 1) * 64], False, True, [tk("AM", s_, j), tk("vb")], pwt)
                            copy("act", B["Wb"][:, 0:128], pwa[:, 0:128], pwat, [tk("Wb", s_)])
                            copy("act", B["Wb"][:, 128:256], pwb[:, 0:128], pwbt, [tk("Wb", s_)])
                        for (s_, g4, hs, B) in GS:
                            pu, put = nps()
                            for j in range(4):
                                mm(pu[:, j * 64:(j + 1) * 64], B["Yy"][cu][:, j, :], B["Wb"][:, j * 64:(j + 1) * 64], True, True, [tk("Yy", s_, cu), tk("Wb", s_)], put)
                            copy("dve", B["Ub"][:, :], pu[:, 0:256], put, [tk("Ub", s_)])
                        for (s_, g4, hs, B) in GS:
                            pOa, pOat = nps(); pOb, pObt = nps()
                            for j, (h, c, hf) in enumerate(hs):
                                ps_ = slice(64 * hf, 64 * hf + 64)
                                pO, pOt = (pOa, pOat) if hf == 0 else (pOb, pObt)
                                o_ = pO[ps_, (j % 2) * 128:(j % 2 + 1) * 128]
                                mm(o_, Mb[ps_, c, :], AR[ps_, c, 128:256], True, False, [tk("AR"), tk("Mb")], pOt)
                                mm(o_, B["Ub"][:, j * 64:(j + 1) * 64], B["AM"][:, j, 128:256], False, False, [tk("Ub", s_), tk("AM", s_, j)], pOt)
                                mm(o_, vb[:, h * 64:(h + 1) * 64], B["AM"][:, j, 384:512], False, True, [tk("vb"), tk("AM", s_, j)], pOt)
                                PM, PMt = (P7, P7t) if hf == 0 else (P6, P6t)
                                m_ = PM[ps_, c * 64:(c + 1) * 64]
                                mm(m_, Bt_tok[:, h * 64:(h + 1) * 64], B["Ub"][:, j * 64:(j + 1) * 64], True, False, [tk("Bt_tok"), tk("Ub", s_)], PMt)
                                mm(m_, Kt_tok[:, h * 64:(h + 1) * 64], vb[:, h * 64:(h + 1) * 64], False, True, [tk("Kt_tok"), tk("vb")], PMt)
                            copy("act", OT[0:64, 2 * g4:2 * g4 + 2, :], pOa[0:64, 0:256].rearrange("p (c t) -> p c t", t=128), pOat, [tk("LW")])
                            copy("act", OT[64:128, 2 * g4:2 * g4 + 2, :], pOb[64:128, 0:256].rearrange("p (c t) -> p c t", t=128), pObt, [tk("LW")])
                    barrier()
                    dve(lambda e: e.tensor_tensor(out=Mt[0:64], in0=M[0:64], in1=P7[0:64, 0:512].rearrange("p (c v) -> p c v", v=64), op=ALU.add), [tk("M")] + P7t, [tk("Mt")])
                    dve(lambda e: e.tensor_tensor(out=Mt[64:128], in0=M[64:128], in1=P6[64:128, 0:512].rearrange("p (c v) -> p c v", v=64), op=ALU.add), [tk("M")] + P6t, [tk("Mt")])
                    dve(lambda e: e.tensor_tensor(out=M[:], in0=Mt[:], in1=eplast[:].to_broadcast([128, 8, 64]), op=ALU.mult), [tk("Mt"), tk("eplast")], [tk("M")])
                    copy("act", Mb[:], M[:], [tk("M")], [tk("Mb")])
                    if i == 0:
                        dump(OT[:].rearrange("p c t -> p (c t)"), 1024, [tk("LW")])
                        dump(M[:].rearrange("p c v -> p (c v)"), 512, [tk("M")])
                    if STEP < 5:
                        continue
                    post(i, T, ht, htk)
                p, pt = nps(2)
                for c in range(8):
                    tr(p[0:64, c * 128:(c + 1) * 128], M[:, c, :], ident[:], [tk("M")] + CI, [pt[c // 4]])
                so = stg[0]; sok = tk("stg", 0)
                copy("act", so[0:64, 0:D], p[0:64, 0:1024], pt, [sok])
                S.dma("sp", O["S_p"].rearrange("(h v) k -> v h k", v=64), so[0:64, 0:D].rearrange("v (h k) -> v h k", k=64), [sok], [tk("o_S_p")])
                barrier()
            with ExitStack() as c2s:
                cur[0] = c2s
                if True:
                    T, smp, ht, htk = front(16)
                    act(lambda e: e.activation(out=EX[:, :, 0:T], in_=LW[:, :, 0:T], func=AF.Exp, scale=LWC), [tk("LW")], [tk("EX")])
                    dve(lambda e: e.scalar_tensor_tensor(out=TMP[:, :, 0:T], in0=KK[:, :, 0:T], scalar=-1.0, in1=ETA[:, :, 0:T], op0=ALU.mult, op1=ALU.mult),
                        [tk("KK"), tk("ETA")], [tk("TMP")])
                    arrs = [(xsb, 0, tk("xsb")), (EX, 0, tk("EX")), (KM, 0, tk("KM")), (xsb, 16, tk("xsb")), (KK, 0, tk("KK")), (TMP, 0, tk("TMP"))]
                    for a, (src, c0, srck) in enumerate(arrs):
                        p, pt = nps(2)
                        for c in range(8):
                            tr(p[0:NST, c * 128:(c + 1) * 128], src[:, c0 + c, 0:NST], ident[:], [srck] + CI, [pt[c // 4]])
                        sg_ = stg[a % 3]; sgk = tk("stg", a % 3)
                        copy(evac_eng(), sg_[0:NST, 0:D], p[0:NST, 0:1024], pt, [sgk])
                        S.dma("sp", SCR6[a], sg_[0:NST, 0:D], [sgk], [tk("SCR6")])
                    Sst = sb("Sst", [128, 4096]); Stm = sb("Stm", [128, 4096]); PR = sb("PR", [128, 6, TS, 64]); osm = sb("osm", [128, TS, 64])
                    sa = sb("sa", [128, 64])
                    Sv = Sst[:].rearrange("p (v k) -> p v k", k=64); Tv = Stm[:].rearrange("p (v k) -> p v k", k=64)
                    scr6v = SCR6.rearrange("a (s t) (h k) -> s h a t k", t=TS, k=64)
                    scrov = SCRO.rearrange("(s t) (h v) -> s h t v", t=TS, v=64)
                    for g2 in range(2):
                        S.dma("sp", Sst[:], I["stS"][g2 * 8192:(g2 + 1) * 8192, :].rearrange("(p v) k -> p (v k)", v=64), [], [tk("Sst")])
                        for sl_ in range(8):
                            for a in range(6):
                                S.dma("sp", PR[sl_ * 16:(sl_ + 1) * 16, a], scr6v[g2 * 8 + sl_][:, a], [tk("SCR6")], [tk("PR")])
                        vec_k = lambda a, t: PR[:, a, t, :].unsqueeze(1).to_broadcast([128, 64, 64])
                        RS_ = [tk("Sst"), tk("PR")]
                        for t in range(TS):
                            dve(lambda e: e.tensor_tensor(out=Tv, in0=Sv, in1=vec_k(4, t), op=ALU.mult), RS_, [tk("Stm")])
                            dve(lambda e: e.tensor_reduce(out=sa[:], in_=Tv, axis=AX.X, op=ALU.add), [tk("Stm")], [tk("sa")])
                            dve(lambda e: e.tensor_tensor(out=Sv, in0=Sv, in1=vec_k(1, t), op=ALU.mult), RS_, [tk("Sst")])
                            dve(lambda e: e.tensor_tensor(out=Tv, in0=sa[:].unsqueeze(2).to_broadcast([128, 64, 64]), in1=vec_k(5, t), op=ALU.mult),
                                [tk("sa"), tk("PR")], [tk("Stm")])
                            dve(lambda e: e.tensor_tensor(out=Sv, in0=Sv, in1=Tv, op=ALU.add), [tk("Sst"), tk("Stm")], [tk("Sst")])
                            dve(lambda e: e.tensor_tensor(out=Tv, in0=PR[:, 3, t, :].unsqueeze(2).to_broadcast([128, 64, 64]), in1=vec_k(2, t), op=ALU.mult),
                                [tk("PR")], [tk("Stm")])
                            dve(lambda e: e.tensor_tensor(out=Sv, in0=Sv, in1=Tv, op=ALU.add), [tk("Sst"), tk("Stm")], [tk("Sst")])
                            dve(lambda e: e.tensor_tensor(out=Tv, in0=Sv, in1=vec_k(0, t), op=ALU.mult), RS_, [tk("Stm")])
                            dve(lambda e: e.tensor_reduce(out=osm[:, t, :], in_=Tv, axis=AX.X, op=ALU.add), [tk("Stm")], [tk("osm")])
                        S.dma("sp", O["S_s"][g2 * 8192:(g2 + 1) * 8192, :].rearrange("(p v) k -> p (v k)", v=64), Sst[:], [tk("Sst")], [tk("o_S_s")])
                        for sl_ in range(8):
                            S.dma("sp", scrov[g2 * 8 + sl_], osm[sl_ * 16:(sl_ + 1) * 16], [tk("osm")], [tk("SCRO")])
                    ot = stg[0]; otk = tk("stg", 0)
                    S.dma("sp", ot[0:NST, 0:D], SCRO, [tk("SCRO")], [otk])
                    p, pt = nps()
                    for c in range(8):
                        tr(p[:, c * 64:(c + 1) * 64], ot[0:NST, c * 128:(c + 1) * 128], ident[0:NST, 0:NST], [otk] + CI, pt)
                    copy("act", OT[:, :, 0:NST], p[:, 0:512].rearrange("p (c t) -> p c t", t=NST), pt, [tk("LW")])
                    post(16, T, ht, htk)
                    barrier()
        cur[0] = ctx
        with ExitStack() as c3:
            cur[0] = c3
            Wo = sb("Wo", [128, 8, D], BF16); Wu = sb("Wu", [128, 8, 4096], BF16); Wd = sb("Wd", [128, 32, D], BF16)
            load_weights(Wo, tk("Wo"), [(0, 1024, 0)], I["w_out"])
            load_weights(Wu, tk("Wu"), [(0, 1540, 0), (1540, 3080, 1540), (3080, 4096, 3080)], I["w_up"])
            load_weights(Wd, tk("Wd"), [(0, 1024, 0)], I["w_down"])
            onesm = sb("onesm", [128, 128])
            pool(lambda e: e.memset(onesm[:], 1.0 / 1024.0), [], [tk("onesm")])
            BA = sb("BA", [128, 8, 128]); BB = sb("BB", [128, 8, 128]); BC = sb("BC", [128, 8, 128])
            h2T = sb("h2T", [128, 8, 128], BF16); aT = sb("aT", [128, 32, 128], BF16)
            rs_ = sb("rs_", [128, 128])
            PK = [tk("pkT1")]

            def bc3(ap, T):
                return ap.unsqueeze(2).to_broadcast([128, 8, T])

            def modbc(c0, T, smp):
                if not smp:
                    return modT[:, c0:c0 + 8, 0:1].to_broadcast([128, 8, T])
                return modT[:, c0:c0 + 8, 1:17].unsqueeze(3).to_broadcast([128, 8, NS, TS])

            def v4(ap, T, smp):
                return ap[:, :, 0:T] if not smp else ap[:, :, 0:T].rearrange("p c (s t) -> p c s t", t=TS)

            def ln_fm(Z, ZK, SQ, SQK, gcol, T):
                p, pt = nps()
                for c in range(8):
                    mm(p[:, 0:T], onesm[:], Z[:, c, 0:T], c == 0, c == 7, [tk("onesm"), ZK], pt)
                dve(lambda e: e.tensor_tensor(out=Z[:, :, 0:T], in0=Z[:, :, 0:T], in1=p[:, 0:T].unsqueeze(1).to_broadcast([128, 8, T]), op=ALU.subtract),
                    [ZK] + pt, [ZK])
                act(lambda e: e.activation(out=SQ[:, :, 0:T], in_=Z[:, :, 0:T], func=AF.Square), [ZK], [SQK])
                p, pt = nps()
                for c in range(8):
                    mm(p[:, 0:T], onesm[:], SQ[:, c, 0:T], c == 0, c == 7, [tk("onesm"), SQK], pt)
                dve(lambda e: e.tensor_scalar(out=rs_[:, 0:T], in0=p[:, 0:T], scalar1=LN_EPS, scalar2=None, op0=ALU.add), pt, [tk("rs_")])
                act(lambda e: e.activation(out=rs_[:, 0:T], in_=rs_[:, 0:T], func=AF.Sqrt), [tk("rs_")], [tk("rs_")])
                dve(lambda e: e.reciprocal(out=rs_[:, 0:T], in_=rs_[:, 0:T]), [tk("rs_")], [tk("rs_")])
                dve(lambda e: e.tensor_tensor(out=Z[:, :, 0:T], in0=Z[:, :, 0:T], in1=rs_[:, 0:T].unsqueeze(1).to_broadcast([128, 8, T]), op=ALU.mult),
                    [ZK, tk("rs_")], [ZK])
                for c in range(8):
                    act(lambda e: e.activation(out=Z[:, c, 0:T], in_=Z[:, c, 0:T], func=AF.Identity, scale=pkT1[:, gcol + c:gcol + c + 1],
                                               bias=pkT1[:, gcol + 8 + c:gcol + 9 + c]), [ZK] + PK, [ZK])

            AK, BK, CK = tk("BA"), tk("BB"), tk("BC")
            for i in range(17):
                T = 128 if i < 16 else NST
                smp = i == 16
                nb = (8 * T + 511) // 512
                def loads(i_):
                    T_ = 128 if i_ < 16 else NST
                    x_ = stg[(2 * i_) % 3]; xk_ = tk("stg", (2 * i_) % 3)
                    if i_ < 16:
                        S.dma("sp", x_[:, 0:D], I["xp"][i_ * 128:(i_ + 1) * 128, :], [], [xk_])
                    else:
                        S.dma("sp", x_[0:NST, 0:D], I["xs"], [], [xk_])
                    ut_ = yat[i_ % 2]; utk_ = tk("yat", i_ % 2)
                    S.dma("sp", ut_[:, :, 0:T_], YA[i_].rearrange("p (c t) -> p c t", c=8)[:, :, 0:T_], [tk("YA", i_)], [utk_])
                if i == 0:
                    loads(0)
                x = stg[(2 * i) % 3]; xk = tk("stg", (2 * i) % 3)
                p, pt = nps(nb)
                for c in range(8):
                    tr(p[:, c * T:(c + 1) * T], x[0:T, c * 128:(c + 1) * 128], ident[0:T, 0:T], [xk] + CI, [pt[(c * T) // 512]])
                copy("act", BA[:, :, 0:T], p[:, 0:8 * T].rearrange("p (c t) -> p c t", t=T), pt, [AK])
                ut = yat[i % 2]; utk = tk("yat", i % 2)
                p, pt = nps(nb)
                for c in range(8):
                    for k in range(8):
                        mm(p[:, c * T:(c + 1) * T], Wo[:, k, c * 128:(c + 1) * 128], ut[:, k, 0:T], k == 0, k == 7, [tk("Wo"), utk], [pt[(c * T) // 512]])
                pv8 = p[:, 0:8 * T].rearrange("p (c t) -> p c t", t=T)
                pv8 = pv8 if not smp else pv8.rearrange("p c (s t) -> p c s t", t=TS)
                dve(lambda e: e.tensor_tensor(out=v4(BC, T, smp), in0=pv8, in1=modbc(16, T, smp), op=ALU.mult), pt + MT, [CK])
                dve(lambda e: e.scalar_tensor_tensor(out=BB[:, :, 0:T], in0=BA[:, :, 0:T], scalar=ALPHA, in1=BC[:, :, 0:T], op0=ALU.mult, op1=ALU.add),
                    [AK, CK], [BK])
                if i + 1 < 17:
                    loads(i + 1)
                ln_fm(BB, BK, BA, AK, 82, T)
                pool(lambda e: e.tensor_tensor(out=v4(BC, T, smp), in0=v4(BB, T, smp), in1=modbc(32, T, smp), op=ALU.mult), [BK] + MT, [CK])
                dve(lambda e: e.tensor_tensor(out=v4(h2T, T, smp), in0=v4(BC, T, smp), in1=modbc(24, T, smp), op=ALU.add), [CK] + MT, [tk("h2T")])
                for fg in range(8):
                    p, pt = nps()
                    for f4 in range(4):
                        f = fg * 4 + f4
                        for k in range(8):
                            mm(p[:, f4 * T:(f4 + 1) * T], Wu[:, k, f * 128:(f + 1) * 128], h2T[:, k, 0:T], k == 0, k == 7, [tk("Wu"), tk("h2T")], pt)
                    act(lambda e: e.activation(out=BC[:, 4 * (fg % 2):4 * (fg % 2) + 4, 0:T], in_=p[:, 0:4 * T].rearrange("p (c t) -> p c t", t=T), func=AF.Relu),
                        pt + ([CK] if fg < 2 else []), [tk("BCr", fg % 2)])
                    dve(lambda e: e.scalar_tensor_tensor(out=aT[:, fg * 4:(fg + 1) * 4, 0:T], in0=p[:, 0:4 * T].rearrange("p (c t) -> p c t", t=T), scalar=0.0,
                                                         in1=BC[:, 4 * (fg % 2):4 * (fg % 2) + 4, 0:T], op0=ALU.max, op1=ALU.mult), pt + [tk("BCr", fg % 2)], [tk("aT")])
                p, pt = nps(nb)
                for c in range(8):
                    for f in range(32):
                        mm(p[:, c * T:(c + 1) * T], Wd[:, f, c * 128:(c + 1) * 128], aT[:, f, 0:T], f == 0, f == 31, [tk("Wd"), tk("aT")], [pt[(c * T) // 512]])
                pv8 = p[:, 0:8 * T].rearrange("p (c t) -> p c t", t=T)
                pv8 = pv8 if not smp else pv8.rearrange("p c (s t) -> p c s t", t=TS)
                dve(lambda e: e.tensor_tensor(out=v4(BC, T, smp), in0=pv8, in1=modbc(40, T, smp), op=ALU.mult), pt + MT, [CK, tk("BCr", 0), tk("BCr", 1)])
                dve(lambda e: e.scalar_tensor_tensor(out=BA[:, :, 0:T], in0=BB[:, :, 0:T], scalar=ALPHA, in1=BC[:, :, 0:T], op0=ALU.mult, op1=ALU.add),
                    [BK, CK], [AK])
                ln_fm(BA, AK, BC, CK, 98, T)
                p, pt = nps(2)
                for c in range(8):
                    tr(p[0:T, c * 128:(c + 1) * 128], BA[:, c, 0:T], ident[:], [AK] + CI, [pt[c // 4]])
                ot = stg[(2 * i + 1) % 3]; otk = tk("stg", (2 * i + 1) % 3)
                copy("act", ot[0:T, 0:D], p[0:T, 0:1024], pt, [otk])
                if not smp:
                    S.dma("pool", O["yp"][i * 128:(i + 1) * 128, :], ot[0:T, 0:D], [otk], [tk("o_yp")])
                else:
                    S.dma("sp", O["ys"], ot[0:T, 0:D], [otk], [tk("o_ys")])
            barrier()
        cur[0] = ctx
        S.finish()
        print("n_ins", S.n_ins, "n_wait", S.n_wait, S.cnt, {q: sum(st["val"]) // 16 for q, st in S.dq.items()})
    return nc


_NC = [None]


def _prep_inputs(inp):
    f = lambda a: np.ascontiguousarray(a, dtype=np.float32)
    w_in = f(inp["w_in"][0])
    pk0 = np.concatenate([inp["b_cond"][0].reshape(48, 128), inp["conv_w"][0].reshape(64, 128), inp["conv_b"][0].reshape(16, 128)], 0)
    pk1 = np.concatenate([inp["rwkv_mu"][0].reshape(26, 128), inp["rwkv_w0"][0].reshape(8, 128), inp["rwkv_a0"][0].reshape(8, 128),
                          inp["rwkv_k_k"][0].reshape(8, 128), inp["rwkv_k_a"][0].reshape(8, 128), inp["rwkv_r_k"][0].reshape(8, 128),
                          inp["rwkv_lnx_w"][0].reshape(8, 128), inp["rwkv_lnx_b"][0].reshape(8, 128), inp["ln1_g"][0].reshape(8, 128),
                          inp["ln1_b"][0].reshape(8, 128), inp["ln2_g"][0].reshape(8, 128), inp["ln2_b"][0].reshape(8, 128)], 0)
    ifb = np.stack([inp["mlstm_i_bias"][0], inp["mlstm_f_bias"][0]], 1)
    vecs = np.stack([inp["mlstm_norm_w"][0], inp["rwkv_lnx_w"][0], inp["rwkv_lnx_b"][0], inp["ln1_g"][0], inp["ln1_b"][0],
                     inp["ln2_g"][0], inp["ln2_b"][0]], 0)
    shared = {
        "w_cond": f(inp["w_cond"][0]), "w_in": w_in, "w_out": f(inp["w_out"][0]), "w_up": f(inp["w_up"][0]),
        "w_down": f(inp["w_down"][0]), "pk0": f(pk0), "pk1": f(pk1), "ifb": f(ifb), "vecs": f(vecs),
        "w2": f(inp["rwkv_w2"][0]), "a2": f(inp["rwkv_a2"][0]), "g2": f(inp["rwkv_g2"][0]),
    }
    maps = []
    for c in range(NCORES):
        sl = slice(c * NS, (c + 1) * NS)
        m = dict(shared)
        m["xp"] = f(inp["x_prompt"][c])
        m["xs"] = f(inp["x_sample"][sl].reshape(NST, D))
        m["cc"] = f(np.concatenate([inp["c_prompt"][c:c + 1], inp["c_sample"][sl]], 0))
        m["stC"] = f(inp["state_mlstm_C"][0, sl].reshape(NS * 4 * 256, 256))
        m["stn"] = f(inp["state_mlstm_n"][0, sl].reshape(NS * 4, 256))
        m["stm"] = f(inp["state_mlstm_m"][0, sl])
        m["stconv"] = f(inp["state_mlstm_conv"][0, sl].reshape(NS * 3, 2048))
        m["stS"] = f(inp["state_rwkv_S"][0, sl].reshape(NS * 16 * 64, 64))
        m["stshift"] = f(inp["state_rwkv_shift"][0, sl])
        maps.append(m)
    return maps


def kernel(**inp):
    inp = {k: np.asarray(v) for k, v in inp.items()}
    maps = _prep_inputs(inp)
    if _NC[0] is None:
        _NC[0] = build()
    res = run_bass_kernel_spmd(_NC[0], maps, core_ids=list(range(NCORES)))
    R = res.results
    g = lambda n: [np.asarray(r[n], dtype=np.float32) for r in R]
    yp = np.stack(g("yp"), 0)
    ys = np.concatenate(g("ys"), 0).reshape(128, TS, D)
    C_p = np.stack(g("C_p"), 0).reshape(1, 8, 4, 256, 256)
    n_p = np.stack(g("n_p"), 0).reshape(1, 8, 4, 256)
    m_p = np.stack(g("m_p"), 0).reshape(1, 8, 4)
    conv_p = np.stack(g("conv_p"), 0).reshape(1, 8, 3, 2048)
    S_p = np.stack(g("S_p"), 0).reshape(1, 8, 16, 64, 64)
    shift_p = np.stack(g("shift_p"), 0).reshape(1, 8, D)
    C_s = np.concatenate(g("C_s"), 0).reshape(1, 128, 4, 256, 256)
    n_s = np.concatenate(g("n_s"), 0).reshape(1, 128, 4, 256)
    m_s = np.concatenate(g("m_s"), 0).reshape(1, 128, 4)
    conv_s = np.concatenate(g("conv_s"), 0).reshape(1, 128, 3, 2048)
    S_s = np.concatenate(g("S_s"), 0).reshape(1, 128, 16, 64, 64)
    shift_s = np.concatenate(g("shift_s"), 0).reshape(1, 128, D)
    return (yp, ys, C_p, n_p, m_p, conv_p, S_p, shift_p, C_s, n_s, m_s, conv_s, S_s, shift_s)
```

```python
import numpy as np
from contextlib import ExitStack
import concourse.bass as bass
import concourse.mybir as mybir
from concourse.bass_utils import run_bass_kernel_spmd

F32 = mybir.dt.float32
BF16 = mybir.dt.bfloat16
AF = mybir.ActivationFunctionType
ALU = mybir.AluOpType
AX = mybir.AxisListType

NCORES = 8
D = 1024
TP = 2048
NS = 16
TS = 4
NST = NS * TS
NTOK = TP + NST
OFF_RWKV = 3080
N_RWKV = 3328
OFF_GATE = 6408
N_IN = 8456
ALPHA = 2.0 ** 0.25
LN_EPS = 1e-5


class Tk:
    __slots__ = ("w", "rs", "name")

    def __init__(self, name=""):
        self.w = None
        self.rs = []
        self.name = name


class Sched:
    ENG = ("pe", "dve", "act", "pool", "sp")
    NOSELF = ("pe",)

    def __init__(self, nc, ctx, n_dma_sems=12):
        self.nc = nc
        self.E = {"pe": nc.tensor, "dve": nc.vector, "act": nc.scalar, "pool": nc.gpsimd, "sp": nc.sync}
        self.semh = {}
        for e in self.ENG:
            self.semh[e] = ctx.enter_context(nc.semaphore("s_" + e))
        self.cnt = {e: 0 for e in self.ENG}
        self.seen = {e: {} for e in self.ENG}
        self.dq = {}
        for q in ("sp", "pool", "act"):
            sems = []
            for i in range(n_dma_sems):
                key = ("d", q, i)
                self.semh[key] = ctx.enter_context(nc.semaphore("d_%s_%d" % (q, i)))
                sems.append(key)
            self.dq[q] = {"keys": sems, "val": [0] * n_dma_sems, "nxt": 0}
        self.n_ins = 0
        self.n_wait = 0

    def _wait(self, en, ev, selfwait=True):
        if ev is None:
            return
        key, val = ev
        if key == en and (not selfwait or en in self.NOSELF):
            return
        if self.seen[en].get(key, 0) >= val:
            return
        self.E[en].wait_ge(self.semh[key], val)
        self.seen[en][key] = val
        self.n_wait += 1

    def _deps(self, en, reads, writes):
        for t in reads:
            self._wait(en, t.w, True)
        for t in writes:
            self._wait(en, t.w, False)
            for r in t.rs:
                self._wait(en, r, False)

    def _commit(self, ev, reads, writes):
        for t in reads:
            t.rs.append(ev)
            if len(t.rs) > 48:
                d = {}
                for k, v in t.rs:
                    if d.get(k, 0) < v:
                        d[k] = v
                t.rs = list(d.items())
        for t in writes:
            t.w = ev
            t.rs = []

    def op(self, en, fn, reads=(), writes=()):
        self._deps(en, reads, writes)
        ins = fn(self.E[en])
        self.cnt[en] += 1
        ins.then_inc(self.semh[en], 1)
        ev = (en, self.cnt[en])
        self._commit(ev, reads, writes)
        self.n_ins += 1
        return ev

    def dma(self, q, out, in_, reads=(), writes=(), **kw):
        self._deps(q, reads, writes)
        st = self.dq[q]
        i = st["nxt"]
        st["nxt"] = (i + 1) % len(st["keys"])
        key = st["keys"][i]
        if st["val"][i] > 0:
            self._wait(q, (key, st["val"][i]))
        ins = self.E[q].dma_start(out=out, in_=in_, **kw)
        st["val"][i] += 16
        ins.then_inc(self.semh[key], 16)
        ev = (key, st["val"][i])
        self._commit(ev, reads, writes)
        self.n_ins += 1
        return ev

    def finish(self):
        for q, st in self.dq.items():
            for i, key in enumerate(st["keys"]):
                if st["val"][i] > 0:
                    self._wait("sp", (key, st["val"][i]))
        for e in self.ENG:
            if e != "sp" and self.cnt[e] > 0:
                self._wait("sp", (e, self.cnt[e]))


IN_SPECS = [
    ("xp", [TP, D]), ("xs", [NST, D]), ("cc", [17, D]),
    ("stC", [NS * 4 * 256, 256]), ("stn", [NS * 4, 256]), ("stm", [NS, 4]), ("stconv", [NS * 3, 2048]),
    ("stS", [NS * 16 * 64, 64]), ("stshift", [NS, D]),
    ("w_cond", [D, 6144]), ("w_in", [D, N_IN]), ("w_out", [D, D]), ("w_up", [D, 4096]), ("w_down", [4096, D]),
    ("pk0", [128, 128]), ("pk1", [114, 128]),
    ("ifb", [4, 2]),
    ("vecs", [7, D]),
    ("w2", [64, D]), ("a2", [64, D]), ("g2", [128, D]),
]
OUT_SPECS = [
    ("yp", [TP, D]), ("ys", [NST, D]), ("C_p", [4 * 256, 256]), ("n_p", [4, 256]), ("m_p", [1, 4]),
    ("conv_p", [3, 2048]), ("S_p", [16 * 64, 64]), ("shift_p", [1, D]),
    ("C_s", [NS * 4 * 256, 256]), ("n_s", [NS * 4, 256]), ("m_s", [NS, 4]), ("conv_s", [NS * 3, 2048]),
    ("S_s", [NS * 16 * 64, 64]), ("shift_s", [NS, D]),
]

STAGE = 1


def build():
    nc = bass.Bass("TRN2", target_bir_lowering=False)
    I = {n: nc.dram_tensor(n, s, F32, kind="ExternalInput").ap() for n, s in IN_SPECS}
    O = {n: nc.dram_tensor(n, s, F32, kind="ExternalOutput").ap() for n, s in OUT_SPECS}
    with ExitStack() as ctx:
        S = Sched(nc, ctx)
        toks = {}

        def tk(*key):
            if key not in toks:
                toks[key] = Tk(str(key))
            return toks[key]

        def sb(name, shape, dt=F32):
            return ctx.enter_context(nc.sbuf_tensor("sb_" + name, shape, dt))

        PSALL = ctx.enter_context(nc.psum_tensor("psall", [128, 4096], F32))
        ps_i = [0]

        def nps(n=1):
            i = ps_i[0]
            if n > 1 and i % n:
                i += n - i % n
            if i + n > 8:
                i = 0
            ps_i[0] = (i + n) % 8
            if n == 1:
                return PSALL[:, i * 512:(i + 1) * 512], tk("ps", i)
            return PSALL[:, i * 512:(i + n) * 512], [tk("ps", i + j) for j in range(n)]

        ev_i = [0]

        def evac_eng():
            ev_i[0] ^= 1
            return "dve" if ev_i[0] else "act"

        def copy(en, out, in_, R, W):
            if en == "act":
                S.op("act", lambda e: e.copy(out=out, in_=in_), R, W)
            else:
                S.op(en, lambda e: e.tensor_copy(out=out, in_=in_), R, W)

        def mm(out, lhsT, rhs, start, stop, R, W):
            S.op("pe", lambda e: e.matmul(out, lhsT=lhsT, rhs=rhs, start=start, stop=stop), R, W)

        def tr(out, in_, idn, R, W):
            S.op("pe", lambda e: e.transpose(out=out, in_=in_, identity=idn), R, W)

        def nps(n=1):
            i = ps_i[0]
            if n == 1:
                ps_i[0] = (i + 1) % 6
            elif n == 2:
                i = (i + 1) // 2 * 2
                if i > 4:
                    i = 0
                ps_i[0] = (i + 2) % 6
            else:
                i = 0
                ps_i[0] = 4
            return PSALL[:, i * 512:(i + n) * 512], [tk("ps", i + j) for j in range(n)]

        P7 = PSALL[:, 7 * 512:8 * 512]
        P7t = [tk("ps", 7)]
        P6 = PSALL[:, 6 * 512:7 * 512]
        P6t = [tk("ps", 6)]

        cur = [ctx]

        def sb(name, shape, dt=F32):
            return cur[0].enter_context(nc.sbuf_tensor("sb_" + name, shape, dt))

        def barrier():
            evs = [(e, S.cnt[e]) for e in S.ENG if S.cnt[e] > 0]
            for q, st in S.dq.items():
                for i2, key in enumerate(st["keys"]):
                    if st["val"][i2] > 0:
                        evs.append((key, st["val"][i2]))
            for e in S.ENG:
                for ev in evs:
                    S._wait(e, ev, False)

        def dve(fn, R, W):
            S.op("dve", fn, R, W)

        def act(fn, R, W):
            S.op("act", fn, R, W)

        def pool(fn, R, W):
            S.op("pool", fn, R, W)

        ident = sb("ident", [128, 128]); identb = sb("identb", [128, 128], BF16)
        pkT0 = sb("pkT0", [128, 128]); pkT1 = sb("pkT1", [128, 114])
        modT = sb("modT", [128, 48, 17])
        yat = [sb("yat%d" % i, [128, 8, 128], BF16) for i in range(2)]
        YA = nc.dram_tensor("YA", [17, 128, 1024], BF16).ap()
        X1 = nc.dram_tensor("X1s", [17, 128, 1024], F32).ap()
        stg = [sb("stg%d" % i, [128, 1540]) for i in range(3)]
        hTt = [sb("hTt%d" % i, [128, 8, 128], BF16) for i in range(2)]
        hsh = sb("hsh", [128, 8, NS], BF16)
        hl = sb("hl", [128, 8]); hs32 = sb("hs32", [128, 8, NST])
        NEGp = sb("NEGp", [128, 128]); NEGs = sb("NEGs", [64, 64])
        Bsel = sb("Bsel", [16, 64]); sel_last = sb("sel_last", [64, 16])
        BM = sb("BM", [128, 16, 64], BF16)
        ifb = sb("ifb", [4, 2]); nfb = sb("nfb", [4, 1])
        ones4 = sb("ones4", [4, 128]); zeros4 = sb("zeros4", [4, 128])

        S.op("pool", lambda e: e.memset(ident[:], 0.0), [], [tk("ident")])
        S.op("pool", lambda e: e.affine_select(out=ident[:], in_=ident[:], pattern=[[-1, 128]], compare_op=ALU.not_equal,
                                               fill=1.0, base=0, channel_multiplier=1), [tk("ident")], [tk("ident")])
        copy("dve", identb[:], ident[:], [tk("ident")], [tk("identb")])
        CI = [tk("ident")]
        CIB = [tk("identb")]
        pool(lambda e: e.memset(NEGp[:], 0.0), [], [tk("NEGp")])
        pool(lambda e: e.affine_select(out=NEGp[:], in_=NEGp[:], pattern=[[1, 128]], compare_op=ALU.is_ge, fill=-30000.0, base=0,
                                       channel_multiplier=-1), [tk("NEGp")], [tk("NEGp")])
        pool(lambda e: e.memset(Bsel[:], 1.0), [], [tk("Bsel")])
        pool(lambda e: e.affine_select(out=Bsel[:], in_=Bsel[:], pattern=[[1, 64]], compare_op=ALU.is_ge, fill=0.0, base=0,
                                       channel_multiplier=-4), [tk("Bsel")], [tk("Bsel")])
        pool(lambda e: e.affine_select(out=Bsel[:], in_=Bsel[:], pattern=[[-1, 64]], compare_op=ALU.is_ge, fill=0.0, base=3,
                                       channel_multiplier=4), [tk("Bsel")], [tk("Bsel")])
        pool(lambda e: e.memset(sel_last[:], 0.0), [], [tk("sel_last")])
        pool(lambda e: e.affine_select(out=sel_last[:], in_=sel_last[:], pattern=[[-4, 16]], compare_op=ALU.not_equal, fill=1.0, base=-3,
                                       channel_multiplier=1), [tk("sel_last")], [tk("sel_last")])
        pool(lambda e: e.memset(BM[:], 1.0), [], [tk("BM")])
        pool(lambda e: e.affine_select(out=BM[:], in_=BM[:], pattern=[[-4, 16], [1, 64]], compare_op=ALU.is_ge, fill=0.0, base=0,
                                       channel_multiplier=0), [tk("BM")], [tk("BM")])
        pool(lambda e: e.affine_select(out=BM[:], in_=BM[:], pattern=[[4, 16], [-1, 64]], compare_op=ALU.is_ge, fill=0.0, base=3,
                                       channel_multiplier=0), [tk("BM")], [tk("BM")])
        p, pt = nps()
        mm(p[0:64, 0:64], Bsel[:, :], Bsel[:, :], True, True, [tk("Bsel")], pt)
        dve(lambda e: e.tensor_scalar(out=NEGs[:], in0=p[0:64, 0:64], scalar1=-1.0, scalar2=30000.0, op0=ALU.add, op1=ALU.mult), pt, [tk("NEGs")])
        pool(lambda e: e.affine_select(out=NEGs[:], in_=NEGs[:], pattern=[[1, 64]], compare_op=ALU.is_ge, fill=-30000.0, base=0,
                                       channel_multiplier=-1), [tk("NEGs")], [tk("NEGs")])
        pool(lambda e: e.memset(ones4[:], 1.0), [], [tk("c4")])
        pool(lambda e: e.memset(zeros4[:], 0.0), [], [tk("c4")])
        S.dma("sp", ifb[:], I["ifb"], [], [tk("ifb")])
        dve(lambda e: e.tensor_scalar(out=nfb[:], in0=ifb[:, 1:2], scalar1=-1.0, scalar2=None, op0=ALU.mult), [tk("ifb")], [tk("nfb")])

        with ExitStack() as c0:
            cur[0] = c0
            pk0 = sb("pk0", [128, 128]); pk1 = sb("pk1", [114, 128])
            S.dma("sp", pk0[:], I["pk0"], [], [tk("pk0")])
            S.dma("sp", pk1[:], I["pk1"], [], [tk("pk1")])
            p, pt = nps()
            tr(p[:, 0:128], pk0[:], ident[:], [tk("pk0")] + CI, pt)
            tr(p[:, 128:242], pk1[:], ident[0:114, 0:114], [tk("pk1")] + CI, pt)
            copy("dve", pkT0[:], p[:, 0:128], pt, [tk("pkT0")])
            copy("dve", pkT1[:], p[:, 128:242], pt, [tk("pkT1")])
            bcT = pkT0[:, 0:48]
            cc = sb("cc", [17, D]); csT = sb("csT", [128, 8, 17])
            S.dma("sp", cc[:], I["cc"], [], [tk("cc")])
            act(lambda e: e.activation(out=cc[:], in_=cc[:], func=AF.Silu), [tk("cc")], [tk("cc")])
            p, pt = nps()
            for k in range(8):
                tr(p[:, k * 17:(k + 1) * 17], cc[:, k * 128:(k + 1) * 128], ident[0:17, 0:17], [tk("cc")] + CI, pt)
            copy("dve", csT[:].rearrange("p k s -> p (k s)"), p[:, 0:136], pt, [tk("csT")])
            wc = [sb("wc%d" % i, [128, 8, 512]) for i in range(2)]
            wcv = I["w_cond"].rearrange("(k p) c -> p k c", p=128)
            for blk in range(12):
                wt = wc[blk % 2]; wtk = tk("wc", blk % 2)
                S.dma(("sp", "pool")[blk % 2], wt[:], wcv[:, :, blk * 512:(blk + 1) * 512], [], [wtk])
                if blk % 4 == 0:
                    p, pt = nps()
                for jj in range(4):
                    j = blk * 4 + jj
                    o = (j % 16) * 17
                    for k in range(8):
                        mm(p[:, o:o + 17], wt[:, k, jj * 128:(jj + 1) * 128], csT[:, k, :], k == 0, k == 7, [wtk, tk("csT")], pt)
                if blk % 4 == 3:
                    g = blk // 4
                    dve(lambda e: e.tensor_tensor(out=modT[:, g * 16:(g + 1) * 16, :], in0=p[:, 0:272].rearrange("p (j s) -> p j s", s=17),
                                                  in1=bcT[:, g * 16:(g + 1) * 16].unsqueeze(2).to_broadcast([128, 16, 17]), op=ALU.add),
                        pt + [tk("pkT0")], [tk("modT")])
            dve(lambda e: e.tensor_scalar_add(out=modT[:, 8:16, :], in0=modT[:, 8:16, :], scalar1=1.0), [tk("modT")], [tk("modT")])
            dve(lambda e: e.tensor_scalar_add(out=modT[:, 32:40, :], in0=modT[:, 32:40, :], scalar1=1.0), [tk("modT")], [tk("modT")])
            barrier()
        cur[0] = ctx
        MT = [tk("modT")]

        XLOADED = set()

        def hT_load(i):
            x = stg[i % 3]; xk = tk("stg", i % 3)
            S.dma("sp", x[:, 0:D], I["xp"][i * 128:(i + 1) * 128, :], [], [xk])
            XLOADED.add(i)

        def make_hT(i, want_out, want_shift):
            ht = hTt[i % 2]; htk = tk("hTt", i % 2)
            x = stg[i % 3]; xk = tk("stg", i % 3)
            if i < 16:
                if i in XLOADED:
                    XLOADED.discard(i)
                else:
                    S.dma("sp", x[:, 0:D], I["xp"][i * 128:(i + 1) * 128, :], [], [xk])
                for half in range(2):
                    p, pt = nps()
                    for kk in range(4):
                        k = half * 4 + kk
                        tr(p[:, kk * 128:(kk + 1) * 128], x[:, k * 128:(k + 1) * 128], ident[:], [xk] + CI, pt)
                    for kk in range(4):
                        k = half * 4 + kk
                        en = evac_eng()
                        src = p[:, kk * 128:(kk + 1) * 128]
                        dst = ht[:, k, :]
                        if en == "dve":
                            dve(lambda e: e.tensor_scalar(out=dst, in0=src, scalar1=modT[:, 8 + k, 0:1], scalar2=modT[:, k, 0:1],
                                                          op0=ALU.mult, op1=ALU.add), pt + MT, [htk])
                        else:
                            act(lambda e: e.activation(out=dst, in_=src, func=AF.Identity, bias=modT[:, k, 0:1], scale=modT[:, 8 + k, 0:1]),
                                pt + MT, [htk])
                        if i == 15 and want_out:
                            dve(lambda e: e.tensor_scalar(out=hl[:, k:k + 1], in0=p[:, kk * 128 + 127:kk * 128 + 128], scalar1=modT[:, 8 + k, 0:1],
                                                          scalar2=modT[:, k, 0:1], op0=ALU.mult, op1=ALU.add), pt + MT, [tk("hl")])
                if i == 15 and want_out:
                    p, pt = nps(2)
                    for k in range(8):
                        tr(p[0:1, k * 128:(k + 1) * 128], hl[:, k:k + 1], ident[:], [tk("hl")] + CI, pt)
                    rw = stg[(i + 1) % 3]; rwk = tk("stg", (i + 1) % 3)
                    copy("act", rw[0:1, 0:D], p[0:1, :], pt, [rwk])
                    S.dma("sp", O["shift_p"], rw[0:1, 0:D], [rwk], [tk("o_shift_p")])
            else:
                S.dma("sp", x[0:NST, 0:D], I["xs"], [], [xk])
                p, pt = nps()
                for k in range(8):
                    tr(p[:, k * 64:(k + 1) * 64], x[0:NST, k * 128:(k + 1) * 128], ident[0:NST, 0:NST], [xk] + CI, pt)
                hsv = hs32[:].rearrange("p k (s t) -> p k s t", t=TS)
                dve(lambda e: e.tensor_tensor(out=hsv, in0=p[:, :].rearrange("p (k s t) -> p k s t", k=8, t=TS),
                                              in1=modT[:, 8:16, 1:17].unsqueeze(3).to_broadcast([128, 8, NS, TS]), op=ALU.mult), pt + MT, [tk("hs32")])
                dve(lambda e: e.tensor_tensor(out=hsv, in0=hsv, in1=modT[:, 0:8, 1:17].unsqueeze(3).to_broadcast([128, 8, NS, TS]), op=ALU.add),
                    [tk("hs32")] + MT, [tk("hs32")])
                copy("act", ht[:, :, 0:NST], hs32[:], [tk("hs32")], [htk])
                if want_out:
                    p, pt = nps(2)
                    for k in range(8):
                        tr(p[0:NS, k * 128:(k + 1) * 128], hsv[:, k, :, TS - 1], ident[:], [tk("hs32")] + CI, pt)
                    rw = stg[(i + 2) % 3]; rwk = tk("stg", (i + 2) % 3)
                    copy("act", rw[0:NS, 0:D], p[0:NS, :], pt, [rwk])
                    S.dma("sp", O["shift_s"], rw[0:NS, 0:D], [rwk], [tk("o_shift_s")])
                if want_shift:
                    x2 = stg[(i + 1) % 3]; x2k = tk("stg", (i + 1) % 3)
                    S.dma("sp", x2[0:NS, 0:D], I["stshift"], [], [x2k])
                    p, pt = nps()
                    for k in range(8):
                        tr(p[:, k * 16:(k + 1) * 16], x2[0:NS, k * 128:(k + 1) * 128], ident[0:NS, 0:NS], [x2k] + CI, pt)
                    copy("dve", hsh[:], p[:, 0:128].rearrange("p (k s) -> p k s", k=8), pt, [tk("hsh")])
            return ht, htk, x, xk

        def load_weights(W, wtk, specs, src):
            n = 0
            nk = W.shape[1]
            for k in range(nk):
                for (c0, c1, d0) in specs:
                    st = stg[n % 3]; stk = tk("stg", n % 3)
                    S.dma(("sp", "pool")[n % 2], st[:, 0:c1 - c0], src[k * 128:(k + 1) * 128, c0:c1], [], [stk])
                    en = ("dve", "act")[n % 2]
                    copy(en, W[:, k, d0:d0 + (c1 - c0)], st[:, 0:c1 - c0], [stk], [wtk])
                    n += 1
        with ExitStack() as c1:
            cur[0] = c1
            Wm = sb("Wm", [128, 8, 4104], BF16); WMK = tk("Wm")
            load_weights(Wm, WMK, [(0, 1540, 0), (1540, 3080, 1540), (OFF_GATE, OFF_GATE + 1024, 3080)], I["w_in"])
            normw = sb("normw", [128, D])
            S.dma("sp", normw[:], I["vecs"][0:1, :].to_broadcast([128, D]), [], [tk("normw")])
            qkpad1 = sb("qkpad", [128, 16 * 131]); carry = sb("carry", [128, 16, 3])
            acc = sb("acc", [128, 8, 128]); acc2 = sb("acc2", [128, 8, 128]); acc3 = sb("acc3", [128, 8, 128]); acc4 = sb("acc4", [128, 8, 128])
            qkT = sb("qkT", [128, 16, 128], BF16)
            k_tok = sb("k_tok", [128, D], BF16)
            v_ext = sb("v_ext", [128, 4, 257], BF16)
            ga = sb("ga", [128, D])
            R_igL = [sb("R_ig%d" % j, [4, 128]) for j in range(2)]; R_eL = [sb("R_e%d" % j, [4, 128]) for j in range(2)]
            R_F = [sb("R_F%d" % i, [4, 128]) for i in range(2)]
            R_Mx = [sb("R_Mx%d" % i, [4, 128]) for i in range(2)]
            R_cL = [sb("R_c%d" % j, [4, 128]) for j in range(2)]; R_AL = [sb("R_A%d" % j, [4, 128]) for j in range(2)]
            R_AWL = [sb("R_AW%d" % j, [4, 128]) for j in range(2)]
            tokrL = [sb("tokr%d" % j, [128, 16]) for j in range(2)]; emtL = [sb("emt%d" % j, [128, 4]) for j in range(2)]
            wit = sb("wit", [128, 4]); mtokL = [sb("mtok%d" % j, [128, 4]) for j in range(2)]
            E = sb("E", [128, 4, 128]); WI = sb("WI", [128, 4, 128])
            qtil = sb("qtil", [128, 2, 128], BF16)
            qtil4 = sb("qtil4", [128, 4, 2, 128], BF16); ST4 = sb("ST4", [128, 4, 128], BF16); dn4 = sb("dn4", [128, 4, 4])
            ST = sb("ST", [128, 128], BF16)
            X = {}
            HTS = {}
            GDONE = set()
            h_a = sb("h_a", [128, 4, 256]); xc = sb("xc", [128, 4, 256])
            sm = sb("sm", [128, 16])
            dn = sb("dn", [128, 4])
            pool(lambda e: e.memset(v_ext[:], 1.0), [], [tk("v_ext")])
            pool(lambda e: e.memset(carry[:], 0.0), [], [tk("carry")])

            def gates(i, ht, htk):
                T = 128 if i < 16 else NST
                smp = i == 16
                pp_ = i % 2
                R_ig = R_igL[pp_]; R_e = R_eL[pp_]; R_c = R_cL[pp_]; R_A = R_AL[pp_]; R_AW = R_AWL[pp_]
                tokr = tokrL[pp_]; emt = emtL[pp_]; mtok = mtokL[pp_]
                p, pt = nps()
                for k in range(8):
                    mm(p[0:4, 0:T], Wm[:, k, 3072:3076], ht[:, k, 0:T], k == 0, k == 7, [WMK, htk], pt)
                for k in range(8):
                    mm(p[0:4, 128:128 + T], Wm[:, k, 3076:3080], ht[:, k, 0:T], k == 0, k == 7, [WMK, htk], pt)
                RK = [tk("rowsA", pp_)]
                act(lambda e: e.activation(out=R_ig[:, 0:T], in_=p[0:4, 0:T], func=AF.Identity, bias=ifb[:, 0:1], scale=1.0), pt + [tk("ifb")], RK)
                act(lambda e: e.activation(out=R_e[:, 0:T], in_=p[0:4, 128:128 + T], func=AF.Exp, bias=nfb[:, 0:1], scale=-1.0), pt + [tk("nfb")], RK)
                act(lambda e: e.activation(out=R_e[:, 0:T], in_=R_e[:, 0:T], func=AF.Ln, bias=1.0, scale=1.0), RK, RK)
                RF = R_F[i % 2]; RFp = R_F[(i + 1) % 2]; RM = R_Mx[i % 2]; RMp = R_Mx[(i + 1) % 2]
                if not smp:
                    dve(lambda e: e.tensor_tensor_scan(out=RF[:, 0:T], data0=ones4[:, 0:T], data1=R_e[:, 0:T],
                                                       initial=(0.0 if i == 0 else RFp[:, 127:128]), op0=ALU.mult, op1=ALU.subtract), RK + [tk("c4")], RK)
                    dve(lambda e: e.tensor_tensor(out=R_c[:, 0:T], in0=R_ig[:, 0:T], in1=RF[:, 0:T], op=ALU.subtract), RK, RK)
                    dve(lambda e: e.tensor_tensor_scan(out=RM[:, 0:T], data0=zeros4[:, 0:T], data1=R_c[:, 0:T],
                                                       initial=(0.0 if i == 0 else RMp[:, 127:128]), op0=ALU.add, op1=ALU.max), RK + [tk("c4")], RK)
                    dve(lambda e: e.tensor_scalar(out=R_A[:, 0:T], in0=RM[:, 0:T], scalar1=-1.0, scalar2=None, op0=ALU.mult), RK, RK)
                    if i == 0:
                        dve(lambda e: e.tensor_copy(out=R_AW[:, 0:T], in_=R_A[:, 0:T]), RK, RK)
                    else:
                        dve(lambda e: e.tensor_scalar(out=R_AW[:, 0:T], in0=R_A[:, 0:T], scalar1=RMp[:, 127:128], scalar2=None, op0=ALU.add), RK, RK)
                else:
                    m0r = sb("m0r", [4, NS])
                    S.dma("sp", m0r[:], I["stm"].rearrange("s h -> h s"), [], [tk("m0r")], allow_slow_non_contiguous=True)
                    v4 = lambda t_: t_[:, 0:NST].rearrange("p (s t) -> p s t", t=TS)
                    Fv = v4(RF); ev = v4(R_e); cvw = v4(R_c); igv = v4(R_ig); Mv = v4(RM); Av = v4(R_A); AWv = v4(R_AW)
                    dve(lambda e: e.tensor_scalar(out=Fv[:, :, 0], in0=ev[:, :, 0], scalar1=-1.0, scalar2=None, op0=ALU.mult), RK, RK)
                    for t in range(1, TS):
                        dve(lambda e: e.tensor_tensor(out=Fv[:, :, t], in0=Fv[:, :, t - 1], in1=ev[:, :, t], op=ALU.subtract), RK, RK)
                    dve(lambda e: e.tensor_tensor(out=R_c[:, 0:T], in0=R_ig[:, 0:T], in1=RF[:, 0:T], op=ALU.subtract), RK, RK)
                    dve(lambda e: e.tensor_tensor(out=Mv[:, :, 0], in0=cvw[:, :, 0], in1=m0r[:, :], op=ALU.max), RK + [tk("m0r")], RK)
                    for t in range(1, TS):
                        dve(lambda e: e.tensor_tensor(out=Mv[:, :, t], in0=Mv[:, :, t - 1], in1=cvw[:, :, t], op=ALU.max), RK, RK)
                    dve(lambda e: e.tensor_scalar(out=R_A[:, 0:T], in0=RM[:, 0:T], scalar1=-1.0, scalar2=None, op0=ALU.mult), RK, RK)
                    dve(lambda e: e.tensor_tensor(out=AWv, in0=Av, in1=m0r[:, :].unsqueeze(2).to_broadcast([4, NS, TS]), op=ALU.add), RK + [tk("m0r")], RK)
                p, pt = nps()
                for q, Rr in enumerate((R_c, R_A, RF, R_AW)):
                    tr(p[0:T, q * 4:(q + 1) * 4], Rr[:, 0:T], ident[0:4, 0:4], RK + CI, pt)
                copy("dve", tokr[0:T, :], p[0:T, 0:16], pt, [tk("tokr", pp_)])
                dve(lambda e: e.tensor_tensor(out=mtok[0:T, :], in0=tokr[0:T, 8:12], in1=tokr[0:T, 4:8], op=ALU.subtract), [tk("tokr", pp_)], [tk("mtok", pp_)])
                act(lambda e: e.activation(out=emt[0:T, :], in_=mtok[0:T, :], func=AF.Exp, scale=-1.0), [tk("mtok", pp_)], [tk("emt", pp_)])
                if smp:
                    return m0r
                return None

            def tile(i):
                CT = X.get("CT"); CTb = X.get("CTb"); wv = X.get("wv"); wv4 = X.get("wv4")
                T = 128 if i < 16 else NST
                smp = i == 16
                if i not in HTS:
                    HTS[i] = make_hT(i, True, False)
                ht, htk, x, xk = HTS.pop(i)
                if i + 1 < 16:
                    hT_load(i + 1)
                pp_ = i % 2
                R_A = R_AL[pp_]; R_AW = R_AWL[pp_]; tokr = tokrL[pp_]; emt = emtL[pp_]; mtok = mtokL[pp_]; RK = [tk("rowsA", pp_)]
                if i not in GDONE:
                    gates(i, ht, htk)
                GDONE.discard(i)
                pad = qkpad1; padk = tk("qkpad")
                if not smp:
                    padv = pad[:].rearrange("p (c t) -> p c t", t=131)
                    pool(lambda e: e.tensor_copy(out=padv[:, :, 0:3], in_=carry[:]), [tk("carry")], [padk])
                else:
                    padv = pad[:, 0:16 * 112].rearrange("p (c s j) -> p c s j", s=NS, j=7)
                    cv = stg[2]; cvk = tk("stg", 2)
                    S.dma("sp", cv[0:48, 0:1024], I["stconv"][:, 0:1024], [], [cvk])
                    cv2 = stg[1]; cv2k = tk("stg", 1)
                    S.dma("sp", cv2[0:48, 0:1024], I["stconv"][:, 1024:2048], [], [cv2k])
                    for hf, (cvx, cvxk) in enumerate(((cv, cvk), (cv2, cv2k))):
                        p, pt = nps()
                        for c in range(8):
                            tr(p[:, c * 48:(c + 1) * 48], cvx[0:48, c * 128:(c + 1) * 128], ident[0:48, 0:48], [cvxk] + CI, pt)
                        copy("dve", padv[:, hf * 8:(hf + 1) * 8, :, 0:3], p[:, 0:384].rearrange("p (c s j) -> p c s j", s=NS, j=3), pt, [padk])
                for cg in range(4):
                    p, pt = nps()
                    for c4 in range(4):
                        c = cg * 4 + c4
                        for k in range(8):
                            mm(p[:, c4 * T:(c4 + 1) * T], Wm[:, k, c * 128:(c + 1) * 128], ht[:, k, 0:T], k == 0, k == 7, [WMK, htk], pt)
                    if not smp:
                        copy(evac_eng(), padv[:, cg * 4:(cg + 1) * 4, 3:131], p[:, 0:512].rearrange("p (c t) -> p c t", t=128), pt, [padk])
                    else:
                        copy(evac_eng(), padv[:, cg * 4:(cg + 1) * 4, :, 3:7], p[:, 0:256].rearrange("p (c s t) -> p c s t", s=NS, t=TS), pt, [padk])
                if i == 15:
                    p, pt = nps(4)
                    for c in range(16):
                        tr(p[0:3, c * 128:(c + 1) * 128], padv[:, c, 128:131], ident[:], [padk] + CI, pt)
                    cvo = stg[2]; cvok = tk("stg", 2)
                    copy("act", cvo[0:3, 0:1024], p[0:3, 0:1024], pt, [cvok])
                    S.dma("sp", O["conv_p"][:, 0:1024], cvo[0:3, 0:1024], [cvok], [tk("o_conv_p")])
                    cvo = stg[1]; cvok = tk("stg", 1)
                    copy("act", cvo[0:3, 0:1024], p[0:3, 1024:2048], pt, [cvok])
                    S.dma("sp", O["conv_p"][:, 1024:2048], cvo[0:3, 0:1024], [cvok], [tk("o_conv_p")])
                if smp:
                    cst = acc[:].rearrange("p c t -> p (c t)")[:, 0:768].rearrange("p (c s j) -> p c s j", s=NS, j=3)
                    pool(lambda e: e.tensor_copy(out=cst, in_=padv[:, :, :, 4:7]), [padk], [tk("acc")])
                    cst2 = acc[:].rearrange("p c t -> p (c t)")[:, 0:768].rearrange("p (c m) -> p c m", m=48)
                    p, pt = nps(4)
                    for c in range(16):
                        tr(p[0:48, c * 128:(c + 1) * 128], cst2[:, c, :], ident[:], [tk("acc")] + CI, pt)
                    for hf in range(2):
                        cvo = stg[2 - hf]; cvok = tk("stg", 2 - hf)
                        copy("act", cvo[0:48, 0:1024], p[0:48, hf * 1024:(hf + 1) * 1024], pt, [cvok])
                        S.dma("sp", O["conv_s"][:, hf * 1024:(hf + 1) * 1024], cvo[0:48, 0:1024], [cvok], [tk("o_conv_s")])
                for hf in range(2):
                    cs_ = slice(hf * 8, hf * 8 + 8)
                    if not smp:
                        accv = acc[:]; acc2v = acc2[:]; acc3v = acc3[:]; acc4v = acc4[:]
                        sl = lambda j: padv[:, cs_, j:j + 128]
                        wb = lambda j: pkT0[:, 48 + 16 * j + hf * 8:56 + 16 * j + hf * 8].unsqueeze(2).to_broadcast([128, 8, 128])
                    else:
                        accv = acc[:, :, 0:NST].rearrange("p c (s t) -> p c s t", t=TS)
                        acc2v = acc2[:, :, 0:NST].rearrange("p c (s t) -> p c s t", t=TS)
                        acc3v = acc3[:, :, 0:NST].rearrange("p c (s t) -> p c s t", t=TS)
                        acc4v = acc4[:, :, 0:NST].rearrange("p c (s t) -> p c s t", t=TS)
                        sl = lambda j: padv[:, cs_, :, j:j + 4]
                        wb = lambda j: pkT0[:, 48 + 16 * j + hf * 8:56 + 16 * j + hf * 8].unsqueeze(2).unsqueeze(3).to_broadcast([128, 8, NS, TS])
                    dve(lambda e: e.tensor_tensor(out=accv, in0=sl(0), in1=wb(0), op=ALU.mult), [padk, tk("pkT0")], [tk("acc")])
                    pool(lambda e: e.tensor_tensor(out=acc2v, in0=sl(1), in1=wb(1), op=ALU.mult), [padk, tk("pkT0")], [tk("acc2")])
                    dve(lambda e: e.tensor_tensor(out=acc3v, in0=sl(2), in1=wb(2), op=ALU.mult), [padk, tk("pkT0")], [tk("acc3")])
                    pool(lambda e: e.tensor_tensor(out=acc4v, in0=sl(3), in1=wb(3), op=ALU.mult), [padk, tk("pkT0")], [tk("acc4")])
                    dve(lambda e: e.tensor_tensor(out=accv, in0=accv, in1=acc2v, op=ALU.add), [tk("acc"), tk("acc2")], [tk("acc")])
                    dve(lambda e: e.tensor_tensor(out=acc3v, in0=acc3v, in1=acc4v, op=ALU.add), [tk("acc3"), tk("acc4")], [tk("acc3")])
                    dve(lambda e: e.tensor_tensor(out=accv, in0=accv, in1=acc3v, op=ALU.add), [tk("acc"), tk("acc3")], [tk("acc")])
                    for c8 in range(8):
                        c = hf * 8 + c8
                        act(lambda e: e.activation(out=qkT[:, c, 0:T], in_=acc[:, c8, 0:T], func=AF.Silu, bias=pkT0[:, 112 + c:113 + c]),
                            [tk("acc"), tk("pkT0")], [tk("qkT")])
                if not smp:
                    pool(lambda e: e.tensor_copy(out=carry[:], in_=padv[:, :, 128:131]), [padk], [tk("carry")])
                if i + 1 < 16:
                    HTS[i + 1] = make_hT(i + 1, True, False)
                    gates(i + 1, HTS[i + 1][0], HTS[i + 1][1])
                    GDONE.add(i + 1)
                p, pt = nps()
                pb = p.bitcast(BF16)
                for c in range(8):
                    tr(pb[0:T, c * 128:(c + 1) * 128], qkT[:, 8 + c, 0:T], identb[:], [tk("qkT")] + CIB, pt)
                dve(lambda e: e.tensor_scalar(out=k_tok[0:T, :], in0=pb[0:T, 0:1024], scalar1=0.0625, scalar2=None, op0=ALU.mult), pt, [tk("k_tok")])
                p, pt = nps(2)
                for j in range(2):
                    for k in range(8):
                        mm(p[0:T, j * 512:(j + 1) * 512], ht[:, k, 0:T], Wm[:, k, 2048 + j * 512:2048 + (j + 1) * 512], k == 0, k == 7, [WMK, htk], [pt[j]])
                copy("act", v_ext[0:T, :, 0:256], p[0:T, 0:1024].rearrange("p (h v) -> p h v", h=4), pt, [tk("v_ext")])
                p, pt = nps(2)
                for j in range(2):
                    for k in range(8):
                        mm(p[0:T, j * 512:(j + 1) * 512], ht[:, k, 0:T], Wm[:, k, 3080 + j * 512:3080 + (j + 1) * 512], k == 0, k == 7, [WMK, htk], [pt[j]])
                act(lambda e: e.activation(out=ga[0:T, :], in_=p[0:T, 0:1024], func=AF.Sigmoid), pt, [tk("ga")])
                if i == 15:
                    S.dma("sp", O["m_p"], mtok[127:128, :], [tk("mtok", pp_)], [tk("o_m_p")])
                if smp:
                    act(lambda e: e.activation(out=wit[0:T, :], in_=tokr[0:T, 12:16], func=AF.Exp), [tk("tokr", pp_)], [tk("wit")])
                    p, pt = nps()
                    mm(p[0:NS, 0:4], sel_last[:, :], mtok[0:NST, :], True, True, [tk("sel_last"), tk("mtok", pp_)], pt)
                    mm(p[0:NS, 4:8], sel_last[:, :], wit[0:NST, :], True, True, [tk("sel_last"), tk("wit")], pt)
                    msd = sb("msd", [NS, 8])
                    copy("dve", msd[:], p[0:NS, 0:8], pt, [tk("msd")])
                    S.dma("sp", O["m_s"], msd[:, 0:4], [tk("msd")], [tk("o_m_s")])
                    n_sh = sb("n_sh", [NS, 4, 256]); nT = sb("nT", [128, 2, 4, NS], BF16)
                    S.dma("sp", n_sh[:], I["stn"].rearrange("(s h) k -> s h k", h=4), [], [tk("n_sh")])
                    p, pt = nps()
                    for kc in range(2):
                        for h in range(4):
                            tr(p[:, (kc * 4 + h) * NS:(kc * 4 + h + 1) * NS], n_sh[:, h, kc * 128:(kc + 1) * 128], ident[0:NS, 0:NS], [tk("n_sh")] + CI, pt)
                    copy("dve", nT[:].rearrange("p a h s -> p (a h s)"), p[:, 0:128], pt, [tk("nT")])
                    Cnat = [sb("Cnat%d" % j, [128, 2, 256]) for j in range(3)]
                    Cnew = [sb("Cnew%d" % j, [128, 2, 256]) for j in range(2)]
                    CTs = [sb("CTs%d" % j, [128, 2, 257], BF16) for j in range(2)]
                    qpad = sb("qpad", [128, 2, NS, NST], BF16)
                    wvm = [sb("wvm%d" % j, [NST, 256], BF16) for j in range(2)]
                NEG = NEGs if smp else NEGp
                if not smp:
                    pAs = []
                    for h in range(4):
                        pA, pAt = nps()
                        mm(pA[0:T, 0:T], ident[0:4, h:h + 1].to_broadcast([4, T]), R_A[:, 0:T], True, False, RK + CI, pAt)
                        mm(pA[0:T, 0:T], ident[0:T, 0:T], NEG[0:T, 0:T], False, True, CI + [tk("NEGp"), tk("NEGs")], pAt)
                        mm(pA[:, 128:128 + T], ident[0:4, h:h + 1].to_broadcast([4, 128]), R_AW[:, 0:T], True, True, RK + CI, pAt)
                        pAs.append((pA, pAt))
                    for h in range(4):
                        pA, pAt = pAs[h]
                        act(lambda e: e.activation(out=E[0:T, h, 0:T], in_=pA[0:T, 0:T], func=AF.Exp, bias=tokr[0:T, h:h + 1], scale=1.0),
                            pAt + [tk("tokr", pp_)], [tk("E", h)])
                        act(lambda e: e.activation(out=WI[:, h, 0:T], in_=pA[:, 128:128 + T], func=AF.Exp), pAt, [tk("WI", h)])
                    for h in range(4):
                        dve(lambda e: e.tensor_tensor(out=qtil4[:, h, :, :], in0=qkT[:, 2 * h:2 * h + 2, :],
                                                      in1=WI[:, h, :].unsqueeze(1).to_broadcast([128, 2, 128]), op=ALU.mult),
                            [tk("qkT"), tk("WI", h)], [tk("qtil4", h)])
                    p2s = []
                    for h in range(4):
                        p2, p2t = nps()
                        for ch in range(2):
                            mm(p2[:, 0:128], qkT[:, 8 + 2 * h + ch, :], qkT[:, 2 * h + ch, :], ch == 0, ch == 1, [tk("qkT")], p2t)
                        p2s.append((p2, p2t))
                    for h in range(4):
                        p2, p2t = p2s[h]
                        dve(lambda e: e.scalar_tensor_tensor(out=ST4[:, h, :], in0=p2[:, 0:128], scalar=0.0625, in1=E[:, h, :], op0=ALU.mult, op1=ALU.mult),
                            p2t + [tk("E", h)], [tk("ST4", h)])
                    p3s = []
                    for h in range(4):
                        p3, p3t = nps()
                        for ch in range(2):
                            mm(p3[:, 0:257], qtil4[:, h, ch, :], CTb[:, ch, h, :], ch == 0, False, [tk("qtil4", h), tk("CTb", h)], p3t)
                        mm(p3[:, 0:257], ST4[:, h, :], v_ext[:, h, :], False, True, [tk("ST4", h), tk("v_ext")], p3t)
                        p3s.append((p3, p3t))
                    for h in range(4):
                        p3, p3t = p3s[h]
                        act(lambda e: e.activation(out=dn4[:, h, 0:1], in_=p3[:, 256:257], func=AF.Abs), p3t, [tk("dn4", h)])
                        dve(lambda e: e.tensor_tensor(out=dn4[:, h, 1:2], in0=dn4[:, h, 0:1], in1=emt[:, h:h + 1], op=ALU.max), [tk("dn4", h), tk("emt", pp_)], [tk("dn4", h)])
                        dve(lambda e: e.reciprocal(out=dn4[:, h, 2:3], in_=dn4[:, h, 1:2]), [tk("dn4", h)], [tk("dn4", h)])
                        dve(lambda e: e.tensor_scalar(out=h_a[:, h, :], in0=p3[:, 0:256], scalar1=dn4[:, h, 2:3], scalar2=None, op0=ALU.mult),
                            p3t + [tk("dn4", h)], [tk("h_a")])
                    for h in range(4):
                        act(lambda e: e.activation(out=wv4[:, h, :], in_=v_ext[:, h, :], func=AF.Identity, scale=E[:, h, 127:128]),
                            [tk("v_ext"), tk("E", h)], [tk("wv4", h)])
                    for h in range(4):
                        p4, p4t = nps(2)
                        for ch in range(2):
                            mm(p4[:, ch * 512:ch * 512 + 257], k_tok[:, h * 256 + ch * 128:h * 256 + (ch + 1) * 128], wv4[:, h, :], True, True,
                               [tk("k_tok"), tk("wv4", h)], [p4t[ch]])
                        dve(lambda e: e.scalar_tensor_tensor(out=CT[:, :, h, :], in0=CT[:, :, h, :], scalar=WI[:, h, 127:128],
                                                             in1=p4[:, :].rearrange("p (a k) -> p a k", a=2)[:, :, 0:257], op0=ALU.mult, op1=ALU.add),
                            [tk("CT", h), tk("WI", h)] + p4t, [tk("CT", h)])
                        copy("act", CTb[:, :, h, :], CT[:, :, h, :], [tk("CT", h)], [tk("CTb", h)])
                for h in (range(4) if smp else []):
                    pA, pAt = nps()
                    mm(pA[0:T, 0:T], ident[0:4, h:h + 1].to_broadcast([4, T]), R_A[:, 0:T], True, False, RK + CI, pAt)
                    mm(pA[0:T, 0:T], ident[0:T, 0:T], NEG[0:T, 0:T], False, True, CI + [tk("NEGp"), tk("NEGs")], pAt)
                    mm(pA[:, 128:128 + T], ident[0:4, h:h + 1].to_broadcast([4, 128]), R_AW[:, 0:T], True, True, RK + CI, pAt)
                    act(lambda e: e.activation(out=E[0:T, h, 0:T], in_=pA[0:T, 0:T], func=AF.Exp, bias=tokr[0:T, h:h + 1], scale=1.0),
                        pAt + [tk("tokr", pp_)], [tk("E", h)])
                    act(lambda e: e.activation(out=WI[:, h, 0:T], in_=pA[:, 128:128 + T], func=AF.Exp), pAt, [tk("WI", h)])
                    dve(lambda e: e.tensor_tensor(out=qtil[:, :, 0:T], in0=qkT[:, 2 * h:2 * h + 2, 0:T],
                                                  in1=WI[:, h, 0:T].unsqueeze(1).to_broadcast([128, 2, T]), op=ALU.mult),
                        [tk("qkT"), tk("WI", h)], [tk("qtil")])
                    p2, p2t = nps()
                    for ch in range(2):
                        mm(p2[0:T, 0:T], qkT[:, 8 + 2 * h + ch, 0:T], qkT[:, 2 * h + ch, 0:T], ch == 0, ch == 1, [tk("qkT")], p2t)
                    dve(lambda e: e.scalar_tensor_tensor(out=ST[0:T, 0:T], in0=p2[0:T, 0:T], scalar=0.0625, in1=E[0:T, h, 0:T], op0=ALU.mult, op1=ALU.mult),
                        p2t + [tk("E", h)], [tk("ST")])
                    p3, p3t = P7, P7t
                    if not smp:
                        for ch in range(2):
                            mm(p3[0:T, 0:257], qtil[:, ch, 0:T], CTb[:, ch, h, :], ch == 0, False, [tk("qtil"), tk("CTb")], p3t)
                    else:
                        dve(lambda e: e.tensor_tensor(out=qpad[:], in0=qtil[:, :, 0:NST].unsqueeze(2).to_broadcast([128, 2, NS, NST]),
                                                      in1=BM[:].unsqueeze(1).to_broadcast([128, 2, NS, NST]), op=ALU.mult),
                            [tk("qtil"), tk("BM")], [tk("qpad")])
                        def stA(s):
                            pr_ = s * 4 + h
                            Cn = Cnat[pr_ % 3]; Cnk = tk("Cnat", pr_ % 3)
                            S.dma(("sp", "act")[s % 2], Cn[:], I["stC"][pr_ * 256:(pr_ + 1) * 256, :].rearrange("(vc p) k -> p vc k", p=128), [], [Cnk])
                            pT, pTt = nps()
                            for kc in range(2):
                                for vc in range(2):
                                    tr(pT[:, kc * 256 + vc * 128:kc * 256 + (vc + 1) * 128], Cn[:, vc, kc * 128:(kc + 1) * 128], ident[:], [Cnk] + CI, pTt)
                            Cs = CTs[s % 2]; Csk = tk("CTs", s % 2)
                            copy("act", Cs[:, :, 0:256], pT[:, 0:512].rearrange("p (a v) -> p a v", a=2), pTt, [Csk])
                            pool(lambda e: e.tensor_copy(out=Cs[:, :, 256:257], in_=nT[:, :, h, s:s + 1]), [tk("nT")], [Csk])

                        def stB(s):
                            pr_ = s * 4 + h
                            Cn = Cnat[pr_ % 3]; Cnk = tk("Cnat", pr_ % 3)
                            Cs = CTs[s % 2]; Csk = tk("CTs", s % 2)
                            for ch in range(2):
                                mm(p3[0:T, 0:257], qpad[:, ch, s, :], Cs[:, ch, :], (s == 0 and ch == 0), False, [tk("qpad"), Csk], p3t)
                            wm_ = wvm[s % 2]; wmk = tk("wvm", s % 2)
                            act(lambda e: e.activation(out=wm_[:, :], in_=v_ext[0:NST, h, 0:256], func=AF.Identity, scale=E[0:NST, h, 4 * s + 3:4 * s + 4]),
                                [tk("v_ext"), tk("E", h)], [wmk])
                            pU, pUt = nps()
                            for vc in range(2):
                                mm(pU[:, vc * 256:(vc + 1) * 256], wm_[:, vc * 128:(vc + 1) * 128], k_tok[0:NST, h * 256:(h + 1) * 256], True, True,
                                   [wmk, tk("k_tok")], pUt)
                            Cw = Cnew[s % 2]; Cwk = tk("Cnew", s % 2)
                            dve(lambda e: e.scalar_tensor_tensor(out=Cw[:], in0=Cn[:], scalar=WI[:, h, 4 * s + 3:4 * s + 4],
                                                                 in1=pU[:, 0:512].rearrange("p (a k) -> p a k", a=2), op0=ALU.mult, op1=ALU.add),
                                [Cnk, tk("WI", h)] + pUt, [Cwk])
                            S.dma("pool", O["C_s"][pr_ * 256:(pr_ + 1) * 256, :].rearrange("(vc p) k -> p vc k", p=128), Cw[:], [Cwk], [tk("o_C_s")])

                        stA(0)
                        for s in range(NS):
                            if s + 1 < NS:
                                stA(s + 1)
                            stB(s)
                    mm(p3[0:T, 0:257], ST[0:T, 0:T], v_ext[0:T, h, :], False, True, [tk("ST"), tk("v_ext")], p3t)
                    act(lambda e: e.activation(out=dn[0:T, 0:1], in_=p3[0:T, 256:257], func=AF.Abs), p3t, [tk("dn")])
                    dve(lambda e: e.tensor_tensor(out=dn[0:T, 1:2], in0=dn[0:T, 0:1], in1=emt[0:T, h:h + 1], op=ALU.max), [tk("dn"), tk("emt", pp_)], [tk("dn")])
                    dve(lambda e: e.reciprocal(out=dn[0:T, 2:3], in_=dn[0:T, 1:2]), [tk("dn")], [tk("dn")])
                    dve(lambda e: e.tensor_scalar(out=h_a[0:T, h, :], in0=p3[0:T, 0:256], scalar1=dn[0:T, 2:3], scalar2=None, op0=ALU.mult),
                        p3t + [tk("dn")], [tk("h_a")])
                    if not smp:
                        pool(lambda e: e.tensor_scalar(out=wv[0:T, :], in0=v_ext[0:T, h, :], scalar1=E[0:T, h, T - 1:T], scalar2=None, op0=ALU.mult),
                             [tk("v_ext"), tk("E", h)], [tk("wv")])
                        p4, p4t = nps(2)
                        for ch in range(2):
                            mm(p4[:, ch * 512:ch * 512 + 257], k_tok[0:T, h * 256 + ch * 128:h * 256 + (ch + 1) * 128], wv[0:T, :], True, True,
                               [tk("k_tok"), tk("wv")], [p4t[ch]])
                        dve(lambda e: e.scalar_tensor_tensor(out=CT[:, :, h, :], in0=CT[:, :, h, :], scalar=WI[:, h, T - 1:T],
                                                             in1=p4[:, :].rearrange("p (a k) -> p a k", a=2)[:, :, 0:257], op0=ALU.mult, op1=ALU.add),
                            [tk("CT", h), tk("WI", h)] + p4t, [tk("CT", h)])
                        copy("act", CTb[:, :, h, :], CT[:, :, h, :], [tk("CT", h)], [tk("CTb", h)])
                if smp:
                    Elast = sb("Elast", [NST, 4, NS], BF16)
                    copy("dve", Elast[:], E[0:NST, :, 3:NST:4], [tk("E", 0), tk("E", 1), tk("E", 2), tk("E", 3)], [tk("Elast")])
                    p, pt = nps(2)
                    for h in range(4):
                        mm(p[0:NS, h * 256:(h + 1) * 256], Elast[:, h, :], k_tok[0:NST, h * 256:(h + 1) * 256], True, True,
                           [tk("Elast"), tk("k_tok")], [pt[h // 2]])
                    for h in range(4):
                        dve(lambda e: e.scalar_tensor_tensor(out=n_sh[:, h, :], in0=n_sh[:, h, :], scalar=msd[:, 4 + h:5 + h], in1=p[0:NS, h * 256:(h + 1) * 256],
                                                             op0=ALU.mult, op1=ALU.add), [tk("n_sh"), tk("msd")] + pt, [tk("n_sh")])
                    S.dma("sp", O["n_s"].rearrange("(s h) k -> s h k", h=4), n_sh[:], [tk("n_sh")], [tk("o_n_s")])
                dve(lambda e: e.tensor_reduce(out=sm[0:T, 0:4], in_=h_a[0:T], axis=AX.X, op=ALU.add), [tk("h_a")], [tk("sm")])
                dve(lambda e: e.tensor_scalar(out=sm[0:T, 4:8], in0=sm[0:T, 0:4], scalar1=1.0 / 256.0, scalar2=None, op0=ALU.mult), [tk("sm")], [tk("sm")])
                dve(lambda e: e.tensor_tensor(out=xc[0:T], in0=h_a[0:T], in1=sm[0:T, 4:8].unsqueeze(2).to_broadcast([T, 4, 256]), op=ALU.subtract),
                    [tk("h_a"), tk("sm")], [tk("xc")])
                for h in range(4):
                    act(lambda e: e.activation(out=h_a[0:T, h, :], in_=xc[0:T, h, :], func=AF.Square, accum_out=sm[0:T, 8 + h:9 + h]),
                        [tk("xc")], [tk("h_a"), tk("sm")])
                dve(lambda e: e.tensor_scalar(out=sm[0:T, 8:12], in0=sm[0:T, 8:12], scalar1=1.0 / 256.0, scalar2=1e-6, op0=ALU.mult, op1=ALU.add),
                    [tk("sm")], [tk("sm")])
                act(lambda e: e.activation(out=sm[0:T, 12:16], in_=sm[0:T, 8:12], func=AF.Sqrt), [tk("sm")], [tk("sm")])
                dve(lambda e: e.reciprocal(out=sm[0:T, 8:12], in_=sm[0:T, 12:16]), [tk("sm")], [tk("sm")])
                dve(lambda e: e.tensor_tensor(out=xc[0:T], in0=xc[0:T], in1=sm[0:T, 8:12].unsqueeze(2).to_broadcast([T, 4, 256]), op=ALU.mult),
                    [tk("xc"), tk("sm")], [tk("xc")])
                xcf = xc[0:T].rearrange("p h v -> p (h v)")
                pool(lambda e: e.tensor_tensor(out=xcf, in0=xcf, in1=normw[0:T, :], op=ALU.mult), [tk("xc"), tk("normw")], [tk("xc")])
                dve(lambda e: e.tensor_tensor(out=xcf, in0=xcf, in1=ga[0:T, :], op=ALU.mult), [tk("xc"), tk("ga")], [tk("xc")])
                p, pt = nps(2)
                for c in range(8):
                    tr(p[:, c * T:(c + 1) * T], xc[0:T].rearrange("p h v -> p (h v)")[:, c * 128:(c + 1) * 128], ident[0:T, 0:T], [tk("xc")] + CI,
                       [pt[(c * T) // 512]])
                yt_ = yat[i % 2]; ytk = tk("yat", i % 2)
                copy("act", yt_[:, :, 0:T], p[:, 0:8 * T].rearrange("p (c t) -> p c t", t=T), pt, [ytk])
                S.dma("sp", YA[i].rearrange("p (c t) -> p c t", c=8)[:, :, 0:T], yt_[:, :, 0:T], [ytk], [tk("YA", i)])
            with ExitStack() as c1p:
                cur[0] = c1p
                X["CT"] = sb("CT", [128, 2, 4, 257]); X["CTb"] = sb("CTb", [128, 2, 4, 257], BF16); X["wv"] = sb("wv", [128, 257], BF16); X["wv4"] = sb("wv4", [128, 4, 257], BF16)
                CT = X["CT"]
                pool(lambda e: e.memset(X["CT"][:], 0.0), [], [tk("CT", h_) for h_ in range(4)])
                pool(lambda e: e.memset(X["CTb"][:], 0.0), [], [tk("CTb", h_) for h_ in range(4)])
                for i in range(16):
                    tile(i)
                Cout = sb("Cout", [128, 2, 256])
                for h in range(4):
                    pO, pOt = nps()
                    for vc in range(2):
                        for kc in range(2):
                            tr(pO[:, vc * 256 + kc * 128:vc * 256 + (kc + 1) * 128], CT[:, kc, h, vc * 128:(vc + 1) * 128], ident[:], [tk("CT", 0), tk("CT", 1), tk("CT", 2), tk("CT", 3)] + CI, pOt)
                    copy("act", Cout[:], pO[:, 0:512].rearrange("p (a k) -> p a k", a=2), pOt, [tk("Cout")])
                    S.dma("sp", O["C_p"][h * 256:(h + 1) * 256, :].rearrange("(vc p) k -> p vc k", p=128), Cout[:], [tk("Cout")], [tk("o_C_p")])
                p, pt = nps(2)
                for h in range(4):
                    for kc in range(2):
                        o_ = (h * 2 + kc) * 128
                        tr(p[0:1, o_:o_ + 128], CT[:, kc, h, 256:257], ident[:], [tk("CT", 0), tk("CT", 1), tk("CT", 2), tk("CT", 3)] + CI, [pt[o_ // 512]])
                rw = stg[0]; rwk = tk("stg", 0)
                copy("act", rw[0:1, 0:D], p[0:1, :], pt, [rwk])
                S.dma("sp", O["n_p"].rearrange("h k -> (h k)").rearrange("(o n) -> o n", o=1), rw[0:1, 0:D], [rwk], [tk("o_n_p")])
                barrier()
            with ExitStack() as c1s:
                cur[0] = c1s
                tile(16)
                barrier()
        cur[0] = ctx
        SCR6 = nc.dram_tensor("SCR6", [6, NST, D], F32).ap()
        SCRO = nc.dram_tensor("SCRO", [NST, D], F32).ap()
        with ExitStack() as c2:
            cur[0] = c2
            Wr = sb("Wr", [128, 8, 4352], BF16); WRK = tk("Wr")
            load_weights(Wr, WRK, [(3080, 4620, 0), (4620, 6160, 1540), (6160, 6408, 3080), (7432, 8456, 3328)], I["w_in"])
            WA2 = sb("WA2", [128, D], BF16); G2b = sb("G2b", [128, D], BF16)
            S.dma("sp", stg[0][0:64, 0:D], I["w2"], [], [tk("stg", 0)])
            S.dma("sp", stg[0][64:128, 0:D], I["a2"], [], [tk("stg", 0)])
            copy("dve", WA2[:], stg[0][:, 0:D], [tk("stg", 0)], [tk("WA2")])
            S.dma("sp", stg[1][:, 0:D], I["g2"], [], [tk("stg", 1)])
            copy("act", G2b[:], stg[1][:, 0:D], [tk("stg", 1)], [tk("G2b")])
            blk2 = sb("blk2", [128, 128]); ones128 = sb("ones128", [128, 128]); omka = sb("omka", [128, 8])
            pool(lambda e: e.memset(blk2[:], 0.0), [], [tk("blk2")])
            pool(lambda e: e.memset(blk2[0:64, 0:64], 1.0), [], [tk("blk2")])
            pool(lambda e: e.memset(blk2[64:128, 64:128], 1.0), [], [tk("blk2")])
            pool(lambda e: e.memset(ones128[:], 1.0), [], [tk("ones128")])
            eps64 = sb("eps64", [128, 1])
            pool(lambda e: e.memset(eps64[:], 64e-5), [], [tk("eps64")])
            dve(lambda e: e.tensor_scalar(out=omka[:], in0=pkT1[:, 50:58], scalar1=-1.0, scalar2=1.0, op0=ALU.mult, op1=ALU.add), [tk("pkT1")], [tk("omka")])
            PK = [tk("pkT1")]
            prg = [sb("prg%d" % j, [128, 516]) for j in range(2)]
            rcarry = sb("rcarry", [128, 26, 1])
            xsb = sb("xsb", [128, 26, 128])
            ETA = sb("ETA", [128, 8, 128]); KK = sb("KK", [128, 8, 128]); KM = sb("KM", [128, 8, 128])
            LW = sb("LW", [128, 8, 128]); LP = sb("LP", [128, 8, 128]); EX = sb("EX", [128, 8, 128]); TMP = sb("TMP", [128, 8, 128])
            RKV = sb("RKV", [128, 8, 128]); OT = LW
            lo24 = sb("lo24", [128, 128], BF16); sgb = sb("sgb", [128, 128], BF16)
            eplast = sb("eplast", [128, 8, 1])
            pool(lambda e: e.memset(rcarry[:], 0.0), [], [tk("rcarry")])
            mu = pkT1[:, 0:26]
            LWC = -0.6065306597126334

            def bc(ap, n, T):
                return ap.unsqueeze(2).to_broadcast([128, n, T])

            HTS2 = {}

            def front(i):
                T = 128 if i < 16 else NST
                smp = i == 16
                if i not in HTS2:
                    HTS2[i] = make_hT(i, False, smp)
                ht, htk, x, xk = HTS2.pop(i)
                if i + 1 < 16:
                    hT_load(i + 1)
                for grp in range(7):
                    cs0 = 4 * grp; n = min(4, 26 - cs0)
                    p, pt = nps()
                    for ci in range(n):
                        c = cs0 + ci
                        for k in range(8):
                            mm(p[:, ci * T:(ci + 1) * T], Wr[:, k, c * 128:(c + 1) * 128], ht[:, k, 0:T], k == 0, k == 7, [WRK, htk], pt)
                        if smp:
                            for k in range(8):
                                mm(p[:, 256 + ci * 16:256 + (ci + 1) * 16], Wr[:, k, c * 128:(c + 1) * 128], hsh[:, k, :], k == 0, k == 7, [WRK, tk("hsh")], pt)
                    pg = prg[grp % 2]; pgk = tk("prg", grp % 2)
                    if not smp:
                        pv = pg[:, 0:516].rearrange("p (c t) -> p c t", t=129)
                        pool(lambda e: e.tensor_copy(out=pv[:, 0:n, 0:1], in_=rcarry[:, cs0:cs0 + n, :]), [tk("rcarry")], [pgk])
                        copy("act", pv[:, 0:n, 1:129], p[:, 0:n * 128].rearrange("p (c t) -> p c t", t=128), pt, [pgk])
                        prev = pv[:, 0:n, 0:128]; cur_ = pv[:, 0:n, 1:129]
                        dst = xsb[:, cs0:cs0 + n, :]
                        mub = bc(mu[:, cs0:cs0 + n], n, 128)
                        pool(lambda e: e.tensor_copy(out=rcarry[:, cs0:cs0 + n, :], in_=pv[:, 0:n, 128:129]), [pgk], [tk("rcarry")])
                    else:
                        pv = pg[:, 0:320].rearrange("p (c s j) -> p c s j", s=NS, j=5)
                        copy("act", pv[:, 0:n, :, 1:5], p[:, 0:n * 64].rearrange("p (c s t) -> p c s t", s=NS, t=TS), pt, [pgk])
                        copy("act", pv[:, 0:n, :, 0], p[:, 256:256 + n * 16].rearrange("p (c s) -> p c s", s=NS), pt, [pgk])
                        prev = pv[:, 0:n, :, 0:4]; cur_ = pv[:, 0:n, :, 1:5]
                        dst = xsb[:, cs0:cs0 + n, 0:NST].rearrange("p c (s t) -> p c s t", t=TS)
                        mub = mu[:, cs0:cs0 + n].unsqueeze(2).unsqueeze(3).to_broadcast([128, n, NS, TS])
                    dve(lambda e: e.tensor_tensor(out=dst, in0=prev, in1=cur_, op=ALU.subtract), [pgk], [tk("xsb")])
                    dve(lambda e: e.tensor_tensor(out=dst, in0=dst, in1=mub, op=ALU.mult), [tk("xsb")] + PK, [tk("xsb")])
                    dve(lambda e: e.tensor_tensor(out=dst, in0=dst, in1=cur_, op=ALU.add), [tk("xsb"), pgk], [tk("xsb")])
                XS = [tk("xsb")]
                if i + 1 < 16:
                    HTS2[i + 1] = make_hT(i + 1, False, False)
                act(lambda e: e.activation(out=lo24[0:64, 0:T], in_=xsb[0:64, 24, 0:T], func=AF.Tanh), XS, [tk("lo24")])
                act(lambda e: e.copy(out=lo24[64:128, 0:T], in_=xsb[64:128, 24, 0:T]), XS, [tk("lo24")])
                act(lambda e: e.activation(out=sgb[:, 0:T], in_=xsb[:, 25, 0:T], func=AF.Sigmoid), XS, [tk("sgb")])
                nb = (8 * T + 511) // 512
                p, pt = nps(nb)
                for c in range(8):
                    mm(p[:, c * T:(c + 1) * T], WA2[0:64, c * 128:(c + 1) * 128], lo24[0:64, 0:T], True, True, [tk("WA2"), tk("lo24")], pt)
                for c in range(8):
                    act(lambda e: e.activation(out=LW[:, c, 0:T], in_=p[:, c * T:(c + 1) * T], func=AF.Sigmoid, bias=pkT1[:, 26 + c:27 + c]), pt + PK, [tk("LW")])
                p, pt = nps(nb)
                for c in range(8):
                    mm(p[:, c * T:(c + 1) * T], WA2[64:128, c * 128:(c + 1) * 128], lo24[64:128, 0:T], True, True, [tk("WA2"), tk("lo24")], pt)
                for c in range(8):
                    act(lambda e: e.activation(out=ETA[:, c, 0:T], in_=p[:, c * T:(c + 1) * T], func=AF.Sigmoid, bias=pkT1[:, 34 + c:35 + c]), pt + PK, [tk("ETA")])
                r_ = xsb[:, 0:8, 0:T]; kr = xsb[:, 8:16, 0:T]; vr = xsb[:, 16:24, 0:T]
                dve(lambda e: e.tensor_tensor(out=KK[:, :, 0:T], in0=kr, in1=bc(pkT1[:, 42:50], 8, T), op=ALU.mult), XS + PK, [tk("KK")])
                act(lambda e: e.activation(out=TMP[:, :, 0:T], in_=KK[:, :, 0:T], func=AF.Square), [tk("KK")], [tk("TMP")])
                p, pt = nps(nb)
                for c in range(8):
                    mm(p[:, c * T:(c + 1) * T], blk2[:], TMP[:, c, 0:T], True, True, [tk("blk2"), tk("TMP")], pt)
                pv8 = p[:, 0:8 * T].rearrange("p (c t) -> p c t", t=T)
                act(lambda e: e.activation(out=EX[:, :, 0:T], in_=pv8, func=AF.Sqrt), pt, [tk("EX")])
                dve(lambda e: e.tensor_scalar(out=EX[:, :, 0:T], in0=EX[:, :, 0:T], scalar1=1e-12, scalar2=None, op0=ALU.max), [tk("EX")], [tk("EX")])
                dve(lambda e: e.reciprocal(out=EX[:, :, 0:T], in_=EX[:, :, 0:T]), [tk("EX")], [tk("EX")])
                dve(lambda e: e.tensor_tensor(out=KK[:, :, 0:T], in0=KK[:, :, 0:T], in1=EX[:, :, 0:T], op=ALU.mult), [tk("KK"), tk("EX")], [tk("KK")])
                for c in range(8):
                    act(lambda e: e.activation(out=KM[:, c, 0:T], in_=ETA[:, c, 0:T], func=AF.Identity, scale=pkT1[:, 50 + c:51 + c], bias=omka[:, c:c + 1]),
                        [tk("ETA"), tk("omka")] + PK, [tk("KM")])
                dve(lambda e: e.tensor_tensor(out=KM[:, :, 0:T], in0=KM[:, :, 0:T], in1=kr, op=ALU.mult), [tk("KM")] + XS, [tk("KM")])
                pool(lambda e: e.tensor_tensor(out=TMP[:, :, 0:T], in0=r_, in1=KM[:, :, 0:T], op=ALU.mult), XS + [tk("KM")], [tk("TMP")])
                pool(lambda e: e.tensor_tensor(out=TMP[:, :, 0:T], in0=TMP[:, :, 0:T], in1=bc(pkT1[:, 58:66], 8, T), op=ALU.mult), [tk("TMP")] + PK, [tk("TMP")])
                p, pt = nps(nb)
                for c in range(8):
                    mm(p[:, c * T:(c + 1) * T], blk2[:], TMP[:, c, 0:T], True, True, [tk("blk2"), tk("TMP")], pt)
                pv8 = p[:, 0:8 * T].rearrange("p (c t) -> p c t", t=T)
                dve(lambda e: e.tensor_tensor(out=RKV[:, :, 0:T], in0=pv8, in1=vr, op=ALU.mult), pt + XS, [tk("RKV")])
                return T, smp, ht, htk

            def post(i, T, ht, htk):
                nb = (8 * T + 511) // 512
                p, pt = nps(nb)
                for c in range(8):
                    mm(p[:, c * T:(c + 1) * T], blk2[:], OT[:, c, 0:T], True, True, [tk("blk2"), tk("LW")], pt)
                pv8 = p[:, 0:8 * T].rearrange("p (c t) -> p c t", t=T)
                dve(lambda e: e.scalar_tensor_tensor(out=OT[:, :, 0:T], in0=pv8, scalar=-1.0 / 64.0, in1=OT[:, :, 0:T], op0=ALU.mult, op1=ALU.add),
                    pt + [tk("LW")], [tk("LW")])
                act(lambda e: e.activation(out=TMP[:, :, 0:T], in_=OT[:, :, 0:T], func=AF.Square), [tk("LW")], [tk("TMP")])
                p, pt = nps(nb)
                for c in range(8):
                    mm(p[:, c * T:(c + 1) * T], blk2[:], TMP[:, c, 0:T], True, True, [tk("blk2"), tk("TMP")], pt)
                pv8 = p[:, 0:8 * T].rearrange("p (c t) -> p c t", t=T)
                act(lambda e: e.activation(out=EX[:, :, 0:T], in_=pv8, func=AF.Ln, bias=eps64[:, 0:1], scale=1.0 / 64.0), pt + [tk("eps64")], [tk("EX")])
                act(lambda e: e.activation(out=EX[:, :, 0:T], in_=EX[:, :, 0:T], func=AF.Exp, scale=-0.5), [tk("EX")], [tk("EX")])
                dve(lambda e: e.tensor_tensor(out=OT[:, :, 0:T], in0=OT[:, :, 0:T], in1=EX[:, :, 0:T], op=ALU.mult), [tk("LW"), tk("EX")], [tk("LW")])
                for c in range(8):
                    act(lambda e: e.activation(out=OT[:, c, 0:T], in_=OT[:, c, 0:T], func=AF.Identity, scale=pkT1[:, 66 + c:67 + c], bias=pkT1[:, 74 + c:75 + c]),
                        [tk("LW")] + PK, [tk("LW")])
                dve(lambda e: e.tensor_tensor(out=OT[:, :, 0:T], in0=OT[:, :, 0:T], in1=RKV[:, :, 0:T], op=ALU.add), [tk("LW"), tk("RKV")], [tk("LW")])
                p, pt = nps(nb)
                for c in range(8):
                    mm(p[:, c * T:(c + 1) * T], G2b[:, c * 128:(c + 1) * 128], sgb[:, 0:T], True, True, [tk("G2b"), tk("sgb")], pt)
                pv8 = p[:, 0:8 * T].rearrange("p (c t) -> p c t", t=T)
                dve(lambda e: e.tensor_tensor(out=OT[:, :, 0:T], in0=pv8, in1=OT[:, :, 0:T], op=ALU.mult), pt + [tk("LW")], [tk("LW")])
                p, pt = nps(nb)
                for c in range(8):
                    for k in range(8):
                        mm(p[:, c * T:(c + 1) * T], Wr[:, k, 3328 + c * 128:3328 + (c + 1) * 128], ht[:, k, 0:T], k == 0, k == 7, [WRK, htk], pt)
                pv8 = p[:, 0:8 * T].rearrange("p (c t) -> p c t", t=T)
                act(lambda e: e.activation(out=EX[:, :, 0:T], in_=pv8, func=AF.Sigmoid), pt, [tk("EX")])
                dve(lambda e: e.tensor_tensor(out=OT[:, :, 0:T], in0=OT[:, :, 0:T], in1=EX[:, :, 0:T], op=ALU.mult), [tk("LW"), tk("EX")], [tk("LW")])
                yt_ = yat[i % 2]; ytk = tk("yat", i % 2)
                S.dma("sp", yt_[:, :, 0:T], YA[i].rearrange("p (c t) -> p c t", c=8)[:, :, 0:T], [tk("YA", i)], [ytk])
                dve(lambda e: e.tensor_tensor(out=yt_[:, :, 0:T], in0=yt_[:, :, 0:T], in1=OT[:, :, 0:T], op=ALU.add), [ytk, tk("LW")], [ytk])
                S.dma("sp", YA[i].rearrange("p (c t) -> p c t", c=8)[:, :, 0:T], yt_[:, :, 0:T], [ytk], [tk("YA", i)])

            with ExitStack() as c2p:
                cur[0] = c2p
                AR = sb("AR", [128, 8, 256], BF16); BT = sb("BT", [128, 8, 128], BF16); KT = sb("KT", [128, 8, 128], BF16)
                vb = sb("vb", [128, D], BF16); Bt_tok = sb("Bt_tok", [128, D], BF16); Kt_tok = sb("Kt_tok", [128, D], BF16)
                AM = sb("AM", [128, 4, 512], BF16)
                Nn = [sb("Nn%d" % j, [128, 4, 128], BF16) for j in range(2)]
                Nt = [sb("Nt%d" % j, [128, 4, 128], BF16) for j in range(2)]
                Yy = [sb("Yy%d" % j, [128, 4, 128], BF16) for j in range(2)]
                Xx = [sb("Xx%d" % j, [128, 4, 128], BF16) for j in range(2)]
                AFu = sb("AFu", [128, 4, 128], BF16); AoL = [sb("Ao%d" % j, [128, 4, 128], BF16) for j in range(3)]; AotL = [sb("Aot%d" % j, [128, 4, 128], BF16) for j in range(3)]
                Pb = sb("Pb", [128, 4, 128], BF16); Pb2 = sb("Pb2", [128, 4, 128], BF16)
                MK = [sb("MK%d" % j, [128, 128], BF16) for j in range(4)]
                selb = sb("selb", [8, 3, 128])
                Wb = sb("Wb", [128, 256], BF16); Ub = sb("Ub", [128, 256], BF16)
                Wb1 = sb("Wb1", [128, 256], BF16); Ub1 = sb("Ub1", [128, 256], BF16); Pb21 = sb("Pb21", [128, 4, 128], BF16)

                def carve(t3, n):
                    v = t3.bitcast(BF16).rearrange("p c t -> p (c t)")
                    return [v[:, q * 512:(q + 1) * 512].rearrange("p (j t) -> p j t", j=4) for q in range(n)]
                eta4 = ETA[:].bitcast(BF16).rearrange("p c t -> p (c t)").rearrange("p (j t) -> p j t", j=4)
                kk4 = carve(KK[:], 4); km4 = carve(KM[:], 4); lp4 = carve(LP[:], 4); xs4 = carve(xsb[:, 8:16, :], 4)
                SETS = [
                    {"AM": AM[:], "AFu": AFu[:], "Ao": [t_[:] for t_ in AoL], "Aot": [t_[:] for t_ in AotL], "Nn": [t_[:] for t_ in Nn], "Nt": [t_[:] for t_ in Nt],
                     "Xx": [t_[:] for t_ in Xx], "Yy": [t_[:] for t_ in Yy], "Pb": Pb[:], "Pb2": Pb2[:], "Wb": Wb, "Ub": Ub},
                    {"AM": eta4, "AFu": kk4[0], "Ao": kk4[1:4], "Aot": km4[0:3], "Pb": km4[3], "Nn": lp4[0:2], "Nt": lp4[2:4],
                     "Xx": xs4[0:2], "Yy": xs4[2:4], "Pb2": Pb21[:], "Wb": Wb1, "Ub": Ub1},
                ]
                M = sb("M", [128, 8, 64]); Mb = sb("Mb", [128, 8, 64], BF16); Mt = sb("Mt", [128, 8, 64])
                M4 = sb("M4", [128, 512], BF16); MTm = sb("MTm", [128, 128], BF16)
                DBG = False
                dbg = TMP[:].rearrange("p c t -> p (c t)")
                dbn = [0]

                def dump(ap, ncols, R):
                    if not DBG:
                        return
                    dve(lambda e: e.tensor_copy(out=dbg[:, 0:ncols], in_=ap), R + [tk("TMP")], [tk("TMP")])
                    S.dma("sp", O["yp"][dbn[0] * 128:(dbn[0] + 1) * 128, 0:ncols], dbg[:, 0:ncols], [tk("TMP")], [tk("o_yp")])
                    dbn[0] += 1
                pool(lambda e: e.memset(M[:], 0.0), [], [tk("M")])
                pool(lambda e: e.memset(Mb[:], 0.0), [], [tk("Mb")])
                pool(lambda e: e.memset(M4[:], 1.0), [], [tk("M4")])
                for q in range(4):
                    pool(lambda e: e.affine_select(out=M4[:, q * 128:(q + 1) * 128], in_=M4[:, q * 128:(q + 1) * 128], pattern=[[1, 128]], compare_op=ALU.is_ge,
                                                   fill=0.0, base=(-1 if q % 2 == 0 else 0), channel_multiplier=-1), [tk("M4")], [tk("M4")])
                pool(lambda e: e.memset(MTm[:], 1.0), [], [tk("MTm")])
                pool(lambda e: e.affine_select(out=MTm[:], in_=MTm[:], pattern=[[-1, 128]], compare_op=ALU.is_ge, fill=0.0, base=-1, channel_multiplier=1),
                     [tk("MTm")], [tk("MTm")])
                pool(lambda e: e.memset(selb[:], 1.0), [], [tk("selb")])
                for q, bs in enumerate((16, 32, 64)):
                    nb_ = 128 // bs
                    pool(lambda e: e.affine_select(out=selb[0:nb_, q, :], in_=selb[0:nb_, q, :], pattern=[[1, 128]], compare_op=ALU.is_ge, fill=0.0, base=0,
                                                   channel_multiplier=-bs), [tk("selb")], [tk("selb")])
                    pool(lambda e: e.affine_select(out=selb[0:nb_, q, :], in_=selb[0:nb_, q, :], pattern=[[-1, 128]], compare_op=ALU.is_ge, fill=0.0, base=bs - 1,
                                                   channel_multiplier=bs), [tk("selb")], [tk("selb")])
                p, pt = nps()
                for q, bs in enumerate((16, 32, 64)):
                    nb_ = 128 // bs
                    mm(p[:, q * 128:(q + 1) * 128], selb[0:nb_, q, :], selb[0:nb_, q, :], True, True, [tk("selb")], pt)
                copy("dve", MK[0][:], p[:, 0:128], pt, [tk("MK")])
                copy("dve", MK[1][:], p[:, 128:256], pt, [tk("MK")])
                dve(lambda e: e.tensor_tensor(out=MK[2][:], in0=p[:, 256:384], in1=MK[1][:], op=ALU.subtract), pt + [tk("MK")], [tk("MK")])
                dve(lambda e: e.tensor_tensor(out=MK[1][:], in0=MK[1][:], in1=MK[0][:], op=ALU.subtract), [tk("MK")], [tk("MK")])
                dve(lambda e: e.tensor_scalar(out=MK[3][:], in0=p[:, 256:384], scalar1=-1.0, scalar2=1.0, op0=ALU.mult, op1=ALU.add), pt, [tk("MK")])
                for i in range(16):
                    T, smp, ht, htk = front(i)
                    STEP = 9.0
                    if STEP < 2:
                        continue
                    for c in range(8):
                        dve(lambda e: e.tensor_tensor_scan(out=LP[:, c, :], data0=ones128[:], data1=LW[:, c, :], initial=0.0, op0=ALU.mult, op1=ALU.add),
                            [tk("LW"), tk("ones128")], [tk("LP")])
                    pool(lambda e: e.tensor_tensor(out=TMP[:], in0=LP[:], in1=LW[:], op=ALU.subtract), [tk("LP"), tk("LW")], [tk("TMP")])
                    act(lambda e: e.activation(out=EX[:], in_=TMP[:], func=AF.Exp, scale=LWC), [tk("TMP")], [tk("EX")])
                    dve(lambda e: e.scalar_tensor_tensor(out=AR[:, :, 0:128], in0=KK[:], scalar=-1.0, in1=EX[:], op0=ALU.mult, op1=ALU.mult),
                        [tk("KK"), tk("EX")], [tk("AR")])
                    act(lambda e: e.activation(out=EX[:], in_=LP[:], func=AF.Exp, scale=-LWC), [tk("LP")], [tk("EX")])
                    pool(lambda e: e.tensor_tensor(out=TMP[:], in0=KK[:], in1=ETA[:], op=ALU.mult), [tk("KK"), tk("ETA")], [tk("TMP")])
                    dve(lambda e: e.tensor_tensor(out=BT[:], in0=TMP[:], in1=EX[:], op=ALU.mult), [tk("TMP"), tk("EX")], [tk("BT")])
                    dve(lambda e: e.tensor_tensor(out=KT[:], in0=KM[:], in1=EX[:], op=ALU.mult), [tk("KM"), tk("EX")], [tk("KT")])
                    act(lambda e: e.activation(out=EX[:], in_=LP[:], func=AF.Exp, scale=LWC), [tk("LP")], [tk("EX")])
                    dve(lambda e: e.tensor_tensor(out=AR[:, :, 128:256], in0=xsb[:, 0:8, :], in1=EX[:], op=ALU.mult), [tk("xsb"), tk("EX")], [tk("AR")])
                    pool(lambda e: e.tensor_copy(out=eplast[:], in_=EX[:, :, 127:128]), [tk("EX")], [tk("eplast")])
                    if i == 0:
                        for blk in range(1):
                            dump(xsb[:, blk * 8:(blk + 1) * 8, :].rearrange("p c t -> p (c t)"), 1024, [tk("xsb")])
                        dump(KK[:].rearrange("p c t -> p (c t)"), 1024, [tk("KK")])
                        dump(KM[:].rearrange("p c t -> p (c t)"), 1024, [tk("KM")])
                        dump(LP[:].rearrange("p c t -> p (c t)"), 1024, [tk("LP")])
                        dump(AR[:, 0:4, :].rearrange("p c t -> p (c t)"), 1024, [tk("AR")])
                        dump(BT[:].rearrange("p c t -> p (c t)"), 1024, [tk("BT")])
                        dump(KT[:].rearrange("p c t -> p (c t)"), 1024, [tk("KT")])
                    p, pt = nps(2)
                    for c in range(8):
                        tr(p[:, c * 128:(c + 1) * 128], xsb[:, 16 + c, :], ident[:], [tk("xsb")] + CI, [pt[c // 4]])
                    copy("act", vb[:], p[:, 0:1024], pt, [tk("vb")])
                    for (src, srck, dstb, dstk) in ((BT, tk("BT"), Bt_tok, tk("Bt_tok")), (KT, tk("KT"), Kt_tok, tk("Kt_tok"))):
                        p, pt = nps()
                        pb = p.bitcast(BF16)
                        for c in range(8):
                            tr(pb[:, c * 128:(c + 1) * 128], src[:, c, :], identb[:], [srck] + CIB, pt)
                        copy("dve", dstb[:], pb[:, 0:1024], pt, [dstk])
                    if STEP < 3:
                        continue
                    barrier()
                    mkb = lambda l: MK[l][:].unsqueeze(1).to_broadcast([128, 4, 128])
                    idb4 = identb[:].unsqueeze(1).to_broadcast([128, 4, 128])
                    fl = lambda t_: t_.rearrange("p j t -> p (j t)")
                    for gp in range(2):
                        GS = []
                        for s_ in range(2):
                            g4 = 2 * gp + s_
                            hs = [(4 * g4, 2 * g4, 0), (4 * g4 + 2, 2 * g4 + 1, 0), (4 * g4 + 1, 2 * g4, 1), (4 * g4 + 3, 2 * g4 + 1, 1)]
                            GS.append((s_, g4, hs, SETS[s_]))
                        for (s_, g4, hs, B) in GS:
                            pNa, pNat = nps(); pNb, pNbt = nps()
                            for j, (h, c, hf) in enumerate(hs):
                                ps_ = slice(64 * hf, 64 * hf + 64)
                                pa, pat = nps()
                                mm(pa[:, 0:256], BT[ps_, c, :], AR[ps_, c, :], True, True, [tk("BT"), tk("AR")], pat)
                                mm(pa[:, 256:512], KT[ps_, c, :], AR[ps_, c, :], True, True, [tk("KT"), tk("AR")], pat)
                                dve(lambda e: e.tensor_tensor(out=B["AM"][:, j, :], in0=pa[:, 0:512], in1=M4[:], op=ALU.mult), pat + [tk("M4")], [tk("AM", s_, j)])
                                pN, pNt = (pNa, pNat) if hf == 0 else (pNb, pNbt)
                                mm(pN[:, (j % 2) * 128:(j % 2 + 1) * 128], AR[ps_, c, 0:128], BT[ps_, c, :], True, True, [tk("BT"), tk("AR")], pNt)
                            for hf_, (pNx, pNxt) in enumerate(((pNa, pNat), (pNb, pNbt))):
                                dve(lambda e: e.tensor_tensor(out=B["AFu"][:, 2 * hf_:2 * hf_ + 2, :], in0=pNx[:, 0:256].rearrange("p (j t) -> p j t", j=2),
                                                              in1=MTm[:].unsqueeze(1).to_broadcast([128, 2, 128]), op=ALU.mult), pNxt + [tk("MTm")], [tk("AF", s_)])
                        for (s_, g4, hs, B) in GS:
                            AMK = [tk("AM", s_, j) for j in range(4)]
                            AtF = B["AM"][:, :, 0:128]
                            pool(lambda e: e.tensor_tensor(out=B["Nn"][0], in0=B["AFu"], in1=mkb(0), op=ALU.mult), [tk("AF", s_), tk("MK")], [tk("Nn", s_, 0)])
                            dve(lambda e: e.tensor_tensor(out=B["Nt"][0], in0=AtF, in1=mkb(0), op=ALU.mult), AMK + [tk("MK")], [tk("Nt", s_, 0)])
                            pool(lambda e: e.tensor_tensor(out=B["Xx"][0], in0=B["Nn"][0], in1=idb4, op=ALU.add), [tk("Nn", s_, 0)] + CIB, [tk("Xx", s_, 0)])
                            dve(lambda e: e.tensor_tensor(out=B["Yy"][0], in0=B["Nt"][0], in1=idb4, op=ALU.add), [tk("Nt", s_, 0)] + CIB, [tk("Yy", s_, 0)])
                        for (s_, g4, hs, B) in GS:
                            AMK = [tk("AM", s_, j) for j in range(4)]
                            AtF = B["AM"][:, :, 0:128]
                            for l in range(1, 4):
                                pool(lambda e: e.tensor_tensor(out=B["Ao"][l - 1], in0=B["AFu"], in1=mkb(l), op=ALU.mult), [tk("AF", s_), tk("MK")], [tk("Ao", s_, l)])
                                pool(lambda e: e.tensor_tensor(out=B["Aot"][l - 1], in0=AtF, in1=mkb(l), op=ALU.mult), AMK + [tk("MK")], [tk("Aot", s_, l)])
                        for r in range(3):
                            a = r % 2; b = (r + 1) % 2
                            for (s_, g4, hs, B) in GS:
                                pn, pnt = nps()
                                for j in range(4):
                                    mm(pn[:, j * 128:(j + 1) * 128], B["Nt"][a][:, j, :], B["Nn"][a][:, j, :], True, True, [tk("Nt", s_, a), tk("Nn", s_, a)], pnt)
                                pq, pqt = nps()
                                for j in range(4):
                                    mm(pq[:, j * 128:(j + 1) * 128], B["Nn"][a][:, j, :], B["Nt"][a][:, j, :], True, True, [tk("Nt", s_, a), tk("Nn", s_, a)], pqt)
                                copy("act", fl(B["Nn"][b]), pn[:, 0:512], pnt, [tk("Nn", s_, b)])
                                copy("dve", fl(B["Nt"][b]), pq[:, 0:512], pqt, [tk("Nt", s_, b)])
                            for (s_, g4, hs, B) in GS:
                                px, pxt = nps()
                                for j in range(4):
                                    mm(px[:, j * 128:(j + 1) * 128], identb[:], B["Xx"][a][:, j, :], True, False, CIB + [tk("Xx", s_, a)], pxt)
                                    mm(px[:, j * 128:(j + 1) * 128], B["Nt"][b][:, j, :], B["Xx"][a][:, j, :], False, True, [tk("Nt", s_, b), tk("Xx", s_, a)], pxt)
                                py, pyt = nps()
                                for j in range(4):
                                    mm(py[:, j * 128:(j + 1) * 128], identb[:], B["Yy"][a][:, j, :], True, False, CIB + [tk("Yy", s_, a)], pyt)
                                    mm(py[:, j * 128:(j + 1) * 128], B["Nn"][b][:, j, :], B["Yy"][a][:, j, :], False, True, [tk("Nn", s_, b), tk("Yy", s_, a)], pyt)
                                copy("act", fl(B["Xx"][b]), px[:, 0:512], pxt, [tk("Xx", s_, b)])
                                copy("dve", fl(B["Yy"][b]), py[:, 0:512], pyt, [tk("Yy", s_, b)])
                        cu = 1
                        for l in range(1, 4):
                            if l < 3:
                                for (s_, g4, hs, B) in GS:
                                    pp, ppt = nps()
                                    for j in range(4):
                                        mm(pp[:, j * 128:(j + 1) * 128], B["Aot"][l - 1][:, j, :], B["Xx"][cu][:, j, :], True, True, [tk("Aot", s_, l), tk("Xx", s_, cu)], ppt)
                                    copy("act", fl(B["Pb"]), pp[:, 0:512], ppt, [tk("Pb", s_)])
                            for (s_, g4, hs, B) in GS:
                                pp2, pp2t = nps()
                                for j in range(4):
                                    mm(pp2[:, j * 128:(j + 1) * 128], B["Ao"][l - 1][:, j, :], B["Yy"][cu][:, j, :], True, True, [tk("Ao", s_, l), tk("Yy", s_, cu)], pp2t)
                                copy("dve", fl(B["Pb2"]), pp2[:, 0:512], pp2t, [tk("Pb2", s_)])
                            if l < 3:
                                for (s_, g4, hs, B) in GS:
                                    px, pxt = nps()
                                    for j in range(4):
                                        mm(px[:, j * 128:(j + 1) * 128], identb[:], B["Xx"][cu][:, j, :], True, False, CIB + [tk("Xx", s_, cu)], pxt)
                                        mm(px[:, j * 128:(j + 1) * 128], B["Yy"][cu][:, j, :], B["Pb"][:, j, :], False, True, [tk("Yy", s_, cu), tk("Pb", s_)], pxt)
                                    copy("act", fl(B["Xx"][1 - cu]), px[:, 0:512], pxt, [tk("Xx", s_, 1 - cu)])
                            for (s_, g4, hs, B) in GS:
                                py, pyt = nps()
                                for j in range(4):
                                    mm(py[:, j * 128:(j + 1) * 128], identb[:], B["Yy"][cu][:, j, :], True, False, CIB + [tk("Yy", s_, cu)], pyt)
                                    mm(py[:, j * 128:(j + 1) * 128], B["Xx"][cu][:, j, :], B["Pb2"][:, j, :], False, True, [tk("Xx", s_, cu), tk("Pb2", s_)], pyt)
                                copy("dve", fl(B["Yy"][1 - cu]), py[:, 0:512], pyt, [tk("Yy", s_, 1 - cu)])
                            cu = 1 - cu
                        for (s_, g4, hs, B) in GS:
                            pwa, pwat = nps(); pwb, pwbt = nps()
                            for j, (h, c, hf) in enumerate(hs):
                                ps_ = slice(64 * hf, 64 * hf + 64)
                                pw, pwt = (pwa, pwat) if hf == 0 else (pwb, pwbt)
                                mm(pw[:, (j % 2) * 64:(j % 2 + 1) * 64], AR[ps_, c, 0:128], Mb[ps_, c, :], True, False, [tk("AR"), tk("Mb")], pwt)
                                mm(pw[:, (j % 2) * 64:(j % 2 + 1) * 64], B["AM"][:, j, 256:384], vb[:, h * 64:(h + 1) * 64], False, True, [tk("AM", s_, j), tk("vb")], pwt)
                            copy("act", B["Wb"][:, 0:128], pwa[:, 0:128], pwat, [tk("Wb", s_)])
                            copy("act", B["Wb"][:, 128:256], pwb[:, 0:128], pwbt, [tk("Wb", s_)])
                        for (s_, g4, hs, B) in GS:
                            pu, put = nps()
                            for j in range(4):
                                mm(pu[:, j * 64:(j + 1) * 64], B["Yy"][cu][:, j, :], B["Wb"][:, j * 64:(j + 1) * 64], True, True, [tk("Yy", s_, cu), tk("Wb", s_)], put)
                            copy("dve", B["Ub"][:, :], pu[:, 0:256], put, [tk("Ub", s_)])
                        for (s_, g4, hs, B) in GS:
                            pOa, pOat = nps(); pOb, pObt = nps()
                            for j, (h, c, hf) in enumerate(hs):
                                ps_ = slice(64 * hf, 64 * hf + 64)
                                pO, pOt = (pOa, pOat) if hf == 0 else (pOb, pObt)
                                o_ = pO[ps_, (j % 2) * 128:(j % 2 + 1) * 128]
                                mm(o_, Mb[ps_, c, :], AR[ps_, c, 128:256], True, False, [tk("AR"), tk("Mb")], pOt)
                                mm(o_, B["Ub"][:, j * 64:(j + 1) * 64], B["AM"][:, j, 128:256], False, False, [tk("Ub", s_), tk("AM", s_, j)], pOt)
                                mm(o_, vb[:, h * 64:(h + 1) * 64], B["AM"][:, j, 384:512], False, True, [tk("vb"), tk("AM", s_, j)], pOt)
                                PM, PMt = (P7, P7t) if hf == 0 else (P6, P6t)
                                m_ = PM[ps_, c * 64:(c + 1) * 64]
                                mm(m_, Bt_tok[:, h * 64:(h + 1) * 64], B["Ub"][:, j * 64:(j + 1) * 64], True, False, [tk("Bt_tok"), tk("Ub", s_)], PMt)
                                mm(m_, Kt_tok[:, h * 64:(h + 1) * 64], vb[:, h * 64:(h + 1) * 64], False, True, [tk("Kt_tok"), tk("vb")], PMt)
                            copy("act", OT[0:64, 2 * g4:2 * g4 + 2, :], pOa[0:64, 0:256].rearrange("p (c t) -> p c t", t=128), pOat, [tk("LW")])
                            copy("act", OT[64:128, 2 * g4:2 * g4 + 2, :], pOb[64:128, 0:256].rearrange("p (c t) -> p c t", t=128), pObt, [tk("LW")])
                    barrier()
                    dve(lambda e: e.tensor_tensor(out=Mt[0:64], in0=M[0:64], in1=P7[0:64, 0:512].rearrange("p (c v) -> p c v", v=64), op=ALU.add), [tk("M")] + P7t, [tk("Mt")])
                    dve(lambda e: e.tensor_tensor(out=Mt[64:128], in0=M[64:128], in1=P6[64:128, 0:512].rearrange("p (c v) -> p c v", v=64), op=ALU.add), [tk("M")] + P6t, [tk("Mt")])
                    dve(lambda e: e.tensor_tensor(out=M[:], in0=Mt[:], in1=eplast[:].to_broadcast([128, 8, 64]), op=ALU.mult), [tk("Mt"), tk("eplast")], [tk("M")])
                    copy("act", Mb[:], M[:], [tk("M")], [tk("Mb")])
                    if i == 0:
                        dump(OT[:].rearrange("p c t -> p (c t)"), 1024, [tk("LW")])
                        dump(M[:].rearrange("p c v -> p (c v)"), 512, [tk("M")])
                    if STEP < 5:
                        continue
                    post(i, T, ht, htk)
                p, pt = nps(2)
                for c in range(8):
                    tr(p[0:64, c * 128:(c + 1) * 128], M[:, c, :], ident[:], [tk("M")] + CI, [pt[c // 4]])
                so = stg[0]; sok = tk("stg", 0)
                copy("act", so[0:64, 0:D], p[0:64, 0:1024], pt, [sok])
                S.dma("sp", O["S_p"].rearrange("(h v) k -> v h k", v=64), so[0:64, 0:D].rearrange("v (h k) -> v h k", k=64), [sok], [tk("o_S_p")])
                barrier()
            with ExitStack() as c2s:
                cur[0] = c2s
                if True:
                    T, smp, ht, htk = front(16)
                    act(lambda e: e.activation(out=EX[:, :, 0:T], in_=LW[:, :, 0:T], func=AF.Exp, scale=LWC), [tk("LW")], [tk("EX")])
                    dve(lambda e: e.scalar_tensor_tensor(out=TMP[:, :, 0:T], in0=KK[:, :, 0:T], scalar=-1.0, in1=ETA[:, :, 0:T], op0=ALU.mult, op1=ALU.mult),
                        [tk("KK"), tk("ETA")], [tk("TMP")])
                    arrs = [(xsb, 0, tk("xsb")), (EX, 0, tk("EX")), (KM, 0, tk("KM")), (xsb, 16, tk("xsb")), (KK, 0, tk("KK")), (TMP, 0, tk("TMP"))]
                    for a, (src, c0, srck) in enumerate(arrs):
                        p, pt = nps(2)
                        for c in range(8):
                            tr(p[0:NST, c * 128:(c + 1) * 128], src[:, c0 + c, 0:NST], ident[:], [srck] + CI, [pt[c // 4]])
                        sg_ = stg[a % 3]; sgk = tk("stg", a % 3)
                        copy(evac_eng(), sg_[0:NST, 0:D], p[0:NST, 0:1024], pt, [sgk])
                        S.dma("sp", SCR6[a], sg_[0:NST, 0:D], [sgk], [tk("SCR6")])
                    Sst = sb("Sst", [128, 4096]); Stm = sb("Stm", [128, 4096]); PR = sb("PR", [128, 6, TS, 64]); osm = sb("osm", [128, TS, 64])
                    sa = sb("sa", [128, 64])
                    Sv = Sst[:].rearrange("p (v k) -> p v k", k=64); Tv = Stm[:].rearrange("p (v k) -> p v k", k=64)
                    scr6v = SCR6.rearrange("a (s t) (h k) -> s h a t k", t=TS, k=64)
                    scrov = SCRO.rearrange("(s t) (h v) -> s h t v", t=TS, v=64)
                    for g2 in range(2):
                        S.dma("sp", Sst[:], I["stS"][g2 * 8192:(g2 + 1) * 8192, :].rearrange("(p v) k -> p (v k)", v=64), [], [tk("Sst")])
                        for sl_ in range(8):
                            for a in range(6):
                                S.dma("sp", PR[sl_ * 16:(sl_ + 1) * 16, a], scr6v[g2 * 8 + sl_][:, a], [tk("SCR6")], [tk("PR")])
                        vec_k = lambda a, t: PR[:, a, t, :].unsqueeze(1).to_broadcast([128, 64, 64])
                        RS_ = [tk("Sst"), tk("PR")]
                        for t in range(TS):
                            dve(lambda e: e.tensor_tensor(out=Tv, in0=Sv, in1=vec_k(4, t), op=ALU.mult), RS_, [tk("Stm")])
                            dve(lambda e: e.tensor_reduce(out=sa[:], in_=Tv, axis=AX.X, op=ALU.add), [tk("Stm")], [tk("sa")])
                            dve(lambda e: e.tensor_tensor(out=Sv, in0=Sv, in1=vec_k(1, t), op=ALU.mult), RS_, [tk("Sst")])
                            dve(lambda e: e.tensor_tensor(out=Tv, in0=sa[:].unsqueeze(2).to_broadcast([128, 64, 64]), in1=vec_k(5, t), op=ALU.mult),
                                [tk("sa"), tk("PR")], [tk("Stm")])
                            dve(lambda e: e.tensor_tensor(out=Sv, in0=Sv, in1=Tv, op=ALU.add), [tk("Sst"), tk("Stm")], [tk("Sst")])
                            dve(lambda e: e.tensor_tensor(out=Tv, in0=PR[:, 3, t, :].unsqueeze(2).to_broadcast([128, 64, 64]), in1=vec_k(2, t), op=ALU.mult),
                                [tk("PR")], [tk("Stm")])
                            dve(lambda e: e.tensor_tensor(out=Sv, in0=Sv, in1=Tv, op=ALU.add), [tk("Sst"), tk("Stm")], [tk("Sst")])
                            dve(lambda e: e.tensor_tensor(out=Tv, in0=Sv, in1=vec_k(0, t), op=ALU.mult), RS_, [tk("Stm")])
                            dve(lambda e: e.tensor_reduce(out=osm[:, t, :], in_=Tv, axis=AX.X, op=ALU.add), [tk("Stm")], [tk("osm")])
                        S.dma("sp", O["S_s"][g2 * 8192:(g2 + 1) * 8192, :].rearrange("(p v) k -> p (v k)", v=64), Sst[:], [tk("Sst")], [tk("o_S_s")])
                        for sl_ in range(8):
                            S.dma("sp", scrov[g2 * 8 + sl_], osm[sl_ * 16:(sl_ + 1) * 16], [tk("osm")], [tk("SCRO")])
                    ot = stg[0]; otk = tk("stg", 0)
                    S.dma("sp", ot[0:NST, 0:D], SCRO, [tk("SCRO")], [otk])
                    p, pt = nps()
                    for c in range(8):
                        tr(p[:, c * 64:(c + 1) * 64], ot[0:NST, c * 128:(c + 1) * 128], ident[0:NST, 0:NST], [otk] + CI, pt)
                    copy("act", OT[:, :, 0:NST], p[:, 0:512].rearrange("p (c t) -> p c t", t=NST), pt, [tk("LW")])
                    post(16, T, ht, htk)
                    barrier()
        cur[0] = ctx
        with ExitStack() as c3:
            cur[0] = c3
            Wo = sb("Wo", [128, 8, D], BF16); Wu = sb("Wu", [128, 8, 4096], BF16); Wd = sb("Wd", [128, 32, D], BF16)
            load_weights(Wo, tk("Wo"), [(0, 1024, 0)], I["w_out"])
            load_weights(Wu, tk("Wu"), [(0, 1540, 0), (1540, 3080, 1540), (3080, 4096, 3080)], I["w_up"])
            load_weights(Wd, tk("Wd"), [(0, 1024, 0)], I["w_down"])
            onesm = sb("onesm", [128, 128])
            pool(lambda e: e.memset(onesm[:], 1.0 / 1024.0), [], [tk("onesm")])
            BA = sb("BA", [128, 8, 128]); BB = sb("BB", [128, 8, 128]); BC = sb("BC", [128, 8, 128])
            h2T = sb("h2T", [128, 8, 128], BF16); aT = sb("aT", [128, 32, 128], BF16)
            rs_ = sb("rs_", [128, 128])
            PK = [tk("pkT1")]

            def bc3(ap, T):
                return ap.unsqueeze(2).to_broadcast([128, 8, T])

            def modbc(c0, T, smp):
                if not smp:
                    return modT[:, c0:c0 + 8, 0:1].to_broadcast([128, 8, T])
                return modT[:, c0:c0 + 8, 1:17].unsqueeze(3).to_broadcast([128, 8, NS, TS])

            def v4(ap, T, smp):
                return ap[:, :, 0:T] if not smp else ap[:, :, 0:T].rearrange("p c (s t) -> p c s t", t=TS)

            def ln_fm(Z, ZK, SQ, SQK, gcol, T):
                p, pt = nps()
                for c in range(8):
                    mm(p[:, 0:T], onesm[:], Z[:, c, 0:T], c == 0, c == 7, [tk("onesm"), ZK], pt)
                dve(lambda e: e.tensor_tensor(out=Z[:, :, 0:T], in0=Z[:, :, 0:T], in1=p[:, 0:T].unsqueeze(1).to_broadcast([128, 8, T]), op=ALU.subtract),
                    [ZK] + pt, [ZK])
                act(lambda e: e.activation(out=SQ[:, :, 0:T], in_=Z[:, :, 0:T], func=AF.Square), [ZK], [SQK])
                p, pt = nps()
                for c in range(8):
                    mm(p[:, 0:T], onesm[:], SQ[:, c, 0:T], c == 0, c == 7, [tk("onesm"), SQK], pt)
                dve(lambda e: e.tensor_scalar(out=rs_[:, 0:T], in0=p[:, 0:T], scalar1=LN_EPS, scalar2=None, op0=ALU.add), pt, [tk("rs_")])
                act(lambda e: e.activation(out=rs_[:, 0:T], in_=rs_[:, 0:T], func=AF.Sqrt), [tk("rs_")], [tk("rs_")])
                dve(lambda e: e.reciprocal(out=rs_[:, 0:T], in_=rs_[:, 0:T]), [tk("rs_")], [tk("rs_")])
                dve(lambda e: e.tensor_tensor(out=Z[:, :, 0:T], in0=Z[:, :, 0:T], in1=rs_[:, 0:T].unsqueeze(1).to_broadcast([128, 8, T]), op=ALU.mult),
                    [ZK, tk("rs_")], [ZK])
                for c in range(8):
                    act(lambda e: e.activation(out=Z[:, c, 0:T], in_=Z[:, c, 0:T], func=AF.Identity, scale=pkT1[:, gcol + c:gcol + c + 1],
                                               bias=pkT1[:, gcol + 8 + c:gcol + 9 + c]), [ZK] + PK, [ZK])

            AK, BK, CK = tk("BA"), tk("BB"), tk("BC")
            for i in range(17):
                T = 128 if i < 16 else NST
                smp = i == 16
                nb = (8 * T + 511) // 512
                def loads(i_):
                    T_ = 128 if i_ < 16 else NST
                    x_ = stg[(2 * i_) % 3]; xk_ = tk("stg", (2 * i_) % 3)
                    if i_ < 16:
                        S.dma("sp", x_[:, 0:D], I["xp"][i_ * 128:(i_ + 1) * 128, :], [], [xk_])
                    else:
                        S.dma("sp", x_[0:NST, 0:D], I["xs"], [], [xk_])
                    ut_ = yat[i_ % 2]; utk_ = tk("yat", i_ % 2)
                    S.dma("sp", ut_[:, :, 0:T_], YA[i_].rearrange("p (c t) -> p c t", c=8)[:, :, 0:T_], [tk("YA", i_)], [utk_])
                if i == 0:
                    loads(0)
                x = stg[(2 * i) % 3]; xk = tk("stg", (2 * i) % 3)
                p, pt = nps(nb)
                for c in range(8):
                    tr(p[:, c * T:(c + 1) * T], x[0:T, c * 128:(c + 1) * 128], ident[0:T, 0:T], [xk] + CI, [pt[(c * T) // 512]])
                copy("act", BA[:, :, 0:T], p[:, 0:8 * T].rearrange("p (c t) -> p c t", t=T), pt, [AK])
                ut = yat[i % 2]; utk = tk("yat", i % 2)
                p, pt = nps(nb)
                for c in range(8):
                    for k in range(8):
                        mm(p[:, c * T:(c + 1) * T], Wo[:, k, c * 128:(c + 1) * 128], ut[:, k, 0:T], k == 0, k == 7, [tk("Wo"), utk], [pt[(c * T) // 512]])
                pv8 = p[:, 0:8 * T].rearrange("p (c t) -> p c t", t=T)
                pv8 = pv8 if not smp else pv8.rearrange("p c (s t) -> p c s t", t=TS)
                dve(lambda e: e.tensor_tensor(out=v4(BC, T, smp), in0=pv8, in1=modbc(16, T, smp), op=ALU.mult), pt + MT, [CK])
                dve(lambda e: e.scalar_tensor_tensor(out=BB[:, :, 0:T], in0=BA[:, :, 0:T], scalar=ALPHA, in1=BC[:, :, 0:T], op0=ALU.mult, op1=ALU.add),
                    [AK, CK], [BK])
                if i + 1 < 17:
                    loads(i + 1)
                ln_fm(BB, BK, BA, AK, 82, T)
                pool(lambda e: e.tensor_tensor(out=v4(BC, T, smp), in0=v4(BB, T, smp), in1=modbc(32, T, smp), op=ALU.mult), [BK] + MT, [CK])
                dve(lambda e: e.tensor_tensor(out=v4(h2T, T, smp), in0=v4(BC, T, smp), in1=modbc(24, T, smp), op=ALU.add), [CK] + MT, [tk("h2T")])
                for fg in range(8):
                    p, pt = nps()
                    for f4 in range(4):
                        f = fg * 4 + f4
                        for k in range(8):
                            mm(p[:, f4 * T:(f4 + 1) * T], Wu[:, k, f * 128:(f + 1) * 128], h2T[:, k, 0:T], k == 0, k == 7, [tk("Wu"), tk("h2T")], pt)
                    act(lambda e: e.activation(out=BC[:, 4 * (fg % 2):4 * (fg % 2) + 4, 0:T], in_=p[:, 0:4 * T].rearrange("p (c t) -> p c t", t=T), func=AF.Relu),
                        pt + ([CK] if fg < 2 else []), [tk("BCr", fg % 2)])
                    dve(lambda e: e.scalar_tensor_tensor(out=aT[:, fg * 4:(fg + 1) * 4, 0:T], in0=p[:, 0:4 * T].rearrange("p (c t) -> p c t", t=T), scalar=0.0,
                                                         in1=BC[:, 4 * (fg % 2):4 * (fg % 2) + 4, 0:T], op0=ALU.max, op1=ALU.mult), pt + [tk("BCr", fg % 2)], [tk("aT")])
                p, pt = nps(nb)
                for c in range(8):
                    for f in range(32):
                        mm(p[:, c * T:(c + 1) * T], Wd[:, f, c * 128:(c + 1) * 128], aT[:, f, 0:T], f == 0, f == 31, [tk("Wd"), tk("aT")], [pt[(c * T) // 512]])
                pv8 = p[:, 0:8 * T].rearrange("p (c t) -> p c t", t=T)
                pv8 = pv8 if not smp else pv8.rearrange("p c (s t) -> p c s t", t=TS)
                dve(lambda e: e.tensor_tensor(out=v4(BC, T, smp), in0=pv8, in1=modbc(40, T, smp), op=ALU.mult), pt + MT, [CK, tk("BCr", 0), tk("BCr", 1)])
                dve(lambda e: e.scalar_tensor_tensor(out=BA[:, :, 0:T], in0=BB[:, :, 0:T], scalar=ALPHA, in1=BC[:, :, 0:T], op0=ALU.mult, op1=ALU.add),
                    [BK, CK], [AK])
                ln_fm(BA, AK, BC, CK, 98, T)
                p, pt = nps(2)
                for c in range(8):
                    tr(p[0:T, c * 128:(c + 1) * 128], BA[:, c, 0:T], ident[:], [AK] + CI, [pt[c // 4]])
                ot = stg[(2 * i + 1) % 3]; otk = tk("stg", (2 * i + 1) % 3)
                copy("act", ot[0:T, 0:D], p[0:T, 0:1024], pt, [otk])
                if not smp:
                    S.dma("pool", O["yp"][i * 128:(i + 1) * 128, :], ot[0:T, 0:D], [otk], [tk("o_yp")])
                else:
                    S.dma("sp", O["ys"], ot[0:T, 0:D], [otk], [tk("o_ys")])
            barrier()
        cur[0] = ctx
        S.finish()
        print("n_ins", S.n_ins, "n_wait", S.n_wait, S.cnt, {q: sum(st["val"]) // 16 for q, st in S.dq.items()})
    return nc


_NC = [None]


def _prep_inputs(inp):
    f = lambda a: np.ascontiguousarray(a, dtype=np.float32)
    w_in = f(inp["w_in"][0])
    pk0 = np.concatenate([inp["b_cond"][0].reshape(48, 128), inp["conv_w"][0].reshape(64, 128), inp["conv_b"][0].reshape(16, 128)], 0)
    pk1 = np.concatenate([inp["rwkv_mu"][0].reshape(26, 128), inp["rwkv_w0"][0].reshape(8, 128), inp["rwkv_a0"][0].reshape(8, 128),
                          inp["rwkv_k_k"][0].reshape(8, 128), inp["rwkv_k_a"][0].reshape(8, 128), inp["rwkv_r_k"][0].reshape(8, 128),
                          inp["rwkv_lnx_w"][0].reshape(8, 128), inp["rwkv_lnx_b"][0].reshape(8, 128), inp["ln1_g"][0].reshape(8, 128),
                          inp["ln1_b"][0].reshape(8, 128), inp["ln2_g"][0].reshape(8, 128), inp["ln2_b"][0].reshape(8, 128)], 0)
    ifb = np.stack([inp["mlstm_i_bias"][0], inp["mlstm_f_bias"][0]], 1)
    vecs = np.stack([inp["mlstm_norm_w"][0], inp["rwkv_lnx_w"][0], inp["rwkv_lnx_b"][0], inp["ln1_g"][0], inp["ln1_b"][0],
                     inp["ln2_g"][0], inp["ln2_b"][0]], 0)
    shared = {
        "w_cond": f(inp["w_cond"][0]), "w_in": w_in, "w_out": f(inp["w_out"][0]), "w_up": f(inp["w_up"][0]),
        "w_down": f(inp["w_down"][0]), "pk0": f(pk0), "pk1": f(pk1), "ifb": f(ifb), "vecs": f(vecs),
        "w2": f(inp["rwkv_w2"][0]), "a2": f(inp["rwkv_a2"][0]), "g2": f(inp["rwkv_g2"][0]),
    }
    maps = []
    for c in range(NCORES):
        sl = slice(c * NS, (c + 1) * NS)
        m = dict(shared)
        m["xp"] = f(inp["x_prompt"][c])
        m["xs"] = f(inp["x_sample"][sl].reshape(NST, D))
        m["cc"] = f(np.concatenate([inp["c_prompt"][c:c + 1], inp["c_sample"][sl]], 0))
        m["stC"] = f(inp["state_mlstm_C"][0, sl].reshape(NS * 4 * 256, 256))
        m["stn"] = f(inp["state_mlstm_n"][0, sl].reshape(NS * 4, 256))
        m["stm"] = f(inp["state_mlstm_m"][0, sl])
        m["stconv"] = f(inp["state_mlstm_conv"][0, sl].reshape(NS * 3, 2048))
        m["stS"] = f(inp["state_rwkv_S"][0, sl].reshape(NS * 16 * 64, 64))
        m["stshift"] = f(inp["state_rwkv_shift"][0, sl])
        maps.append(m)
    return maps


def kernel(**inp):
    inp = {k: np.asarray(v) for k, v in inp.items()}
    maps = _prep_inputs(inp)
    if _NC[0] is None:
        _NC[0] = build()
    res = run_bass_kernel_spmd(_NC[0], maps, core_ids=list(range(NCORES)))
    R = res.results
    g = lambda n: [np.asarray(r[n], dtype=np.float32) for r in R]
    yp = np.stack(g("yp"), 0)
    ys = np.concatenate(g("ys"), 0).reshape(128, TS, D)
    C_p = np.stack(g("C_p"), 0).reshape(1, 8, 4, 256, 256)
    n_p = np.stack(g("n_p"), 0).reshape(1, 8, 4, 256)
    m_p = np.stack(g("m_p"), 0).reshape(1, 8, 4)
    conv_p = np.stack(g("conv_p"), 0).reshape(1, 8, 3, 2048)
    S_p = np.stack(g("S_p"), 0).reshape(1, 8, 16, 64, 64)
    shift_p = np.stack(g("shift_p"), 0).reshape(1, 8, D)
    C_s = np.concatenate(g("C_s"), 0).reshape(1, 128, 4, 256, 256)
    n_s = np.concatenate(g("n_s"), 0).reshape(1, 128, 4, 256)
    m_s = np.concatenate(g("m_s"), 0).reshape(1, 128, 4)
    conv_s = np.concatenate(g("conv_s"), 0).reshape(1, 128, 3, 2048)
    S_s = np.concatenate(g("S_s"), 0).reshape(1, 128, 16, 64, 64)
    shift_s = np.concatenate(g("shift_s"), 0).reshape(1, 128, D)
    return (yp, ys, C_p, n_p, m_p, conv_p, S_p, shift_p, C_s, n_s, m_s, conv_s, S_s, shift_s)
```
